# Optimizing a Trainium2 kernel written in Bass

```python
import jax, jax.numpy as jnp
from jax import lax
import numpy as np


D_MODEL = 1024
BATCH = 4
SEQ = 8192
DEPTH = 2
DEC_BATCH = 32
DEC_SEQ = 2048
PAST_LEN = 128

D_PLE = 256
D_CONV = D_MODEL // 2
D_RWKV = D_MODEL // 2
HEAD_SIZE = 64
N_HEADS = D_RWKV // HEAD_SIZE
LORA_W = 64
LORA_A = 64
LORA_G = 128
D_FF = 11 * D_MODEL // 4
NORM_EPS = 1e-6
GN_EPS = HEAD_SIZE * 1e-5
IN_SIZES = (D_CONV, D_CONV, D_CONV, D_RWKV, D_RWKV, D_RWKV,
            LORA_W + LORA_A, LORA_W + LORA_A, LORA_G, D_MODEL, D_MODEL)
IN_COLS = sum(IN_SIZES)

kernel_name = 'hybrid_bidir_conv_rwkv7_encoder'


def _rmsnorm(x, g):
    xf = x.astype(jnp.float32)
    y = xf * lax.rsqrt(jnp.mean(xf * xf, axis=-1, keepdims=True) + NORM_EPS)
    return (y * g.astype(jnp.float32)).astype(x.dtype)


def _split(z, sizes):
    idx = [int(s) for s in np.cumsum(sizes)[:-1]]
    return jnp.split(z, idx, axis=-1)


def _shift_prev(z):
    return jnp.pad(z, ((0, 0), (1, 0), (0, 0)))[:, :-1]


def _shift_next(z):
    return jnp.pad(z, ((0, 0), (0, 1), (0, 0)))[:, 1:]


def _conv3(x, w, b):
    xp = jnp.pad(x, ((0, 0), (1, 1), (0, 0)))
    return xp[:, :-2] * w[0] + xp[:, 1:-1] * w[1] + xp[:, 2:] * w[2] + b


def _wkv_scan(r, w, kk, b, k, v, reverse):
    bsz = r.shape[0]
    xs = tuple(jnp.moveaxis(t.astype(jnp.float32), 1, 0) for t in (r, w, kk, b, k, v))

    def step(S, inp):
        r_t, w_t, kk_t, b_t, k_t, v_t = inp
        sa = jnp.einsum('bhvk,bhk->bhv', S, kk_t)
        S = (S * w_t[:, :, None, :] - sa[..., None] * b_t[:, :, None, :]
             + v_t[..., None] * k_t[:, :, None, :])
        return S, jnp.einsum('bhvk,bhk->bhv', S, r_t)

    S0 = jnp.zeros((bsz, N_HEADS, HEAD_SIZE, HEAD_SIZE), jnp.float32)
    _, ys = lax.scan(step, S0, xs, reverse=reverse)
    return jnp.moveaxis(ys, 0, 1)


def _rwkv_branch(r, k, v, zf, zb, gd, shift_mu, decay_w0, decay_w2, iclr_a0, iclr_a2,
                 gate_g2, k_k, k_a, r_k, gn_w, gn_b, w_branch_b):
    bsz, T, _ = r.shape
    f32 = jnp.float32

    def heads(t):
        return t.reshape(bsz, T, N_HEADS, HEAD_SIZE)

    r32, k32, v32 = r.astype(f32), k.astype(f32), v.astype(f32)
    kk = heads(k32 * k_k)
    kk = kk / jnp.maximum(jnp.linalg.norm(kk, axis=-1, keepdims=True), 1e-12)
    outs = []
    for d, (z, z_shift, rev) in enumerate(((zf, _shift_prev(zf), False),
                                           (zb, _shift_next(zb), True))):
        z = (z + shift_mu[d] * (z_shift - z)).astype(f32)
        zw, za = z[..., :LORA_W], z[..., LORA_W:]
        logit = decay_w0[d] + jnp.tanh(zw) @ decay_w2[d]
        w = jnp.exp(-jnp.exp(-jax.nn.softplus(-logit) - 0.5))
        a = jax.nn.sigmoid(iclr_a0[d] + za @ iclr_a2[d])
        kd = k32 * (1.0 + (a - 1.0) * k_a)
        outs.append(_wkv_scan(heads(r32), heads(w), kk, kk * heads(a), heads(kd),
                              heads(v32), rev))
    y = outs[0] + outs[1]
    mean = jnp.mean(y, axis=-1, keepdims=True)
    var = jnp.mean(jnp.square(y - mean), axis=-1, keepdims=True)
    y = ((y - mean) * lax.rsqrt(var + GN_EPS)).reshape(bsz, T, D_RWKV) * gn_w + gn_b
    bonus = jnp.sum(heads(r32) * heads(k32) * r_k, axis=-1, keepdims=True) * heads(v32)
    g = jax.nn.sigmoid(gd.astype(f32)) @ gate_g2
    out = (y + bonus.reshape(bsz, T, D_RWKV)) * g
    return out.astype(r.dtype) @ w_branch_b


def _mixer(u, w_in, conv_w, conv_b, w_branch_a, shift_mu, decay_w0, decay_w2, iclr_a0,
           iclr_a2, gate_g2, k_k, k_a, r_k, gn_w, gn_b, w_branch_b, w_out):
    proj = u @ w_in
    (hc, b_gate, c_gate, r, k, v, zf, zb, gd,
     gate_conv, gate_rwkv) = _split(proj, IN_SIZES)
    y_conv = (b_gate * _conv3(c_gate * hc, conv_w, conv_b)) @ w_branch_a
    y_rwkv = _rwkv_branch(r, k, v, zf, zb, gd, shift_mu, decay_w0, decay_w2, iclr_a0,
                          iclr_a2, gate_g2, k_k, k_a, r_k, gn_w, gn_b, w_branch_b)
    merged = jax.nn.sigmoid(gate_conv) * y_conv + jax.nn.sigmoid(gate_rwkv) * y_rwkv
    return merged @ w_out


def _conv_ffn(u, w_up, ffn_conv_w, ffn_conv_b, w_down):
    h = _conv3(u @ w_up, ffn_conv_w, ffn_conv_b)
    hg, hv = jnp.split(h, 2, axis=-1)
    return (jax.nn.gelu(hg, approximate=True) * hv) @ w_down


def _layer(x, p, norm_mix_pre, norm_mix_post, norm_ffn_pre, norm_ffn_post, norm_ple_post,
           w_in, conv_w, conv_b, w_branch_a, shift_mu, decay_w0, decay_w2, iclr_a0, iclr_a2,
           gate_g2, k_k, k_a, r_k, gn_w, gn_b, w_branch_b, w_out, w_up, ffn_conv_w,
           ffn_conv_b, w_down, w_ple, w_ple_gate):
    u = _rmsnorm(x, norm_mix_pre)
    m = _mixer(u, w_in, conv_w, conv_b, w_branch_a, shift_mu, decay_w0, decay_w2, iclr_a0,
               iclr_a2, gate_g2, k_k, k_a, r_k, gn_w, gn_b, w_branch_b, w_out)
    x = x + _rmsnorm(m, norm_mix_post)
    f = _conv_ffn(_rmsnorm(x, norm_ffn_pre), w_up, ffn_conv_w, ffn_conv_b, w_down)
    x = x + _rmsnorm(f, norm_ffn_post)
    gate = jax.nn.sigmoid(x @ w_ple_gate)
    x = x + _rmsnorm(gate * (p @ w_ple), norm_ple_post)
    return x


def setup_inputs(seed: int = 0) -> dict:
    key = jax.random.key(seed)
    ks = iter(jax.random.split(key, 48))
    L = DEPTH

    def nrm(shape, scale):
        return scale * jax.random.normal(next(ks), shape, jnp.float32)

    def gain(shape):
        return 1.0 + nrm(shape, 0.05)

    return {
        'x_prompt': nrm((BATCH, SEQ, D_MODEL), 1.0),
        'x_sample': nrm((DEC_BATCH, DEC_SEQ, D_MODEL), 1.0),
        'p_prompt': nrm((DEPTH, BATCH, SEQ, D_PLE), 1.0),
        'p_sample': nrm((DEPTH, DEC_BATCH, DEC_SEQ, D_PLE), 1.0),
        'norm_mix_pre': gain((L, D_MODEL)),
        'norm_mix_post': gain((L, D_MODEL)),
        'norm_ffn_pre': gain((L, D_MODEL)),
        'norm_ffn_post': gain((L, D_MODEL)),
        'norm_ple_post': gain((L, D_MODEL)),
        'w_in': nrm((L, D_MODEL, IN_COLS), D_MODEL ** -0.5),
        'conv_w': nrm((L, 3, D_CONV), 3 ** -0.5),
        'conv_b': nrm((L, D_CONV), 0.02),
        'w_branch_a': nrm((L, D_CONV, D_MODEL), D_CONV ** -0.5),
        'shift_mu': jax.random.uniform(next(ks), (L, 2, LORA_W + LORA_A), jnp.float32, 0.2, 0.8),
        'decay_w0': -3.0 + nrm((L, 2, D_RWKV), 1.5),
        'decay_w2': nrm((L, 2, LORA_W, D_RWKV), 0.1),
        'iclr_a0': nrm((L, 2, D_RWKV), 0.5),
        'iclr_a2': nrm((L, 2, LORA_A, D_RWKV), LORA_A ** -0.5),
        'gate_g2': nrm((L, LORA_G, D_RWKV), LORA_G ** -0.5),
        'k_k': 0.85 + nrm((L, D_RWKV), 0.05),
        'k_a': 1.0 + nrm((L, D_RWKV), 0.05),
        'r_k': nrm((L, N_HEADS, HEAD_SIZE), 0.1),
        'gn_w': gain((L, D_RWKV)),
        'gn_b': nrm((L, D_RWKV), 0.02),
        'w_branch_b': nrm((L, D_RWKV, D_MODEL), D_RWKV ** -0.5),
        'w_out': nrm((L, D_MODEL, D_MODEL), D_MODEL ** -0.5),
        'w_up': nrm((L, D_MODEL, 2 * D_FF), D_MODEL ** -0.5),
        'ffn_conv_w': nrm((L, 3, 2 * D_FF), 3 ** -0.5),
        'ffn_conv_b': nrm((L, 2 * D_FF), 0.02),
        'w_down': nrm((L, D_FF, D_MODEL), D_FF ** -0.5),
        'w_ple': nrm((L, D_PLE, D_MODEL), D_PLE ** -0.5),
        'w_ple_gate': nrm((L, D_MODEL, D_MODEL), D_MODEL ** -0.5),
    }


def reference(x_prompt, x_sample, p_prompt, p_sample, norm_mix_pre, norm_mix_post,
              norm_ffn_pre, norm_ffn_post, norm_ple_post, w_in, conv_w, conv_b, w_branch_a,
              shift_mu, decay_w0, decay_w2, iclr_a0, iclr_a2, gate_g2, k_k, k_a, r_k, gn_w,
              gn_b, w_branch_b, w_out, w_up, ffn_conv_w, ffn_conv_b, w_down, w_ple,
              w_ple_gate):
    y_prompt, y_sample = x_prompt, x_sample
    for i in range(DEPTH):
        lp = dict(norm_mix_pre=norm_mix_pre[i], norm_mix_post=norm_mix_post[i],
                  norm_ffn_pre=norm_ffn_pre[i], norm_ffn_post=norm_ffn_post[i],
                  norm_ple_post=norm_ple_post[i], w_in=w_in[i], conv_w=conv_w[i],
                  conv_b=conv_b[i], w_branch_a=w_branch_a[i], shift_mu=shift_mu[i],
                  decay_w0=decay_w0[i], decay_w2=decay_w2[i], iclr_a0=iclr_a0[i],
                  iclr_a2=iclr_a2[i], gate_g2=gate_g2[i], k_k=k_k[i], k_a=k_a[i],
                  r_k=r_k[i], gn_w=gn_w[i], gn_b=gn_b[i], w_branch_b=w_branch_b[i],
                  w_out=w_out[i], w_up=w_up[i], ffn_conv_w=ffn_conv_w[i],
                  ffn_conv_b=ffn_conv_b[i], w_down=w_down[i], w_ple=w_ple[i],
                  w_ple_gate=w_ple_gate[i])
        y_prompt = _layer(y_prompt, p_prompt[i], **lp)
        y_sample = _layer(y_sample, p_sample[i], **lp)
    return (y_prompt, y_sample)
```

```python
import numpy as np
from contextlib import ExitStack
import concourse.bass as bass
import concourse.mybir as mybir
from concourse.bass_utils import run_bass_kernel_spmd

F32, BF16 = mybir.dt.float32, mybir.dt.bfloat16
AF = mybir.ActivationFunctionType
ALU = mybir.AluOpType
AX = mybir.AxisListType

D = 1024
INC = 5504
DFF = 2816
CDEC = -0.6065306597126334
NORM_EPS = 1e-6
GN_EPS = 64 * 1e-5
GELU_C = 1.5957691216057308

C_IDENT, C_SU, C_SL, C_UI, C_LI, C_BLK, C_ONES, C_HSEL, C_RESET, C_EPS, C_GNEPS = (
    0, 128, 256, 384, 512, 640, 768, 896, 928, 1440, 1441)
NCON = 1442
V_NMP, V_NMPOST, V_NFP, V_NFPOST, V_NPLE = 0, 8, 16, 24, 32
V_CONVW, V_CONVB, V_KK, V_KA, V_RK, V_W0, V_A0 = 40, 52, 56, 60, 64, 68, 76
V_FCW, V_FCB, V_MU, V_1MKA = 84, 216, 260, 264
NV = 268


class Buf:
    __slots__ = ("lw", "rd")

    def __init__(self):
        self.lw = None
        self.rd = []


ENGS = ("pe", "act", "dve", "pool", "sp")


class Sched:
    def __init__(self, ring=8):
        self.prog = {e: [] for e in ENGS}
        self.cnt = {e: 0 for e in ENGS}
        self.known = {e: {} for e in ENGS}
        self.ring = ring
        self.dma_next = {e: 0 for e in ENGS}
        self.dma_cnt = {e: [0] * ring for e in ENGS}

    def _deps(self, eng, reads, writes):
        deps = {}

        def add(p):
            k, v = p
            if deps.get(k, 0) < v:
                deps[k] = v

        for r in reads:
            if r.lw is not None and not (r.lw[0] == eng and eng == "pe"):
                add(r.lw)
        for w in writes:
            if w.lw is not None and w.lw[0] != eng:
                add(w.lw)
            for p in w.rd:
                if p[0] != eng:
                    add(p)
        return deps

    def _filter(self, eng, deps):
        kn = self.known[eng]
        out = []
        for k, v in deps.items():
            if kn.get(k, 0) >= v:
                continue
            kn[k] = v
            out.append((k, v))
        return out

    def _commit(self, me, reads, writes):
        for r in reads:
            r.rd.append(me)
        for w in writes:
            w.lw = me
            w.rd = []

    def op(self, eng, fn, reads=(), writes=()):
        waits = self._filter(eng, self._deps(eng, reads, writes))
        self.cnt[eng] += 1
        self.prog[eng].append((waits, fn, (eng, 1)))
        self._commit((eng, self.cnt[eng]), reads, writes)

    def dma(self, eng, fn, reads=(), writes=()):
        deps = self._deps(eng, reads, writes)
        slot = self.dma_next[eng] % self.ring
        self.dma_next[eng] += 1
        key = ("dma", eng, slot)
        c = self.dma_cnt[eng][slot]
        if c > 0 and deps.get(key, 0) < 16 * c:
            deps[key] = 16 * c
        waits = self._filter(eng, deps)
        self.dma_cnt[eng][slot] = c + 1
        self.prog[eng].append((waits, fn, (key, 16)))
        self._commit((key, 16 * (c + 1)), reads, writes)

    def _all(self):
        deps = {}
        for e in ENGS:
            if e != "sp" and self.cnt[e] > 0:
                deps[e] = self.cnt[e]
            for slot in range(self.ring):
                c = self.dma_cnt[e][slot]
                if c > 0:
                    deps[("dma", e, slot)] = 16 * c
        return deps

    def barrier(self):
        deps = self._all()
        for e in ENGS:
            d = {k: v for k, v in deps.items() if k != e}
            waits = self._filter(e, d)
            if waits:
                self.prog[e].append((waits, None, None))

    def finish(self):
        self.barrier()

    def emit(self, nc):
        keys = [e for e in ENGS if e != "sp" and self.cnt[e] > 0]
        for e in ENGS:
            for slot in range(self.ring):
                if self.dma_cnt[e][slot] > 0:
                    keys.append(("dma", e, slot))
        with ExitStack() as st:
            sems = {}
            for i, k in enumerate(keys):
                sems[k] = st.enter_context(nc.semaphore("s%d" % i))
            block = st.enter_context(nc.Block())

            def run(engname):
                def body(e):
                    for waits, fn, inc in self.prog[engname]:
                        for k, v in waits:
                            e.wait_ge(sems[k], v)
                        if fn is not None:
                            fn(e).then_inc(sems[inc[0]], inc[1])
                return body

            block.sync(run("sp"))
            block.tensor(run("pe"))
            block.scalar(run("act"))
            block.vector(run("dve"))
            block.gpsimd(run("pool"))


class Arena:
    def __init__(self, ap, size):
        self.ap, self.size, self.off = ap, size, 0

    def alloc(self, n, dtype):
        n16 = n * (2 if dtype == F32 else 1)
        start = (self.off + 15) // 16 * 16
        assert start + n16 <= self.size, ("arena overflow", start + n16, self.size)
        v = self.ap[:, start:start + n16]
        if dtype == F32:
            v = v.bitcast(F32)
        self.off = start + n16
        return v


def build_program(NSEG, SEG, DEPTH, debug=False, upto=99):
    NT = NSEG * SEG
    NCH = NT // 128
    CPS = SEG // 128
    assert NT % 512 == 0 and SEG % 128 == 0
    nc = bass.Bass("TRN2", target_bir_lowering=False)
    L = DEPTH

    def din(name, shape, dt=F32):
        return nc.dram_tensor(name, list(shape), dt, kind="ExternalInput").ap()

    def dscr(name, shape, dt):
        kind = "ExternalOutput" if debug else "Internal"
        return nc.dram_tensor(name, list(shape), dt, kind=kind).ap()

    xT = din("xT", [D, NT])
    pT = din("pT", [L, 256, NT])
    masks_d = din("masks", [128, NSEG + 1])
    consts_d = din("consts", [128, NCON])
    vecs_d = din("vecs", [L, 128, NV])
    gnwb_d = din("gnwb", [L, 128, 1024])
    w_in = din("w_in", [L, D, INC])
    w_a = din("w_branch_a", [L, 512, D])
    w_b = din("w_branch_b", [L, 512, D])
    w_out = din("w_out", [L, D, D])
    w_up = din("w_up", [L, D, 2 * DFF])
    w_down = din("w_down", [L, DFF, D])
    w_ple = din("w_ple", [L, 256, D])
    w_pg = din("w_ple_gate", [L, D, D])
    dw2_d = din("decay_w2", [L, 2, 64, 512])
    a2_d = din("iclr_a2", [L, 2, 64, 512])
    g2_d = din("gate_g2", [L, 128, 512])
    yT = nc.dram_tensor("yT", [D, NT], F32, kind="ExternalOutput").ap()

    qS = dscr("qS", [512, NT + 2], BF16)
    bgS = dscr("bgS", [512, NT], BF16)
    rS = dscr("rS", [512, NT], BF16)
    kS = dscr("kS", [512, NT], BF16)
    vS = dscr("vS", [NT, 512], BF16)
    zS = dscr("zS", [256, NT + 2], F32)
    sgdS = dscr("sgdS", [128, NT], BF16)
    sgcS = dscr("sgcS", [D, NT], BF16)
    sgrS = dscr("sgrS", [D, NT], BF16)
    rkS = dscr("rkS", [NT, 8], F32)
    fmS = dscr("fmS", [2, NCH, 64, 8 * 4 * 128], BF16)
    tmS = dscr("tmS", [2, NT, 1024], BF16)
    gcS = dscr("gcS", [2, NCH, 128, 4], F32)
    yS = dscr("yS", [2, NT, 512], F32)
    x1S = dscr("x1S", [D, NT + 2], F32)
    xL = dscr("xL", [D, NT], F32)
    wupS = dscr("wupS", [L, 22, 128, 8 * 2 * 128], BF16)

    S = Sched()
    st = ExitStack()
    ARENA_N = 88 * 1024
    arena_t = st.enter_context(nc.sbuf_tensor("arena", [128, ARENA_N], BF16))
    cf = st.enter_context(nc.sbuf_tensor("cf", [128, NCON], F32))
    cb = st.enter_context(nc.sbuf_tensor("cb", [128, 256], BF16))
    mk = st.enter_context(nc.sbuf_tensor("mk", [128, NSEG + 1], F32))
    vec = st.enter_context(nc.sbuf_tensor("vec", [128, NV], F32))
    psum = [st.enter_context(nc.psum_tensor("ps%d" % i, [128, 512], F32)) for i in range(8)]
    psb = [Buf() for _ in range(8)]
    A = Arena(arena_t, ARENA_N)
    b_c = Buf()
    b_vec = Buf()
    ident_bf = cb[:, 0:128]
    ones_bf = cb[:, 128:256]
    pctr = [0]

    def nextps():
        i = pctr[0] % 8
        pctr[0] += 1
        return psum[i], psb[i]

    def dma(out, in_, r=(), w=()):
        S.dma("sp", lambda e: e.dma_start(out=out, in_=in_), r, w)

    def mm(out, lhsT, rhs, start, stop, r, w):
        S.op("pe", lambda e: e.matmul(out, lhsT=lhsT, rhs=rhs, start=start, stop=stop), r, w)

    def transpose(out, in_, r, w):
        S.op("pe", lambda e: e.transpose(out, in_, ident_bf), list(r) + [b_c], w)

    def act(out, in_, func, r, w, bias=None, scale=None):
        kw = {}
        if bias is not None:
            kw["bias"] = bias
        if scale is not None:
            kw["scale"] = scale
        S.op("act", lambda e: e.activation(out=out, in_=in_, func=func, **kw), r, w)

    def tt(eng, out, in0, in1, op, r, w):
        S.op(eng, lambda e: e.tensor_tensor(out=out, in0=in0, in1=in1, op=op), r, w)

    def ts(eng, out, in0, s1, s2, op0, op1, r, w):
        if op1 is None:
            S.op(eng, lambda e: e.tensor_scalar(out=out, in0=in0, scalar1=s1, scalar2=None, op0=op0), r, w)
        else:
            S.op(eng, lambda e: e.tensor_scalar(out=out, in0=in0, scalar1=s1, scalar2=s2, op0=op0, op1=op1), r, w)

    def stt(out, in0, scalar, in1, op0, op1, r, w):
        S.op("dve", lambda e: e.scalar_tensor_tensor(out=out, in0=in0, scalar=scalar, in1=in1, op0=op0, op1=op1), r, w)

    def cp(eng, out, in_, r, w):
        if eng == "act":
            S.op("act", lambda e: e.copy(out=out, in_=in_), r, w)
        else:
            S.op(eng, lambda e: e.tensor_copy(out=out, in_=in_), r, w)

    def amul(out, in_, m, r, w):
        S.op("act", lambda e: e.mul(out=out, in_=in_, mul=m), r, w)

    def memset(eng, ap, val, w):
        S.op(eng, lambda e: e.memset(ap, val), (), w)

    def recip(out, in_, r, w):
        S.op("dve", lambda e: e.reciprocal(out=out, in_=in_), r, w)

    cast_rr = [0]

    def load_weight(dst, src, stages, bst, bdst):
        n = dst.shape[-1]
        i = cast_rr[0]
        cast_rr[0] += 1
        sg = stages[i % len(stages)][:, 0:n]
        bs = bst[i % len(stages)]
        dma(sg, src, (), [bs])
        cp(("dve", "act", "pool")[i % 3], dst, sg, [bs], [bdst])

    def halo_fix(eng, ap, b, rbufs, wbufs):
        if b == 0 or b == NSEG:
            memset(eng, ap, 0.0, wbufs)
        else:
            np_ = ap.shape[0]
            ts(eng, ap, ap, mk[0:np_, b:b + 1], None, ALU.mult, None, list(rbufs) + [b_c], wbufs)

    def rms_rstd(sq_chunks, nchunk, W, rstd_tmp, rstd, b_sq, b_tmp, b_rstd):
        ps, pb = nextps()
        for c in range(nchunk):
            mm(ps[:, 0:W], ones_bf, sq_chunks(c), c == 0, c == nchunk - 1, [b_sq, b_c], [pb])
        act(rstd_tmp, ps[:, 0:W], AF.Sqrt, [pb, b_c], [b_tmp], bias=cf[:, C_EPS:C_EPS + 1], scale=1.0 / D)
        recip(rstd, rstd_tmp, [b_tmp], [b_rstd])

    dma(cf[:], consts_d[:, :], (), [b_c])
    dma(mk[:], masks_d[:, :], (), [b_c])
    cp("dve", ident_bf, cf[:, C_IDENT:C_IDENT + 128], [b_c], [b_c])
    cp("dve", ones_bf, cf[:, C_ONES:C_ONES + 128], [b_c], [b_c])
    SU = cf[:, C_SU:C_SU + 128]
    SL = cf[:, C_SL:C_SL + 128]
    UI = cf[:, C_UI:C_UI + 128]
    LI = cf[:, C_LI:C_LI + 128]
    BLK = cf[:, C_BLK:C_BLK + 128]
    RESET = cf[:, C_RESET:C_RESET + 512]

    def pass_W0():
        A.off = 0
        stg = [A.alloc(2 * DFF, F32) for _ in range(2)]
        s16 = [A.alloc(2 * DFF, BF16) for _ in range(2)]
        bs = [Buf(), Buf()]
        b16 = [Buf(), Buf()]
        i = 0
        for l in range(L):
            for kc in range(8):
                s = i % 2
                dma(stg[s], w_up[l, kc * 128:(kc + 1) * 128, :], (), [bs[s]])
                cp(("dve", "act", "pool")[i % 3], s16[s], stg[s], [bs[s]], [b16[s]])
                for gv in range(2):
                    dst = wupS[l].rearrange("j p (k g m) -> p j k g m", k=8, g=2)[:, :, kc, gv, :]
                    src = s16[s][:, gv * DFF:(gv + 1) * DFF].rearrange("p (j m) -> p j m", m=128)
                    dma(dst, src, [b16[s]], ())
                i += 1
        S.barrier()

    def pass_P1(l, xin):
        A.off = 0
        W1 = A.alloc(8 * INC, BF16).rearrange("p (k n) -> p k n", k=8)
        bW1 = Buf()
        mark = A.off
        stg = [A.alloc(INC, F32) for _ in range(2)]
        bst = [Buf(), Buf()]
        dma(vec[:], vecs_d[l], (), [b_vec])
        for kc in range(8):
            load_weight(W1[:, kc, :], w_in[l, kc * 128:(kc + 1) * 128, :], stg, bst, bW1)
        S.barrier()
        A.off = mark
        xt = A.alloc(8 * 512, F32).rearrange("p (c t) -> p c t", c=8)
        sq = A.alloc(8 * 512, BF16).rearrange("p (c t) -> p c t", c=8)
        ub = A.alloc(8 * 512, BF16).rearrange("p (c t) -> p c t", c=8)
        rtmp = A.alloc(512, F32)
        rstd = A.alloc(512, F32)
        hc_s = A.alloc(4 * 512, BF16).rearrange("p (c t) -> p c t", c=4)
        bg_s = A.alloc(4 * 512, BF16).rearrange("p (c t) -> p c t", c=4)
        q_s = A.alloc(4 * 512, BF16).rearrange("p (c t) -> p c t", c=4)
        r_s = A.alloc(4 * 512, BF16).rearrange("p (c t) -> p c t", c=4)
        k_s = A.alloc(4 * 512, BF16).rearrange("p (c t) -> p c t", c=4)
        v_s = A.alloc(4 * 512, BF16).rearrange("p (c t) -> p c t", c=4)
        z_s = A.alloc(2 * 512, F32).rearrange("p (c t) -> p c t", c=2)
        sgd_s = A.alloc(512, BF16)
        sgc_s = A.alloc(8 * 512, BF16).rearrange("p (c t) -> p c t", c=8)
        sgr_s = A.alloc(8 * 512, BF16).rearrange("p (c t) -> p c t", c=8)
        b_xt, b_sq, b_ub, b_rt, b_rs = Buf(), Buf(), Buf(), Buf(), Buf()
        b_hc, b_bg, b_q, b_r, b_k, b_v, b_z, b_sgd, b_sgc, b_sgr = [Buf() for _ in range(10)]

        def fm(ap):
            return ap.rearrange("(c p) t -> p c t", p=128)

        import os
        P1LVL = int(os.environ.get("P1LVL", "20"))
        for ti in range(NT // 512):
            if P1LVL < 1:
                break
            t0 = ti * 512
            dma(xt, fm(xin)[:, :, t0:t0 + 512], (), [b_xt])
            act(sq, xt, AF.Square, [b_xt], [b_sq])
            if P1LVL < 2:
                continue
            rms_rstd(lambda c: sq[:, c, :], 8, 512, rtmp, rstd, b_sq, b_rt, b_rs)
            if P1LVL < 3:
                continue
            for c in range(8):
                stt(ub[:, c, :], xt[:, c, :], vec[:, V_NMP + c:V_NMP + c + 1], rstd, ALU.mult, ALU.mult,
                    [b_xt, b_rs, b_vec], [b_ub])

            def proj(f):
                ps, pb = nextps()
                for kc in range(8):
                    mm(ps[:], W1[:, kc, f * 128:(f + 1) * 128], ub[:, kc, :], kc == 0, kc == 7, [bW1, b_ub], [pb])
                return ps, pb

            if P1LVL < 4:
                continue
            for f in range(43):
                if 20 <= f < 24:
                    continue
                if P1LVL < 20 and f >= {4: 4, 5: 8, 6: 12, 7: 20, 8: 27, 9: 28, 10: 34, 11: 35, 12: 42, 13: 43}[P1LVL]:
                    continue
                ps, pb = proj(f)
                if f < 4:
                    cp("act", hc_s[:, f, :], ps[:], [pb], [b_hc])
                elif f < 8:
                    cp("act", bg_s[:, f - 4, :], ps[:], [pb], [b_bg])
                    if f == 7:
                        dma(fm(bgS)[:, :, t0:t0 + 512], bg_s, [b_bg], ())
                elif f < 12:
                    tt("dve", q_s[:, f - 8, :], ps[:], hc_s[:, f - 8, :], ALU.mult, [pb, b_hc], [b_q])
                    if f == 11:
                        dma(fm(qS)[:, :, 1 + t0:1 + t0 + 512], q_s, [b_q], ())
                elif f < 16:
                    cp("act", r_s[:, f - 12, :], ps[:], [pb], [b_r])
                    if f == 15:
                        dma(fm(rS)[:, :, t0:t0 + 512], r_s, [b_r], ())
                elif f < 20:
                    cp("dve", k_s[:, f - 16, :], ps[:], [pb], [b_k])
                    if f == 19:
                        dma(fm(kS)[:, :, t0:t0 + 512], k_s, [b_k], ())
                elif f < 26:
                    cp("act", z_s[:, f - 24, :], ps[:], [pb], [b_z])
                    if f == 25:
                        dma(fm(zS)[:, :, 1 + t0:1 + t0 + 512], z_s, [b_z], ())
                elif f == 26:
                    act(sgd_s, ps[:], AF.Sigmoid, [pb], [b_sgd])
                    dma(sgdS[:, t0:t0 + 512], sgd_s, [b_sgd], ())
                elif f < 35:
                    act(sgc_s[:, f - 27, :], ps[:], AF.Sigmoid, [pb], [b_sgc])
                    if f == 34:
                        dma(fm(sgcS)[:, :, t0:t0 + 512], sgc_s, [b_sgc], ())
                else:
                    act(sgr_s[:, f - 35, :], ps[:], AF.Sigmoid, [pb], [b_sgr])
                    if f == 42:
                        dma(fm(sgrS)[:, :, t0:t0 + 512], sgr_s, [b_sgr], ())
            for tb in range(4):
                ps, pb = nextps()
                for kc in range(8):
                    mm(ps[:], ub[:, kc, tb * 128:(tb + 1) * 128], W1[:, kc, 2560:3072], kc == 0, kc == 7,
                       [bW1, b_ub], [pb])
                cp(("dve", "act")[tb % 2], v_s[:, tb, :], ps[:], [pb], [b_v])
            dma(vS[t0:t0 + 512, :].rearrange("(b p) f -> p b f", p=128), v_s, [b_v], ())
        S.barrier()

    def pass_P2pre(l):
        A.off = 0
        dw2 = A.alloc(2 * 512, BF16).rearrange("p (d n) -> p d n", d=2)
        a2 = A.alloc(2 * 512, BF16).rearrange("p (d n) -> p d n", d=2)
        stg = [A.alloc(512, F32) for _ in range(2)]
        bst = [Buf(), Buf()]
        b_w = Buf()
        for d in range(2):
            load_weight(dw2[0:64, d, :], dw2_d[l, d], [s[0:64] for s in stg], bst, b_w)
            load_weight(a2[0:64, d, :], a2_d[l, d], [s[0:64] for s in stg], bst, b_w)
        ts("dve", vec[:, V_1MKA:V_1MKA + 4], vec[:, V_KA:V_KA + 4], -1.0, 1.0, ALU.mult, ALU.add, [b_vec], [b_vec])
        r_t = A.alloc(4 * 512, BF16).rearrange("p (c t) -> p c t", c=4)
        k_t = A.alloc(4 * 512, BF16).rearrange("p (c t) -> p c t", c=4)
        zt = [A.alloc(514, F32) for _ in range(4)]
        kk = A.alloc(4 * 512, F32).rearrange("p (c t) -> p c t", c=4)
        t1 = A.alloc(512, F32)
        t2 = A.alloc(512, F32)
        t3 = A.alloc(512, F32)
        prod = A.alloc(4 * 512, F32).rearrange("p (c t) -> p c t", c=4)
        rk_s = A.alloc(32, F32)
        zs = [A.alloc(512, F32) for _ in range(2)]
        tz = A.alloc(512, BF16)
        zab = A.alloc(512, BF16)
        sgm = A.alloc(512, F32)
        av = A.alloc(512, F32)
        cs = A.alloc(512, F32)
        ex = A.alloc(512, F32)
        rs_ = A.alloc(512, F32)
        ri = A.alloc(512, F32)
        E = [A.alloc(512, F32) for _ in range(4)]
        kd = A.alloc(512, F32)
        bb = A.alloc(512, F32)
        fmst = [A.alloc(4 * 4 * 128, BF16).rearrange("p (c a t) -> p c a t", c=4, a=4) for _ in range(2)]
        sc_s = A.alloc(2 * 4 * 512, BF16).rearrange("p (a c t) -> p a c t", a=2, c=4)
        tm_s = A.alloc(4 * 2 * 512, BF16).rearrange("p (b a f) -> p b a f", b=4, a=2)
        gc_s = A.alloc(16, F32).rearrange("p (c f) -> p c f", c=4)
        b_r, b_k, b_kk, b_t1, b_t2, b_t3, b_prod, b_rk = [Buf() for _ in range(8)]
        b_z = [Buf() for _ in range(4)]
        b_zs = [Buf(), Buf()]
        b_tz, b_zab, b_sgm, b_av, b_cs, b_ex, b_rs, b_ri, b_kd, b_bb = [Buf() for _ in range(10)]
        b_E = [Buf() for _ in range(4)]
        b_fm = [Buf(), Buf()]
        b_sc, b_tm, b_gc = Buf(), Buf(), Buf()

        def fm(ap):
            return ap.rearrange("(c p) t -> p c t", p=128)

        for ti in range(NT // 512):
            t0 = ti * 512
            dma(r_t, fm(rS)[:, :, t0:t0 + 512], (), [b_r])
            dma(k_t, fm(kS)[:, :, t0:t0 + 512], (), [b_k])
            for i in range(4):
                dma(zt[i][0:64, :], zS[i * 64:(i + 1) * 64, t0:t0 + 514], (), [b_z[i]])
            if t0 % SEG == 0:
                for i in (0, 1):
                    halo_fix("pool", zt[i][0:64, 0:1], t0 // SEG, [b_z[i]], [b_z[i]])
            if (t0 + 512) % SEG == 0:
                for i in (2, 3):
                    halo_fix("pool", zt[i][0:64, 513:514], (t0 + 512) // SEG, [b_z[i]], [b_z[i]])
            for fc in range(4):
                ts("dve", t1, k_t[:, fc, :], vec[:, V_KK + fc:V_KK + fc + 1], None, ALU.mult, None, [b_k, b_vec], [b_t1])
                tt("pool", t2, t1, t1, ALU.mult, [b_t1], [b_t2])
                ps, pb = nextps()
                mm(ps[:], BLK, t2, True, True, [b_t2, b_c], [pb])
                act(t3, ps[:], AF.Sqrt, [pb], [b_t3])
                ts("dve", t3, t3, 1e-12, None, ALU.max, None, [b_t3], [b_t3])
                recip(t3, t3, [b_t3], [b_t3])
                tt("dve", kk[:, fc, :], t1, t3, ALU.mult, [b_t1, b_t3], [b_kk])
                stt(prod[:, fc, :], r_t[:, fc, :], vec[:, V_RK + fc:V_RK + fc + 1], k_t[:, fc, :], ALU.mult, ALU.mult,
                    [b_r, b_k, b_vec], [b_prod])
            ps, pb = nextps()
            for tb in range(4):
                for fc in range(4):
                    mm(ps[:, tb * 8:(tb + 1) * 8], prod[:, fc, tb * 128:(tb + 1) * 128],
                       cf[:, C_HSEL + fc * 8:C_HSEL + (fc + 1) * 8], fc == 0, fc == 3, [b_prod, b_c], [pb])
            cp("act", rk_s, ps[:, 0:32], [pb], [b_rk])
            dma(rkS[t0:t0 + 512, :].rearrange("(b p) h -> p b h", p=128), rk_s.rearrange("p (b h) -> p b h", b=4),
                [b_rk], ())
            for d in range(2):
                for part in range(2):
                    Z = zt[d * 2 + part]
                    cur = Z[0:64, 1:513]
                    sh = Z[0:64, 0:512] if d == 0 else Z[0:64, 2:514]
                    bz = b_z[d * 2 + part]
                    tt("pool", t1[0:64, :], sh, cur, ALU.subtract, [bz], [b_t1])
                    stt(zs[part][0:64, :], t1[0:64, :], vec[0:64, V_MU + d * 2 + part:V_MU + d * 2 + part + 1], cur,
                        ALU.mult, ALU.add, [b_t1, bz, b_vec], [b_zs[part]])
                act(tz[0:64, :], zs[0][0:64, :], AF.Tanh, [b_zs[0]], [b_tz])
                cp("dve", zab[0:64, :], zs[1][0:64, :], [b_zs[1]], [b_zab])
                for fc in range(4):
                    ps, pb = nextps()
                    mm(ps[:], dw2[0:64, d, fc * 128:(fc + 1) * 128], tz[0:64, :], True, True, [b_w, b_tz], [pb])
                    act(sgm, ps[:], AF.Sigmoid, [pb, b_vec], [b_sgm],
                        bias=vec[:, V_W0 + d * 4 + fc:V_W0 + d * 4 + fc + 1], scale=1.0)
                    ps2, pb2 = nextps()
                    mm(ps2[:], a2[0:64, d, fc * 128:(fc + 1) * 128], zab[0:64, :], True, True, [b_w, b_zab], [pb2])
                    act(av, ps2[:], AF.Sigmoid, [pb2, b_vec], [b_av],
                        bias=vec[:, V_A0 + d * 4 + fc:V_A0 + d * 4 + fc + 1], scale=1.0)
                    S.op("dve", lambda e: e.tensor_tensor_scan(out=cs, data0=RESET, data1=sgm, initial=0.0,
                                                               op0=ALU.mult, op1=ALU.add),
                         [b_sgm, b_c], [b_cs])
                    cs3 = cs.rearrange("p (c t) -> p c t", c=4)
                    tot_bc = cs3[:, :, 127:128].to_broadcast([128, 4, 128])
                    tt("pool", ex, cs, sgm, ALU.subtract, [b_cs, b_sgm], [b_ex])
                    tt("dve", rs_.rearrange("p (c t) -> p c t", c=4), tot_bc, cs3, ALU.subtract, [b_cs], [b_rs])
                    if d == 0:
                        e1s, e2s, e4s = cs, ex, rs_
                        br1, br2, br4 = b_cs, b_ex, b_rs
                    else:
                        tt("pool", ri, rs_, sgm, ALU.add, [b_rs, b_sgm], [b_ri])
                        e1s, e2s, e4s = ri, rs_, ex
                        br1, br2, br4 = b_ri, b_rs, b_ex
                    act(E[0], e1s, AF.Exp, [br1], [b_E[0]], scale=CDEC)
                    act(E[1], e2s, AF.Exp, [br2], [b_E[1]], scale=CDEC)
                    act(E[2], e1s, AF.Exp, [br1], [b_E[2]], scale=-CDEC)
                    act(E[3], e4s, AF.Exp, [br4], [b_E[3]], scale=CDEC)
                    act(gc_s[:, :, fc], cs3[:, :, 127], AF.Exp, [b_cs], [b_gc], scale=CDEC)
                    ts("dve", t2, av, vec[:, V_KA + fc:V_KA + fc + 1], vec[:, V_1MKA + fc:V_1MKA + fc + 1],
                       ALU.mult, ALU.add, [b_av, b_vec], [b_t2])
                    tt("dve", kd, t2, k_t[:, fc, :], ALU.mult, [b_t2, b_k], [b_kd])
                    tt("pool", bb, kk[:, fc, :], av, ALU.mult, [b_kk, b_av], [b_bb])
                    F_ = fmst[fc % 2]
                    bF = b_fm[fc % 2]

                    def v4(ap):
                        return ap.rearrange("p (c t) -> p c t", c=4)

                    tt("pool", F_[:, :, 0, :], v4(kk[:, fc, :]), v4(E[1]), ALU.mult, [b_kk, b_E[1]], [bF])
                    tt("dve", F_[:, :, 1, :], v4(bb), v4(E[2]), ALU.mult, [b_bb, b_E[2]], [bF])
                    tt("dve", F_[:, :, 2, :], v4(kd), v4(E[2]), ALU.mult, [b_kd, b_E[2]], [bF])
                    tt("pool", F_[:, :, 3, :], v4(r_t[:, fc, :]), v4(E[0]), ALU.mult, [b_r, b_E[0]], [bF])
                    tt("dve", sc_s[:, 0, fc, :], kd, E[3], ALU.mult, [b_kd, b_E[3]], [b_sc])
                    tt("pool", sc_s[:, 1, fc, :], bb, E[3], ALU.mult, [b_bb, b_E[3]], [b_sc])
                    for half in range(2):
                        hp = 2 * fc + half
                        dst = fmS[d, t0 // 128:t0 // 128 + 4].rearrange("c k (h x) -> k c h x", h=8)[:, :, hp, :]
                        dma(dst, F_[half * 64:(half + 1) * 64].rearrange("p c a t -> p c (a t)"), [bF], ())
                dma(gcS[d, t0 // 128:t0 // 128 + 4].rearrange("c p f -> p c f"), gc_s, [b_gc], ())
                for a_ in range(2):
                    for tb in range(4):
                        ps, pb = nextps()
                        psT = ps[:].bitcast(BF16)
                        for fc in range(4):
                            transpose(psT[:, fc * 128:(fc + 1) * 128], sc_s[:, a_, fc, tb * 128:(tb + 1) * 128],
                                      [b_sc], [pb])
                        cp(("act", "dve")[tb % 2], tm_s[:, tb, a_, :], psT[:, 0:512], [pb], [b_tm])
                dma(tmS[d, t0:t0 + 512, :].rearrange("(b p) x -> p b x", p=128),
                    tm_s.rearrange("p b a f -> p b (a f)"), [b_tm], ())
        S.barrier()

    def pass_P2scan(l):
        A.off = 0
        NG = NCH * 8
        gall = [A.alloc(NG, F32).rearrange("p (c a f) -> p c a f", a=2, f=4) for _ in range(2)]
        b_gall = Buf()
        for d in range(2):
            for half in range(2):
                for c0 in range(0, NCH, 16):
                    c1 = min(NCH, c0 + 16)
                    dma(gall[d][0:64, c0:c1, half, :],
                        gcS[d, c0:c1, half * 64:(half + 1) * 64, :].rearrange("c p f -> p c f"), (), [b_gall])

        class Ctx:
            pass

        ctxs = []
        for d in range(2):
            cx = Ctx()
            cx.d = d
            cx.fm = [A.alloc(8 * 4 * 128, BF16).rearrange("p (h a t) -> p h a t", h=8, a=4) for _ in range(2)]
            cx.tm = [A.alloc(1024, BF16) for _ in range(2)]
            cx.v = [A.alloc(512, BF16) for _ in range(2)]
            cx.b_in = [Buf(), Buf()]
            cx.N = [[A.alloc(512, BF16).rearrange("p (h t) -> p h t", h=4) for _ in range(2)] for _ in range(2)]
            cx.NT = [[A.alloc(512, BF16).rearrange("p (h t) -> p h t", h=4) for _ in range(2)] for _ in range(2)]
            cx.bN = [[Buf(), Buf()] for _ in range(2)]
            cx.bNT = [[Buf(), Buf()] for _ in range(2)]
            cx.P = [A.alloc(512, BF16).rearrange("p (h t) -> p h t", h=4) for _ in range(2)]
            cx.bP = [Buf(), Buf()]
            cx.ARB = [A.alloc(512, BF16).rearrange("p (h t) -> p h t", h=4) for _ in range(2)]
            cx.AKD = [A.alloc(512, BF16).rearrange("p (h t) -> p h t", h=4) for _ in range(2)]
            cx.ARKD = [A.alloc(512, BF16).rearrange("p (h t) -> p h t", h=4) for _ in range(2)]
            cx.bARB = [Buf(), Buf()]
            cx.bAKD = [Buf(), Buf()]
            cx.bARKD = [Buf(), Buf()]
            cx.Xn = A.alloc(512, BF16)
            cx.bXn = Buf()
            cx.U = A.alloc(512, BF16)
            cx.bU = Buf()
            cx.ybuf = [A.alloc(512, F32) for _ in range(2)]
            cx.bY = [Buf(), Buf()]
            cx.S32 = [A.alloc(512, F32) for _ in range(2)]
            cx.Sbf = [A.alloc(512, BF16) for _ in range(2)]
            cx.bS32 = [Buf(), Buf()]
            cx.bSbf = [Buf(), Buf()]
            cx.t1 = A.alloc(512, F32)
            cx.bt1 = Buf()
            cx.cur = 0
            ctxs.append(cx)

        ident4 = cb[:, 0:128].rearrange("p (o t) -> p o t", o=1).to_broadcast([128, 4, 128])

        def m4(m):
            return m.rearrange("p (o t) -> p o t", o=1).to_broadcast([128, 4, 128])

        def chunk(cx, c, it):
            d = cx.d
            par = it % 2
            fm_, tm_, v_ = cx.fm[par], cx.tm[par], cx.v[par]
            b_in = cx.b_in[par]
            dma(fm_[0:64].rearrange("p h a t -> p (h a t)"), fmS[d, c], (), [b_in])
            dma(tm_, tmS[d, c * 128:(c + 1) * 128, :], (), [b_in])
            dma(v_, vS[c * 128:(c + 1) * 128, :], (), [b_in])
            mS, mSt, mR = (SU, SL, UI) if d == 0 else (SL, SU, LI)
            KK = lambda h: fm_[0:64, h, 0, :]
            BH = lambda h: fm_[0:64, h, 1, :]
            KD = lambda h: fm_[0:64, h, 2, :]
            RT = lambda h: fm_[0:64, h, 3, :]
            Vh = lambda h: v_[:, h * 64:(h + 1) * 64]
            KDs = lambda h: tm_[:, h * 64:(h + 1) * 64]
            Bs = lambda h: tm_[:, 512 + h * 64:512 + (h + 1) * 64]
            ps4 = lambda ps: ps[:].rearrange("p (h t) -> p h t", h=4)

            def prod4(g, lf, rf, rbufs):
                ps, pb = nextps()
                for i in range(4):
                    h = g * 4 + i
                    mm(ps[:, i * 128:(i + 1) * 128], lf(h), rf(h), True, True, rbufs, [pb])
                return ps, pb

            for g in range(2):
                cn = 0
                N, NT_ = cx.N[g], cx.NT[g]
                bN, bNT = cx.bN[g], cx.bNT[g]
                P, bP = cx.P[g], cx.bP[g]
                ps, pb = prod4(g, BH, KK, [b_in])
                tt("dve", N[cn], ps4(ps), m4(mS), ALU.mult, [pb, b_c], [bN[cn]])
                ps, pb = prod4(g, KK, BH, [b_in])
                tt("dve", NT_[cn], ps4(ps), m4(mSt), ALU.mult, [pb, b_c], [bNT[cn]])
                ps, pb = prod4(g, BH, RT, [b_in])
                tt("dve", cx.ARB[g], ps4(ps), m4(mR), ALU.mult, [pb, b_c], [cx.bARB[g]])
                ps, pb = prod4(g, KD, KK, [b_in])
                tt("dve", cx.AKD[g], ps4(ps), m4(mS), ALU.mult, [pb, b_c], [cx.bAKD[g]])
                ps, pb = prod4(g, KD, RT, [b_in])
                tt("dve", cx.ARKD[g], ps4(ps), m4(mR), ALU.mult, [pb, b_c], [cx.bARKD[g]])
                tt("pool", P, ident4, N[cn], ALU.subtract, [b_c, bN[cn]], [bP])
                for lvl in range(1, 7):
                    nn = 1 - cn
                    if lvl < 6:
                        ps, pb = prod4(g, lambda h: NT_[cn][:, h % 4, :], lambda h: N[cn][:, h % 4, :],
                                       [bN[cn], bNT[cn]])
                        cp("act", N[nn], ps4(ps), [pb], [bN[nn]])
                    ps, pb = prod4(g, lambda h: N[cn][:, h % 4, :], lambda h: NT_[cn][:, h % 4, :],
                                   [bN[cn], bNT[cn]])
                    cp("act", NT_[nn], ps4(ps), [pb], [bNT[nn]])
                    ps, pb = prod4(g, lambda h: NT_[nn][:, h % 4, :], lambda h: P[:, h % 4, :], [bNT[nn], bP])
                    tt("dve", P, ps4(ps), P, ALU.add, [pb, bP], [bP])
                    cn = nn
            cur = cx.cur
            nxt = 1 - cur
            S32, Sbf = cx.S32[cur], cx.Sbf[cur]
            bS32, bSbf = cx.bS32[cur], cx.bSbf[cur]
            first = (c % CPS == 0) if d == 0 else ((c + 1) % CPS == 0)
            if first:
                b = c // CPS if d == 0 else (c + 1) // CPS
                halo_fix("pool", S32[0:64, :], b, [bS32], [bS32])
                halo_fix("pool", Sbf[0:64, :], b, [bSbf], [bSbf])
            gC = gall[d][0:64, c].rearrange("p a f -> p (a f)")
            S32v = S32[0:64, :].rearrange("p (f a v) -> p a f v", f=4, a=2)
            t1v = cx.t1[0:64, :].rearrange("p (f a v) -> p a f v", f=4, a=2)
            gCv = gall[d][0:64, c].rearrange("p a (f o) -> p a f o", o=1).to_broadcast([64, 2, 4, 64])
            tt("pool", t1v, S32v, gCv, ALU.mult, [bS32, b_gall], [cx.bt1])
            ps, pb = nextps()
            for h in range(8):
                g = h // 4
                mm(ps[:, h * 64:(h + 1) * 64], KK(h), Sbf[0:64, h * 64:(h + 1) * 64], True, False, [b_in, bSbf], [pb])
                mm(ps[:, h * 64:(h + 1) * 64], cx.AKD[g][:, h % 4, :], Vh(h), False, True, [cx.bAKD[g], b_in], [pb])
            amul(cx.Xn, ps[:], -1.0, [pb], [cx.bXn])
            ps, pb = nextps()
            for h in range(8):
                g = h // 4
                mm(ps[:, h * 64:(h + 1) * 64], cx.P[g][:, h % 4, :], cx.Xn[:, h * 64:(h + 1) * 64], True, True,
                   [cx.bP[g], cx.bXn], [pb])
            cp("dve", cx.U, ps[:], [pb], [cx.bU])
            ps, pb = nextps()
            for h in range(8):
                g = h // 4
                o = ps[:, h * 64:(h + 1) * 64]
                mm(o, RT(h), Sbf[0:64, h * 64:(h + 1) * 64], True, False, [b_in, bSbf], [pb])
                mm(o, cx.ARKD[g][:, h % 4, :], Vh(h), False, False, [cx.bARKD[g], b_in], [pb])
                mm(o, cx.ARB[g][:, h % 4, :], cx.U[:, h * 64:(h + 1) * 64], False, True, [cx.bARB[g], cx.bU], [pb])
            yb_, bY = cx.ybuf[par], cx.bY[par]
            cp("act", yb_, ps[:], [pb], [bY])
            dma(yS[d, c * 128:(c + 1) * 128, :], yb_, [bY], ())
            ps, pb = nextps()
            for h in range(8):
                o = ps[0:64, h * 64:(h + 1) * 64]
                mm(o, KDs(h), Vh(h), True, False, [b_in], [pb])
                mm(o, Bs(h), cx.U[:, h * 64:(h + 1) * 64], False, True, [b_in, cx.bU], [pb])
            tt("dve", cx.Sbf[nxt][0:64, :], ps[0:64, :], cx.t1[0:64, :], ALU.add, [pb, cx.bt1], [cx.bSbf[nxt]])
            tt("dve", cx.S32[nxt][0:64, :], ps[0:64, :], cx.t1[0:64, :], ALU.add, [pb, cx.bt1], [cx.bS32[nxt]])
            cx.cur = nxt

        for it in range(NCH):
            chunk(ctxs[0], it, it)
            chunk(ctxs[1], NCH - 1 - it, it)
        S.barrier()

    def pass_P3a(l, xin):
        A.off = 0
        wa = A.alloc(4 * D, BF16).rearrange("p (k n) -> p k n", k=4)
        wb = A.alloc(4 * D, BF16).rearrange("p (k n) -> p k n", k=4)
        wo = A.alloc(8 * D, BF16).rearrange("p (k n) -> p k n", k=8)
        g2 = A.alloc(512, BF16)
        gnw = A.alloc(512, F32)
        gnb = A.alloc(512, F32)
        stg = [A.alloc(D, F32) for _ in range(2)]
        bst = [Buf(), Buf()]
        b_w = Buf()
        for kc in range(4):
            load_weight(wa[:, kc, :], w_a[l, kc * 128:(kc + 1) * 128, :], stg, bst, b_w)
            load_weight(wb[:, kc, :], w_b[l, kc * 128:(kc + 1) * 128, :], stg, bst, b_w)
        for kc in range(8):
            load_weight(wo[:, kc, :], w_out[l, kc * 128:(kc + 1) * 128, :], stg, bst, b_w)
        load_weight(g2, g2_d[l], [s[:, 0:512] for s in stg], bst, b_w)
        dma(gnw, gnwb_d[l][:, 0:512], (), [b_w])
        dma(gnb, gnwb_d[l][:, 512:1024], (), [b_w])

        def a3(n, c, dt):
            return A.alloc(c * n, dt).rearrange("p (c t) -> p c t", c=c)

        q_t = a3(514, 4, BF16)
        bg_t = a3(512, 4, BF16)
        sgd_t = A.alloc(512, BF16)
        sgc_t = a3(512, 8, BF16)
        sgr_t = a3(512, 8, BF16)
        x_t = a3(512, 8, F32)
        yf = [A.alloc(512, F32) for _ in range(2)]
        ybk = [A.alloc(512, F32) for _ in range(2)]
        v_t = [A.alloc(512, BF16) for _ in range(2)]
        rk_t = [A.alloc(8, F32) for _ in range(2)]
        y32 = A.alloc(512, F32)
        tmp = A.alloc(512, F32)
        tmp2 = A.alloc(512, F32)
        stat = A.alloc(64, F32)
        o_bf = A.alloc(512, BF16)
        oT = a3(512, 4, BF16)
        cq = A.alloc(512, F32)
        ca = a3(512, 4, BF16)
        mg1 = a3(512, 8, F32)
        merged = a3(512, 8, BF16)
        m32 = a3(512, 8, F32)
        sqm = a3(512, 8, BF16)
        rtmp = A.alloc(512, F32)
        rstd = A.alloc(512, F32)
        x1_t = m32
        (b_q, b_bg, b_sgd, b_sgc, b_sgr, b_x, b_y32, b_tmp, b_tmp2, b_stat, b_o, b_oT, b_cq, b_ca, b_mg1, b_mer,
         b_m32, b_sqm, b_rt, b_rs, b_x1) = [Buf() for _ in range(21)]
        b_x1 = b_m32
        b_tb = [Buf(), Buf()]

        def fm(ap):
            return ap.rearrange("(c p) t -> p c t", p=128)

        def h8(ap):
            return ap.rearrange("p (h v) -> p h v", h=8)

        def st8(i):
            return stat[:, i * 8:(i + 1) * 8]

        def bc8(i):
            return stat[:, i * 8:(i + 1) * 8].rearrange("p (h o) -> p h o", o=1).to_broadcast([128, 8, 64])

        for ti in range(NT // 512):
            t0 = ti * 512
            dma(q_t, fm(qS)[:, :, t0:t0 + 514], (), [b_q])
            dma(bg_t, fm(bgS)[:, :, t0:t0 + 512], (), [b_bg])
            dma(sgd_t, sgdS[:, t0:t0 + 512], (), [b_sgd])
            dma(sgc_t, fm(sgcS)[:, :, t0:t0 + 512], (), [b_sgc])
            dma(sgr_t, fm(sgrS)[:, :, t0:t0 + 512], (), [b_sgr])
            dma(x_t, fm(xin)[:, :, t0:t0 + 512], (), [b_x])
            import os
            P3LVL = int(os.environ.get("P3LVL", "20"))
            if P3LVL < 2:
                continue
            if t0 % SEG == 0:
                halo_fix("dve", q_t[:, :, 0:1], t0 // SEG, [b_q], [b_q])
            if (t0 + 512) % SEG == 0:
                halo_fix("dve", q_t[:, :, 513:514], (t0 + 512) // SEG, [b_q], [b_q])
            for tb in range(4):
                if P3LVL < 3:
                    continue
                pp = tb % 2
                r0 = t0 + tb * 128
                dma(yf[pp], yS[0, r0:r0 + 128, :], (), [b_tb[pp]])
                dma(ybk[pp], yS[1, r0:r0 + 128, :], (), [b_tb[pp]])
                dma(v_t[pp], vS[r0:r0 + 128, :], (), [b_tb[pp]])
                dma(rk_t[pp], rkS[r0:r0 + 128, :], (), [b_tb[pp]])
                bt = b_tb[pp]
                tt("pool", y32, yf[pp], ybk[pp], ALU.add, [bt], [b_y32])
                S.op("dve", lambda e: e.tensor_reduce(out=st8(0), in_=h8(y32), axis=AX.X, op=ALU.add), [b_y32], [b_stat])
                tt("pool", tmp, y32, y32, ALU.mult, [b_y32], [b_tmp])
                S.op("dve", lambda e: e.tensor_reduce(out=st8(1), in_=h8(tmp), axis=AX.X, op=ALU.add), [b_tmp], [b_stat])
                ts("dve", st8(0), st8(0), 1.0 / 64, None, ALU.mult, None, [b_stat], [b_stat])
                tt("dve", st8(2), st8(0), st8(0), ALU.mult, [b_stat], [b_stat])
                stt(st8(3), st8(1), 1.0 / 64, st8(2), ALU.mult, ALU.subtract, [b_stat], [b_stat])
                act(st8(4), st8(3), AF.Sqrt, [b_stat, b_c], [b_stat], bias=cf[:, C_GNEPS:C_GNEPS + 1], scale=1.0)
                recip(st8(5), st8(4), [b_stat], [b_stat])
                if P3LVL < 4:
                    continue
                tt("dve", h8(tmp), h8(y32), bc8(0), ALU.subtract, [b_y32, b_stat], [b_tmp])
                tt("dve", h8(tmp2), h8(tmp), bc8(5), ALU.mult, [b_tmp, b_stat], [b_tmp2])
                tt("pool", tmp, tmp2, gnw, ALU.mult, [b_tmp2, b_w], [b_tmp])
                tt("pool", tmp2, tmp, gnb, ALU.add, [b_tmp, b_w], [b_tmp2])
                rkb = rk_t[pp].rearrange("p (h o) -> p h o", o=1).to_broadcast([128, 8, 64])
                tt("pool", h8(tmp), h8(v_t[pp]), rkb, ALU.mult, [bt], [b_tmp])
                tt("dve", y32, tmp2, tmp, ALU.add, [b_tmp2, b_tmp], [b_y32])
                if P3LVL < 5:
                    continue
                ps, pb = nextps()
                mm(ps[:], sgd_t[:, tb * 128:(tb + 1) * 128], g2, True, True, [b_sgd, b_w], [pb])
                tt("dve", o_bf, ps[:], y32, ALU.mult, [pb, b_y32], [b_o])
                ps, pb = nextps()
                psT = ps[:].bitcast(BF16)
                for fc in range(4):
                    transpose(psT[:, fc * 128:(fc + 1) * 128], o_bf[:, fc * 128:(fc + 1) * 128], [b_o], [pb])
                cp("act", oT[:, :, tb * 128:(tb + 1) * 128], psT[:, 0:512].rearrange("p (c t) -> p c t", c=4),
                   [pb], [b_oT])
            if P3LVL < 6:
                continue
            for fc in range(4):
                cw = lambda j: vec[:, V_CONVW + j * 4 + fc:V_CONVW + j * 4 + fc + 1]
                ts("dve", cq, q_t[:, fc, 1:513], cw(1), vec[:, V_CONVB + fc:V_CONVB + fc + 1], ALU.mult, ALU.add,
                   [b_q, b_vec], [b_cq])
                stt(cq, q_t[:, fc, 0:512], cw(0), cq, ALU.mult, ALU.add, [b_q, b_vec, b_cq], [b_cq])
                stt(cq, q_t[:, fc, 2:514], cw(2), cq, ALU.mult, ALU.add, [b_q, b_vec, b_cq], [b_cq])
                tt("pool", ca[:, fc, :], cq, bg_t[:, fc, :], ALU.mult, [b_cq, b_bg], [b_ca])
            if P3LVL < 7:
                continue
            for mc in range(8):
                ps, pb = nextps()
                for kc in range(4):
                    mm(ps[:], wa[:, kc, mc * 128:(mc + 1) * 128], ca[:, kc, :], kc == 0, kc == 3, [b_w, b_ca], [pb])
                tt("dve", mg1[:, mc, :], ps[:], sgc_t[:, mc, :], ALU.mult, [pb, b_sgc], [b_mg1])
            if P3LVL < 8:
                continue
            for mc in range(8):
                ps, pb = nextps()
                for kc in range(4):
                    mm(ps[:], wb[:, kc, mc * 128:(mc + 1) * 128], oT[:, kc, :], kc == 0, kc == 3, [b_w, b_oT], [pb])
                tt("dve", tmp, ps[:], sgr_t[:, mc, :], ALU.mult, [pb, b_sgr], [b_tmp])
                tt("pool", merged[:, mc, :], tmp, mg1[:, mc, :], ALU.add, [b_tmp, b_mg1], [b_mer])
            if P3LVL < 9:
                continue
            for mc in range(8):
                ps, pb = nextps()
                for kc in range(8):
                    mm(ps[:], wo[:, kc, mc * 128:(mc + 1) * 128], merged[:, kc, :], kc == 0, kc == 7, [b_w, b_mer], [pb])
                cp("dve", m32[:, mc, :], ps[:], [pb], [b_m32])
                act(sqm[:, mc, :], m32[:, mc, :], AF.Square, [b_m32], [b_sqm])
            if P3LVL < 10:
                continue
            rms_rstd(lambda c: sqm[:, c, :], 8, 512, rtmp, rstd, b_sqm, b_rt, b_rs)
            if P3LVL < 11:
                continue
            for mc in range(8):
                tt("pool", tmp, m32[:, mc, :], rstd, ALU.mult, [b_m32, b_rs], [b_tmp])
                stt(x1_t[:, mc, :], tmp, vec[:, V_NMPOST + mc:V_NMPOST + mc + 1], x_t[:, mc, :], ALU.mult, ALU.add,
                    [b_tmp, b_vec, b_x], [b_x1])
            if P3LVL < 12:
                continue
            dma(fm(x1S)[:, :, 1 + t0:1 + t0 + 512], x1_t, [b_x1], ())
        S.barrier()

    def pass_P3b(l, xout):
        A.off = 0
        wd = A.alloc(22 * D, BF16).rearrange("p (k n) -> p k n", k=22)
        wpg = A.alloc(8 * D, BF16).rearrange("p (k n) -> p k n", k=8)
        wpl = A.alloc(2 * D, BF16).rearrange("p (k n) -> p k n", k=2)
        mark = A.off
        stg = [A.alloc(D, F32) for _ in range(2)]
        bst = [Buf(), Buf()]
        b_w = Buf()
        for kc in range(22):
            load_weight(wd[:, kc, :], w_down[l, kc * 128:(kc + 1) * 128, :], stg, bst, b_w)
        for kc in range(8):
            load_weight(wpg[:, kc, :], w_pg[l, kc * 128:(kc + 1) * 128, :], stg, bst, b_w)
        for kc in range(2):
            load_weight(wpl[:, kc, :], w_ple[l, kc * 128:(kc + 1) * 128, :], stg, bst, b_w)
        S.barrier()
        A.off = mark
        WM = 412

        def a3(n, c, dt):
            return A.alloc(c * n, dt).rearrange("p (c t) -> p c t", c=c)

        x1w = a3(WM, 8, F32)
        sq = a3(WM, 8, BF16)
        u = a3(WM, 8, BF16)
        rtmp = A.alloc(WM, F32)
        rstd = A.alloc(WM, F32)
        p_t = a3(WM, 2, F32)
        p_b = a3(WM, 2, BF16)
        wj = [A.alloc(8 * 2 * 128, BF16).rearrange("p (k g m) -> p k g m", k=8, g=2) for _ in range(3)]
        b_wj = [Buf() for _ in range(3)]
        cg = A.alloc(WM, F32)
        cv = A.alloc(WM, F32)
        g1 = A.alloc(WM, F32)
        g2_ = A.alloc(WM, F32)
        g3 = A.alloc(WM, F32)
        actb = a3(WM, 22, BF16)
        m32 = a3(WM, 8, F32)
        sqm = sq
        x2_t = a3(WM, 8, F32)
        x2b = u
        gate = A.alloc(WM, F32)
        tmp = A.alloc(WM, F32)
        (b_x1, b_sq, b_u, b_rt, b_rs, b_p, b_pb, b_cg, b_cv, b_g1, b_g2, b_g3, b_act, b_m32, b_sqm, b_x2, b_x2b,
         b_gate, b_tmp) = [Buf() for _ in range(19)]
        b_sqm = b_sq
        b_x2b = b_u

        def fm(ap):
            return ap.rearrange("(c p) t -> p c t", p=128)

        tiles = []
        for sgi in range(NSEG):
            a = sgi * SEG
            npc = (SEG + 409) // 410
            base = SEG // npc
            rem = SEG - base * npc
            for i in range(npc):
                n = base + (1 if i < rem else 0)
                tiles.append((a, n))
                a += n
        wctr = 0
        for (a0, n) in tiles:
            W = n + 2
            dma(x1w[:, :, 0:W], fm(x1S)[:, :, a0:a0 + W], (), [b_x1])
            dma(p_t[:, :, 0:n], pT[l].rearrange("(c p) t -> p c t", p=128)[:, :, a0:a0 + n], (), [b_p])
            cp("pool", p_b[:, :, 0:n], p_t[:, :, 0:n], [b_p], [b_pb])
            act(sq[:, :, 0:W], x1w[:, :, 0:W], AF.Square, [b_x1], [b_sq])
            rms_rstd(lambda c: sq[:, c, 0:W], 8, W, rtmp[:, 0:W], rstd[:, 0:W], b_sq, b_rt, b_rs)
            for c in range(8):
                stt(u[:, c, 0:W], x1w[:, c, 0:W], vec[:, V_NFP + c:V_NFP + c + 1], rstd[:, 0:W], ALU.mult, ALU.mult,
                    [b_x1, b_rs, b_vec], [b_u])
            if a0 % SEG == 0:
                halo_fix("dve", u[:, :, 0:1], a0 // SEG, [b_u], [b_u])
            if (a0 + n) % SEG == 0:
                halo_fix("dve", u[:, :, W - 1:W], (a0 + n) // SEG, [b_u], [b_u])
            for j in range(22):
                wi = wctr % 3
                wctr += 1
                dma(wj[wi].rearrange("p k g m -> p (k g m)"), wupS[l, j], (), [b_wj[wi]])
                res = []
                for gv in range(2):
                    ps, pb = nextps()
                    for kc in range(8):
                        mm(ps[:, 0:W], wj[wi][:, kc, gv, :], u[:, kc, 0:W], kc == 0, kc == 7, [b_wj[wi], b_u], [pb])
                    res.append((ps, pb))
                for gv in range(2):
                    ps, pb = res[gv]
                    c_ = gv * 22 + j
                    dst, bd = (cg, b_cg) if gv == 0 else (cv, b_cv)
                    fw = lambda jj: vec[:, V_FCW + jj * 44 + c_:V_FCW + jj * 44 + c_ + 1]
                    act(dst[:, 0:n], ps[:, 1:W - 1], AF.Identity, [pb, b_vec], [bd],
                        bias=vec[:, V_FCB + c_:V_FCB + c_ + 1], scale=fw(1))
                    stt(dst[:, 0:n], ps[:, 0:n], fw(0), dst[:, 0:n], ALU.mult, ALU.add, [pb, b_vec, bd], [bd])
                    stt(dst[:, 0:n], ps[:, 2:W], fw(2), dst[:, 0:n], ALU.mult, ALU.add, [pb, b_vec, bd], [bd])
                tt("pool", g1[:, 0:n], cg[:, 0:n], cg[:, 0:n], ALU.mult, [b_cg], [b_g1])
                ts("pool", g1[:, 0:n], g1[:, 0:n], 0.044715, 1.0, ALU.mult, ALU.add, [b_g1], [b_g1])
                tt("pool", g2_[:, 0:n], g1[:, 0:n], cg[:, 0:n], ALU.mult, [b_g1, b_cg], [b_g2])
                act(g3[:, 0:n], g2_[:, 0:n], AF.Sigmoid, [b_g2], [b_g3], scale=GELU_C)
                tt("pool", g1[:, 0:n], cg[:, 0:n], cv[:, 0:n], ALU.mult, [b_cg, b_cv, b_g2], [b_g1])
                tt("dve", actb[:, j, 0:n], g1[:, 0:n], g3[:, 0:n], ALU.mult, [b_g1, b_g3], [b_act])
            for mc in range(8):
                ps, pb = nextps()
                for kc in range(22):
                    mm(ps[:, 0:n], wd[:, kc, mc * 128:(mc + 1) * 128], actb[:, kc, 0:n], kc == 0, kc == 21,
                       [b_w, b_act], [pb])
                cp("dve", m32[:, mc, 0:n], ps[:, 0:n], [pb], [b_m32])
                act(sqm[:, mc, 0:n], m32[:, mc, 0:n], AF.Square, [b_m32], [b_sqm])
            rms_rstd(lambda c: sqm[:, c, 0:n], 8, n, rtmp[:, 0:n], rstd[:, 0:n], b_sqm, b_rt, b_rs)
            for mc in range(8):
                tt("pool", tmp[:, 0:n], m32[:, mc, 0:n], rstd[:, 0:n], ALU.mult, [b_m32, b_rs], [b_tmp])
                stt(x2_t[:, mc, 0:n], tmp[:, 0:n], vec[:, V_NFPOST + mc:V_NFPOST + mc + 1], x1w[:, mc, 1:W - 1],
                    ALU.mult, ALU.add, [b_tmp, b_vec, b_x1], [b_x2])
            cp("pool", x2b[:, :, 0:n], x2_t[:, :, 0:n], [b_x2], [b_x2b])
            for mc in range(8):
                ps, pb = nextps()
                for kc in range(8):
                    mm(ps[:, 0:n], wpg[:, kc, mc * 128:(mc + 1) * 128], x2b[:, kc, 0:n], kc == 0, kc == 7,
                       [b_w, b_x2b], [pb])
                act(gate[:, 0:n], ps[:, 0:n], AF.Sigmoid, [pb], [b_gate])
                ps2, pb2 = nextps()
                for kc in range(2):
                    mm(ps2[:, 0:n], wpl[:, kc, mc * 128:(mc + 1) * 128], p_b[:, kc, 0:n], kc == 0, kc == 1,
                       [b_w, b_pb], [pb2])
                tt("dve", m32[:, mc, 0:n], ps2[:, 0:n], gate[:, 0:n], ALU.mult, [pb2, b_gate], [b_m32])
                act(sqm[:, mc, 0:n], m32[:, mc, 0:n], AF.Square, [b_m32], [b_sqm])
            rms_rstd(lambda c: sqm[:, c, 0:n], 8, n, rtmp[:, 0:n], rstd[:, 0:n], b_sqm, b_rt, b_rs)
            for mc in range(8):
                tt("pool", tmp[:, 0:n], m32[:, mc, 0:n], rstd[:, 0:n], ALU.mult, [b_m32, b_rs], [b_tmp])
                stt(x1w[:, mc, 0:n], tmp[:, 0:n], vec[:, V_NPLE + mc:V_NPLE + mc + 1], x2_t[:, mc, 0:n],
                    ALU.mult, ALU.add, [b_tmp, b_vec, b_x2], [b_x1])
            dma(fm(xout)[:, :, a0:a0 + n], x1w[:, :, 0:n], [b_x1], ())
        S.barrier()

    pass_W0()
    for l in range(L):
        xin = xT if l == 0 else xL
        xout = yT if l == L - 1 else xL
        if upto >= 1:
            pass_P1(l, xin)
        if upto >= 2:
            pass_P2pre(l)
        if upto >= 3:
            pass_P2scan(l)
        if upto >= 4:
            pass_P3a(l, xin)
        if upto >= 5:
            pass_P3b(l, xout)
    S.finish()
    S.emit(nc)
    st.close()
    return nc


def make_consts():
    c = np.zeros((128, NCON), np.float32)
    i = np.arange(128)
    c[:, C_IDENT:C_IDENT + 128] = np.eye(128)
    c[:, C_SU:C_SU + 128] = (i[:, None] < i[None, :])
    c[:, C_SL:C_SL + 128] = (i[:, None] > i[None, :])
    c[:, C_UI:C_UI + 128] = (i[:, None] <= i[None, :])
    c[:, C_LI:C_LI + 128] = (i[:, None] >= i[None, :])
    c[:, C_BLK:C_BLK + 128] = ((i[:, None] // 64) == (i[None, :] // 64))
    c[:, C_ONES:C_ONES + 128] = 1.0
    for fc in range(4):
        for h in range(8):
            c[:, C_HSEL + fc * 8 + h] = (h == 2 * fc + i // 64)
    r = np.ones(512, np.float32)
    r[::128] = 0.0
    c[:, C_RESET:C_RESET + 512] = r[None, :]
    c[:, C_EPS] = NORM_EPS
    c[:, C_GNEPS] = GN_EPS
    return c


def make_vecs(inp, L):
    v = np.zeros((L, 128, NV), np.float32)

    def fmaj(a):
        return np.ascontiguousarray(a.reshape(-1, 128).T)

    for l in range(L):
        v[l, :, V_NMP:V_NMP + 8] = fmaj(inp["norm_mix_pre"][l])
        v[l, :, V_NMPOST:V_NMPOST + 8] = fmaj(inp["norm_mix_post"][l])
        v[l, :, V_NFP:V_NFP + 8] = fmaj(inp["norm_ffn_pre"][l])
        v[l, :, V_NFPOST:V_NFPOST + 8] = fmaj(inp["norm_ffn_post"][l])
        v[l, :, V_NPLE:V_NPLE + 8] = fmaj(inp["norm_ple_post"][l])
        for j in range(3):
            v[l, :, V_CONVW + j * 4:V_CONVW + j * 4 + 4] = fmaj(inp["conv_w"][l, j])
            v[l, :, V_FCW + j * 44:V_FCW + j * 44 + 44] = fmaj(inp["ffn_conv_w"][l, j])
        v[l, :, V_CONVB:V_CONVB + 4] = fmaj(inp["conv_b"][l])
        v[l, :, V_FCB:V_FCB + 44] = fmaj(inp["ffn_conv_b"][l])
        v[l, :, V_KK:V_KK + 4] = fmaj(inp["k_k"][l])
        v[l, :, V_KA:V_KA + 4] = fmaj(inp["k_a"][l])
        v[l, :, V_RK:V_RK + 4] = fmaj(inp["r_k"][l].reshape(-1))
        for d in range(2):
            v[l, :, V_W0 + d * 4:V_W0 + d * 4 + 4] = fmaj(inp["decay_w0"][l, d])
            v[l, :, V_A0 + d * 4:V_A0 + d * 4 + 4] = fmaj(inp["iclr_a0"][l, d])
            v[l, 0:64, V_MU + d * 2] = inp["shift_mu"][l, d, 0:64]
            v[l, 0:64, V_MU + d * 2 + 1] = inp["shift_mu"][l, d, 64:128]
    return v


_PROG_CACHE = {}


def run_cores(seqs_per_core, carry_per_core, inp, NSEG, SEG, L, debug=False, upto=99):
    key = (NSEG, SEG, L, debug, upto)
    if key not in _PROG_CACHE:
        _PROG_CACHE[key] = build_program(NSEG, SEG, L, debug, upto)
    nc = _PROG_CACHE[key]
    consts = make_consts()
    vecs = make_vecs(inp, L)
    gnwb = np.zeros((L, 128, 1024), np.float32)
    for l in range(L):
        gnwb[l, :, 0:512] = inp["gn_w"][l][None, :]
        gnwb[l, :, 512:1024] = inp["gn_b"][l][None, :]
    shared = {
        "consts": consts, "vecs": vecs, "gnwb": gnwb,
        "w_in": inp["w_in"], "w_branch_a": inp["w_branch_a"], "w_branch_b": inp["w_branch_b"],
        "w_out": inp["w_out"], "w_up": inp["w_up"], "w_down": inp["w_down"], "w_ple": inp["w_ple"],
        "w_ple_gate": inp["w_ple_gate"], "decay_w2": inp["decay_w2"], "iclr_a2": inp["iclr_a2"],
        "gate_g2": inp["gate_g2"],
    }
    shared = {k: np.ascontiguousarray(np.asarray(v, np.float32)) for k, v in shared.items()}
    in_maps = []
    for (x, p), carry in zip(seqs_per_core, carry_per_core):
        m = dict(shared)
        m["xT"] = np.ascontiguousarray(x.T)
        m["pT"] = np.ascontiguousarray(np.transpose(p, (0, 2, 1)))
        mk = np.zeros((128, NSEG + 1), np.float32)
        mk[:, :] = np.asarray(carry, np.float32)[None, :]
        m["masks"] = mk
        in_maps.append(m)
    res = run_bass_kernel_spmd(nc, in_maps, core_ids=list(range(len(in_maps))))
    return res.results


def kernel(**inp):
    inp = {k: np.asarray(v) for k, v in inp.items()}
    xp, xs = inp["x_prompt"], inp["x_sample"]
    pp, psm = inp["p_prompt"], inp["p_sample"]
    L = pp.shape[0]
    SEG, NSEG = 2048, 6
    per_core = []
    carries = []
    plan = []
    for c in range(8):
        if c < 4:
            segs = [("p", c), ("s", 2 * c), ("s", 2 * c + 1)]
            carry = [0, 1, 1, 1, 0, 0, 0]
        else:
            segs = [("s", 8 + 6 * (c - 4) + i) for i in range(6)]
            carry = [0] * 7
        xs_l, ps_l = [], []
        for kind, i in segs:
            if kind == "p":
                xs_l.append(xp[i])
                ps_l.append(pp[:, i])
            else:
                xs_l.append(xs[i])
                ps_l.append(psm[:, i])
        per_core.append((np.concatenate(xs_l, axis=0), np.concatenate(ps_l, axis=1)))
        carries.append(carry)
        plan.append(segs)
    results = run_cores(per_core, carries, inp, NSEG, SEG, L)
    y_p = np.empty(xp.shape, np.float32)
    y_s = np.empty(xs.shape, np.float32)
    for c in range(8):
        y = np.ascontiguousarray(results[c]["yT"].T)
        off = 0
        for kind, i in plan[c]:
            if kind == "p":
                y_p[i] = y[off:off + 8192]
                off += 8192
            else:
                y_s[i] = y[off:off + 2048]
                off += 2048
    return (y_p, y_s)
```

```python
import numpy as np
from contextlib import ExitStack
import concourse.bass as bass
import concourse.mybir as mybir
from concourse.bass_utils import run_bass_kernel_spmd

F32, BF16 = mybir.dt.float32, mybir.dt.bfloat16
AF = mybir.ActivationFunctionType
ALU = mybir.AluOpType
AX = mybir.AxisListType

D = 1024
INC = 5504
DFF = 2816
CDEC = -0.6065306597126334
NORM_EPS = 1e-6
GN_EPS = 64 * 1e-5
GELU_C = 1.5957691216057308

C_IDENT, C_SU, C_SL, C_UI, C_LI, C_BLK, C_ONES, C_HSEL, C_RESET, C_EPS, C_GNEPS = (
    0, 128, 256, 384, 512, 640, 768, 896, 928, 1440, 1441)
NCON = 1442
V_NMP, V_NMPOST, V_NFP, V_NFPOST, V_NPLE = 0, 8, 16, 24, 32
V_CONVW, V_CONVB, V_KK, V_KA, V_RK, V_W0, V_A0 = 40, 52, 56, 60, 64, 68, 76
V_FCW, V_FCB, V_MU, V_1MKA = 84, 216, 260, 264
NV = 268


class Buf:
    __slots__ = ("lw", "rd")

    def __init__(self):
        self.lw = None
        self.rd = []


ENGS = ("pe", "act", "dve", "pool", "sp")


class Sched:
    def __init__(self, ring=8):
        self.prog = {e: [] for e in ENGS}
        self.cnt = {e: 0 for e in ENGS}
        self.known = {e: {} for e in ENGS}
        self.ring = ring
        self.dma_next = {e: 0 for e in ENGS}
        self.dma_cnt = {e: [0] * ring for e in ENGS}

    def _deps(self, eng, reads, writes):
        deps = {}

        def add(p):
            k, v = p
            if deps.get(k, 0) < v:
                deps[k] = v

        for r in reads:
            if r.lw is not None and not (r.lw[0] == eng and eng == "pe"):
                add(r.lw)
        for w in writes:
            if w.lw is not None and w.lw[0] != eng:
                add(w.lw)
            for p in w.rd:
                if p[0] != eng:
                    add(p)
        return deps

    def _filter(self, eng, deps):
        kn = self.known[eng]
        out = []
        for k, v in deps.items():
            if kn.get(k, 0) >= v:
                continue
            kn[k] = v
            out.append((k, v))
        return out

    def _commit(self, me, reads, writes):
        for r in reads:
            r.rd.append(me)
        for w in writes:
            w.lw = me
            w.rd = []

    def op(self, eng, fn, reads=(), writes=()):
        waits = self._filter(eng, self._deps(eng, reads, writes))
        self.cnt[eng] += 1
        self.prog[eng].append((waits, fn, (eng, 1)))
        self._commit((eng, self.cnt[eng]), reads, writes)

    def dma(self, eng, fn, reads=(), writes=()):
        deps = self._deps(eng, reads, writes)
        slot = self.dma_next[eng] % self.ring
        self.dma_next[eng] += 1
        key = ("dma", eng, slot)
        c = self.dma_cnt[eng][slot]
        if c > 0 and deps.get(key, 0) < 16 * c:
            deps[key] = 16 * c
        waits = self._filter(eng, deps)
        self.dma_cnt[eng][slot] = c + 1
        self.prog[eng].append((waits, fn, (key, 16)))
        self._commit((key, 16 * (c + 1)), reads, writes)

    def _all(self):
        deps = {}
        for e in ENGS:
            if e != "sp" and self.cnt[e] > 0:
                deps[e] = self.cnt[e]
            for slot in range(self.ring):
                c = self.dma_cnt[e][slot]
                if c > 0:
                    deps[("dma", e, slot)] = 16 * c
        return deps

    def barrier(self):
        deps = self._all()
        for e in ENGS:
            d = {k: v for k, v in deps.items() if k != e}
            waits = self._filter(e, d)
            if waits:
                self.prog[e].append((waits, None, None))

    def finish(self):
        self.barrier()

    def emit(self, nc):
        keys = [e for e in ENGS if e != "sp" and self.cnt[e] > 0]
        for e in ENGS:
            for slot in range(self.ring):
                if self.dma_cnt[e][slot] > 0:
                    keys.append(("dma", e, slot))
        with ExitStack() as st:
            sems = {}
            for i, k in enumerate(keys):
                sems[k] = st.enter_context(nc.semaphore("s%d" % i))
            block = st.enter_context(nc.Block())

            def run(engname):
                def body(e):
                    for waits, fn, inc in self.prog[engname]:
                        for k, v in waits:
                            e.wait_ge(sems[k], v)
                        if fn is not None:
                            fn(e).then_inc(sems[inc[0]], inc[1])
                return body

            block.sync(run("sp"))
            block.tensor(run("pe"))
            block.scalar(run("act"))
            block.vector(run("dve"))
            block.gpsimd(run("pool"))


class Arena:
    def __init__(self, ap, size):
        self.ap, self.size, self.off = ap, size, 0

    def alloc(self, n, dtype):
        n16 = n * (2 if dtype == F32 else 1)
        start = (self.off + 15) // 16 * 16
        assert start + n16 <= self.size, ("arena overflow", start + n16, self.size)
        v = self.ap[:, start:start + n16]
        if dtype == F32:
            v = v.bitcast(F32)
        self.off = start + n16
        return v


def build_program(NSEG, SEG, DEPTH, debug=False, upto=99):
    NT = NSEG * SEG
    NCH = NT // 128
    CPS = SEG // 128
    assert NT % 512 == 0 and SEG % 128 == 0
    nc = bass.Bass("TRN2", target_bir_lowering=False)
    L = DEPTH

    def din(name, shape, dt=F32):
        return nc.dram_tensor(name, list(shape), dt, kind="ExternalInput").ap()

    def dscr(name, shape, dt):
        kind = "ExternalOutput" if debug else "Internal"
        return nc.dram_tensor(name, list(shape), dt, kind=kind).ap()

    xT = din("xT", [D, NT])
    pT = din("pT", [L, 256, NT])
    masks_d = din("masks", [128, NSEG + 1])
    consts_d = din("consts", [128, NCON])
    vecs_d = din("vecs", [L, 128, NV])
    gnwb_d = din("gnwb", [L, 128, 1024])
    w_in = din("w_in", [L, D, INC])
    w_a = din("w_branch_a", [L, 512, D])
    w_b = din("w_branch_b", [L, 512, D])
    w_out = din("w_out", [L, D, D])
    w_up = din("w_up", [L, D, 2 * DFF])
    w_down = din("w_down", [L, DFF, D])
    w_ple = din("w_ple", [L, 256, D])
    w_pg = din("w_ple_gate", [L, D, D])
    dw2_d = din("decay_w2", [L, 2, 64, 512])
    a2_d = din("iclr_a2", [L, 2, 64, 512])
    g2_d = din("gate_g2", [L, 128, 512])
    yT = nc.dram_tensor("yT", [D, NT], F32, kind="ExternalOutput").ap()

    qS = dscr("qS", [512, NT + 2], BF16)
    bgS = dscr("bgS", [512, NT], BF16)
    rS = dscr("rS", [512, NT], BF16)
    kS = dscr("kS", [512, NT], BF16)
    vS = dscr("vS", [NT, 512], BF16)
    zS = dscr("zS", [256, NT + 2], F32)
    sgdS = dscr("sgdS", [128, NT], BF16)
    sgcS = dscr("sgcS", [D, NT], BF16)
    sgrS = dscr("sgrS", [D, NT], BF16)
    rkS = dscr("rkS", [NT, 8], F32)
    fmS = dscr("fmS", [2, NCH, 64, 8 * 4 * 128], BF16)
    tmS = dscr("tmS", [2, NT, 1024], BF16)
    gcS = dscr("gcS", [2, NCH, 128, 4], F32)
    yS = dscr("yS", [2, NT, 512], F32)
    x1S = dscr("x1S", [D, NT + 2], F32)
    xL = dscr("xL", [D, NT], F32)
    wupS = dscr("wupS", [L, 22, 128, 8 * 2 * 128], BF16)

    S = Sched()
    st = ExitStack()
    ARENA_N = 90 * 1024
    arena_t = st.enter_context(nc.sbuf_tensor("arena", [128, ARENA_N], BF16))
    cf = st.enter_context(nc.sbuf_tensor("cf", [128, NCON], F32))
    cb = st.enter_context(nc.sbuf_tensor("cb", [128, 256], BF16))
    mk = st.enter_context(nc.sbuf_tensor("mk", [128, NSEG + 1], F32))
    vec = st.enter_context(nc.sbuf_tensor("vec", [128, NV], F32))
    psum = [st.enter_context(nc.psum_tensor("ps%d" % i, [128, 512], F32)) for i in range(8)]
    psb = [Buf() for _ in range(8)]
    A = Arena(arena_t, ARENA_N)
    b_c = Buf()
    b_vec = Buf()
    ident_bf = cb[:, 0:128]
    ones_bf = cb[:, 128:256]
    pctr = [0]

    def nextps():
        i = pctr[0] % 8
        pctr[0] += 1
        return psum[i], psb[i]

    def dma(out, in_, r=(), w=()):
        S.dma("sp", lambda e: e.dma_start(out=out, in_=in_), r, w)

    def mm(out, lhsT, rhs, start, stop, r, w):
        S.op("pe", lambda e: e.matmul(out, lhsT=lhsT, rhs=rhs, start=start, stop=stop), r, w)

    def transpose(out, in_, r, w):
        S.op("pe", lambda e: e.transpose(out, in_, ident_bf), list(r) + [b_c], w)

    def act(out, in_, func, r, w, bias=None, scale=None):
        kw = {}
        if bias is not None:
            kw["bias"] = bias
        if scale is not None:
            kw["scale"] = scale
        S.op("act", lambda e: e.activation(out=out, in_=in_, func=func, **kw), r, w)

    def tt(eng, out, in0, in1, op, r, w):
        S.op(eng, lambda e: e.tensor_tensor(out=out, in0=in0, in1=in1, op=op), r, w)

    def ts(eng, out, in0, s1, s2, op0, op1, r, w):
        if op1 is None:
            S.op(eng, lambda e: e.tensor_scalar(out=out, in0=in0, scalar1=s1, scalar2=None, op0=op0), r, w)
        else:
            S.op(eng, lambda e: e.tensor_scalar(out=out, in0=in0, scalar1=s1, scalar2=s2, op0=op0, op1=op1), r, w)

    def stt(out, in0, scalar, in1, op0, op1, r, w):
        S.op("dve", lambda e: e.scalar_tensor_tensor(out=out, in0=in0, scalar=scalar, in1=in1, op0=op0, op1=op1), r, w)

    def cp(eng, out, in_, r, w):
        if eng == "act":
            S.op("act", lambda e: e.copy(out=out, in_=in_), r, w)
        else:
            S.op(eng, lambda e: e.tensor_copy(out=out, in_=in_), r, w)

    def amul(out, in_, m, r, w):
        S.op("act", lambda e: e.mul(out=out, in_=in_, mul=m), r, w)

    def memset(eng, ap, val, w):
        S.op(eng, lambda e: e.memset(ap, val), (), w)

    def recip(out, in_, r, w):
        S.op("dve", lambda e: e.reciprocal(out=out, in_=in_), r, w)

    cast_rr = [0]

    def load_weight(dst, src, stages, bst, bdst):
        n = dst.shape[-1]
        i = cast_rr[0]
        cast_rr[0] += 1
        sg = stages[i % len(stages)][:, 0:n]
        bs = bst[i % len(stages)]
        dma(sg, src, (), [bs])
        cp(("dve", "act", "pool")[i % 3], dst, sg, [bs], [bdst])

    def halo_fix(eng, ap, b, rbufs, wbufs):
        if b == 0 or b == NSEG:
            memset(eng, ap, 0.0, wbufs)
        else:
            np_ = ap.shape[0]
            ts(eng, ap, ap, mk[0:np_, b:b + 1], None, ALU.mult, None, list(rbufs) + [b_c], wbufs)

    def rms_rstd(sq_chunks, nchunk, W, rstd_tmp, rstd, b_sq, b_tmp, b_rstd):
        ps, pb = nextps()
        for c in range(nchunk):
            mm(ps[:, 0:W], ones_bf, sq_chunks(c), c == 0, c == nchunk - 1, [b_sq, b_c], [pb])
        act(rstd_tmp, ps[:, 0:W], AF.Sqrt, [pb, b_c], [b_tmp], bias=cf[:, C_EPS:C_EPS + 1], scale=1.0 / D)
        recip(rstd, rstd_tmp, [b_tmp], [b_rstd])

    dma(cf[:], consts_d[:, :], (), [b_c])
    dma(mk[:], masks_d[:, :], (), [b_c])
    cp("dve", ident_bf, cf[:, C_IDENT:C_IDENT + 128], [b_c], [b_c])
    cp("dve", ones_bf, cf[:, C_ONES:C_ONES + 128], [b_c], [b_c])
    SU = cf[:, C_SU:C_SU + 128]
    SL = cf[:, C_SL:C_SL + 128]
    UI = cf[:, C_UI:C_UI + 128]
    LI = cf[:, C_LI:C_LI + 128]
    BLK = cf[:, C_BLK:C_BLK + 128]
    RESET = cf[:, C_RESET:C_RESET + 512]

    def pass_W0():
        A.off = 0
        stg = [A.alloc(2 * DFF, F32) for _ in range(2)]
        s16 = [A.alloc(2 * DFF, BF16) for _ in range(2)]
        bs = [Buf(), Buf()]
        b16 = [Buf(), Buf()]
        i = 0
        for l in range(L):
            for kc in range(8):
                s = i % 2
                dma(stg[s], w_up[l, kc * 128:(kc + 1) * 128, :], (), [bs[s]])
                cp(("dve", "act", "pool")[i % 3], s16[s], stg[s], [bs[s]], [b16[s]])
                for gv in range(2):
                    dst = wupS[l].rearrange("j p (k g m) -> p j k g m", k=8, g=2)[:, :, kc, gv, :]
                    src = s16[s][:, gv * DFF:(gv + 1) * DFF].rearrange("p (j m) -> p j m", m=128)
                    dma(dst, src, [b16[s]], ())
                i += 1
        S.barrier()

    def pass_P1(l, xin):
        A.off = 0
        W1 = A.alloc(8 * INC, BF16).rearrange("p (k n) -> p k n", k=8)
        bW1 = Buf()
        mark = A.off
        stg = [A.alloc(INC, F32) for _ in range(2)]
        bst = [Buf(), Buf()]
        dma(vec[:], vecs_d[l], (), [b_vec])
        for kc in range(8):
            load_weight(W1[:, kc, :], w_in[l, kc * 128:(kc + 1) * 128, :], stg, bst, bW1)
        S.barrier()
        A.off = mark
        xt = A.alloc(8 * 512, F32).rearrange("p (c t) -> p c t", c=8)
        sq = A.alloc(8 * 512, BF16).rearrange("p (c t) -> p c t", c=8)
        ub = A.alloc(8 * 512, BF16).rearrange("p (c t) -> p c t", c=8)
        rtmp = A.alloc(512, F32)
        rstd = A.alloc(512, F32)
        hc_s = A.alloc(4 * 512, BF16).rearrange("p (c t) -> p c t", c=4)
        bg_s = A.alloc(4 * 512, BF16).rearrange("p (c t) -> p c t", c=4)
        q_s = A.alloc(4 * 512, BF16).rearrange("p (c t) -> p c t", c=4)
        r_s = A.alloc(4 * 512, BF16).rearrange("p (c t) -> p c t", c=4)
        k_s = A.alloc(4 * 512, BF16).rearrange("p (c t) -> p c t", c=4)
        v_s = A.alloc(4 * 512, BF16).rearrange("p (c t) -> p c t", c=4)
        z_s = A.alloc(2 * 512, F32).rearrange("p (c t) -> p c t", c=2)
        sgd_s = A.alloc(512, BF16)
        sgc_s = A.alloc(8 * 512, BF16).rearrange("p (c t) -> p c t", c=8)
        sgr_s = A.alloc(8 * 512, BF16).rearrange("p (c t) -> p c t", c=8)
        b_xt, b_sq, b_ub, b_rt, b_rs = Buf(), Buf(), Buf(), Buf(), Buf()
        b_hc, b_bg, b_q, b_r, b_k, b_v, b_z, b_sgd, b_sgc, b_sgr = [Buf() for _ in range(10)]

        def fm(ap):
            return ap.rearrange("(c p) t -> p c t", p=128)

        import os
        P1LVL = int(os.environ.get("P1LVL", "20"))
        for ti in range(NT // 512):
            if P1LVL < 1:
                break
            t0 = ti * 512
            dma(xt, fm(xin)[:, :, t0:t0 + 512], (), [b_xt])
            act(sq, xt, AF.Square, [b_xt], [b_sq])
            if P1LVL < 2:
                continue
            rms_rstd(lambda c: sq[:, c, :], 8, 512, rtmp, rstd, b_sq, b_rt, b_rs)
            if P1LVL < 3:
                continue
            for c in range(8):
                stt(ub[:, c, :], xt[:, c, :], vec[:, V_NMP + c:V_NMP + c + 1], rstd, ALU.mult, ALU.mult,
                    [b_xt, b_rs, b_vec], [b_ub])

            def proj(f):
                ps, pb = nextps()
                for kc in range(8):
                    mm(ps[:], W1[:, kc, f * 128:(f + 1) * 128], ub[:, kc, :], kc == 0, kc == 7, [bW1, b_ub], [pb])
                return ps, pb

            if P1LVL < 4:
                continue
            for f in range(43):
                if 20 <= f < 24:
                    continue
                if P1LVL < 20 and f >= {4: 4, 5: 8, 6: 12, 7: 20, 8: 27, 9: 28, 10: 34, 11: 35, 12: 42, 13: 43}[P1LVL]:
                    continue
                ps, pb = proj(f)
                if f < 4:
                    cp("act", hc_s[:, f, :], ps[:], [pb], [b_hc])
                elif f < 8:
                    cp("act", bg_s[:, f - 4, :], ps[:], [pb], [b_bg])
                    if f == 7:
                        dma(fm(bgS)[:, :, t0:t0 + 512], bg_s, [b_bg], ())
                elif f < 12:
                    tt("dve", q_s[:, f - 8, :], ps[:], hc_s[:, f - 8, :], ALU.mult, [pb, b_hc], [b_q])
                    if f == 11:
                        dma(fm(qS)[:, :, 1 + t0:1 + t0 + 512], q_s, [b_q], ())
                elif f < 16:
                    cp("act", r_s[:, f - 12, :], ps[:], [pb], [b_r])
                    if f == 15:
                        dma(fm(rS)[:, :, t0:t0 + 512], r_s, [b_r], ())
                elif f < 20:
                    cp("dve", k_s[:, f - 16, :], ps[:], [pb], [b_k])
                    if f == 19:
                        dma(fm(kS)[:, :, t0:t0 + 512], k_s, [b_k], ())
                elif f < 26:
                    cp("act", z_s[:, f - 24, :], ps[:], [pb], [b_z])
                    if f == 25:
                        dma(fm(zS)[:, :, 1 + t0:1 + t0 + 512], z_s, [b_z], ())
                elif f == 26:
                    act(sgd_s, ps[:], AF.Sigmoid, [pb], [b_sgd])
                    dma(sgdS[:, t0:t0 + 512], sgd_s, [b_sgd], ())
                elif f < 35:
                    act(sgc_s[:, f - 27, :], ps[:], AF.Sigmoid, [pb], [b_sgc])
                    if f == 34:
                        dma(fm(sgcS)[:, :, t0:t0 + 512], sgc_s, [b_sgc], ())
                else:
                    act(sgr_s[:, f - 35, :], ps[:], AF.Sigmoid, [pb], [b_sgr])
                    if f == 42:
                        dma(fm(sgrS)[:, :, t0:t0 + 512], sgr_s, [b_sgr], ())
            for tb in range(4):
                ps, pb = nextps()
                for kc in range(8):
                    mm(ps[:], ub[:, kc, tb * 128:(tb + 1) * 128], W1[:, kc, 2560:3072], kc == 0, kc == 7,
                       [bW1, b_ub], [pb])
                cp(("dve", "act")[tb % 2], v_s[:, tb, :], ps[:], [pb], [b_v])
            dma(vS[t0:t0 + 512, :].rearrange("(b p) f -> p b f", p=128), v_s, [b_v], ())
        S.barrier()

    def pass_P2pre(l):
        A.off = 0
        dw2 = A.alloc(2 * 512, BF16).rearrange("p (d n) -> p d n", d=2)
        a2 = A.alloc(2 * 512, BF16).rearrange("p (d n) -> p d n", d=2)
        stg = [A.alloc(512, F32) for _ in range(2)]
        bst = [Buf(), Buf()]
        b_w = Buf()
        for d in range(2):
            load_weight(dw2[0:64, d, :], dw2_d[l, d], [s[0:64] for s in stg], bst, b_w)
            load_weight(a2[0:64, d, :], a2_d[l, d], [s[0:64] for s in stg], bst, b_w)
        ts("dve", vec[:, V_1MKA:V_1MKA + 4], vec[:, V_KA:V_KA + 4], -1.0, 1.0, ALU.mult, ALU.add, [b_vec], [b_vec])
        r_t = A.alloc(4 * 512, BF16).rearrange("p (c t) -> p c t", c=4)
        k_t = A.alloc(4 * 512, BF16).rearrange("p (c t) -> p c t", c=4)
        zt = [A.alloc(514, F32) for _ in range(4)]
        kk = A.alloc(4 * 512, F32).rearrange("p (c t) -> p c t", c=4)
        prod = A.alloc(4 * 512, F32).rearrange("p (c t) -> p c t", c=4)
        rk_s = A.alloc(32, F32)
        zs = [A.alloc(512, F32) for _ in range(2)]
        tz = A.alloc(512, BF16)
        zab = A.alloc(512, BF16)

        class TS:
            pass

        sets = []
        for _i in range(2):
            X = TS()
            for nm in ("t1", "t2", "t3", "sgm", "av", "cs", "ex", "rs_", "ri", "kd", "bb"):
                setattr(X, nm, A.alloc(512, F32))
                setattr(X, "b_" + nm, Buf())
            X.E = [A.alloc(512, F32) for _ in range(4)]
            X.b_E = [Buf() for _ in range(4)]
            sets.append(X)
        fmst = [A.alloc(4 * 4 * 128, BF16).rearrange("p (c a t) -> p c a t", c=4, a=4) for _ in range(2)]
        sc_s = A.alloc(2 * 4 * 512, BF16).rearrange("p (a c t) -> p a c t", a=2, c=4)
        tm_s = A.alloc(4 * 2 * 512, BF16).rearrange("p (b a f) -> p b a f", b=4, a=2)
        gc_s = A.alloc(16, F32).rearrange("p (c f) -> p c f", c=4)
        b_r, b_k, b_kk, b_prod, b_rk = [Buf() for _ in range(5)]
        b_z = [Buf() for _ in range(4)]
        b_zs = [Buf(), Buf()]
        b_tz, b_zab = Buf(), Buf()
        b_fm = [Buf(), Buf()]
        b_sc, b_tm, b_gc = Buf(), Buf(), Buf()

        def fm(ap):
            return ap.rearrange("(c p) t -> p c t", p=128)

        for ti in range(NT // 512):
            t0 = ti * 512
            dma(r_t, fm(rS)[:, :, t0:t0 + 512], (), [b_r])
            dma(k_t, fm(kS)[:, :, t0:t0 + 512], (), [b_k])
            for i in range(4):
                dma(zt[i][0:64, :], zS[i * 64:(i + 1) * 64, t0:t0 + 514], (), [b_z[i]])
            if t0 % SEG == 0:
                for i in (0, 1):
                    halo_fix("pool", zt[i][0:64, 0:1], t0 // SEG, [b_z[i]], [b_z[i]])
            if (t0 + 512) % SEG == 0:
                for i in (2, 3):
                    halo_fix("pool", zt[i][0:64, 513:514], (t0 + 512) // SEG, [b_z[i]], [b_z[i]])
            for fc in range(4):
                X = sets[fc % 2]
                t1, t2, t3, b_t1, b_t2, b_t3 = X.t1, X.t2, X.t3, X.b_t1, X.b_t2, X.b_t3
                ts("dve", t1, k_t[:, fc, :], vec[:, V_KK + fc:V_KK + fc + 1], None, ALU.mult, None, [b_k, b_vec], [b_t1])
                tt("pool", t2, t1, t1, ALU.mult, [b_t1], [b_t2])
                ps, pb = nextps()
                mm(ps[:], BLK, t2, True, True, [b_t2, b_c], [pb])
                act(t3, ps[:], AF.Sqrt, [pb], [b_t3])
                ts("dve", t3, t3, 1e-12, None, ALU.max, None, [b_t3], [b_t3])
                recip(t3, t3, [b_t3], [b_t3])
                tt("dve", kk[:, fc, :], t1, t3, ALU.mult, [b_t1, b_t3], [b_kk])
                stt(prod[:, fc, :], r_t[:, fc, :], vec[:, V_RK + fc:V_RK + fc + 1], k_t[:, fc, :], ALU.mult, ALU.mult,
                    [b_r, b_k, b_vec], [b_prod])
            ps, pb = nextps()
            for tb in range(4):
                for fc in range(4):
                    mm(ps[:, tb * 8:(tb + 1) * 8], prod[:, fc, tb * 128:(tb + 1) * 128],
                       cf[:, C_HSEL + fc * 8:C_HSEL + (fc + 1) * 8], fc == 0, fc == 3, [b_prod, b_c], [pb])
            cp("act", rk_s, ps[:, 0:32], [pb], [b_rk])
            dma(rkS[t0:t0 + 512, :].rearrange("(b p) h -> p b h", p=128), rk_s.rearrange("p (b h) -> p b h", b=4),
                [b_rk], ())
            for d in range(2):
                for part in range(2):
                    t1, b_t1 = sets[part].t1, sets[part].b_t1
                    Z = zt[d * 2 + part]
                    cur = Z[0:64, 1:513]
                    sh = Z[0:64, 0:512] if d == 0 else Z[0:64, 2:514]
                    bz = b_z[d * 2 + part]
                    tt("pool", t1[0:64, :], sh, cur, ALU.subtract, [bz], [b_t1])
                    stt(zs[part][0:64, :], t1[0:64, :], vec[0:64, V_MU + d * 2 + part:V_MU + d * 2 + part + 1], cur,
                        ALU.mult, ALU.add, [b_t1, bz, b_vec], [b_zs[part]])
                act(tz[0:64, :], zs[0][0:64, :], AF.Tanh, [b_zs[0]], [b_tz])
                cp("dve", zab[0:64, :], zs[1][0:64, :], [b_zs[1]], [b_zab])
                for fc in range(4):
                    X = sets[fc % 2]
                    t2, sgm, av, cs, ex, rs_, ri, kd, bb, E = X.t2, X.sgm, X.av, X.cs, X.ex, X.rs_, X.ri, X.kd, X.bb, X.E
                    b_t2, b_sgm, b_av, b_cs, b_ex, b_rs, b_ri, b_kd, b_bb, b_E = (
                        X.b_t2, X.b_sgm, X.b_av, X.b_cs, X.b_ex, X.b_rs_, X.b_ri, X.b_kd, X.b_bb, X.b_E)
                    ps, pb = nextps()
                    mm(ps[:], dw2[0:64, d, fc * 128:(fc + 1) * 128], tz[0:64, :], True, True, [b_w, b_tz], [pb])
                    act(sgm, ps[:], AF.Sigmoid, [pb, b_vec], [b_sgm],
                        bias=vec[:, V_W0 + d * 4 + fc:V_W0 + d * 4 + fc + 1], scale=1.0)
                    ps2, pb2 = nextps()
                    mm(ps2[:], a2[0:64, d, fc * 128:(fc + 1) * 128], zab[0:64, :], True, True, [b_w, b_zab], [pb2])
                    act(av, ps2[:], AF.Sigmoid, [pb2, b_vec], [b_av],
                        bias=vec[:, V_A0 + d * 4 + fc:V_A0 + d * 4 + fc + 1], scale=1.0)
                    S.op("dve", lambda e, cs=cs, sgm=sgm: e.tensor_tensor_scan(out=cs, data0=RESET, data1=sgm, initial=0.0,
                                                                               op0=ALU.mult, op1=ALU.add),
                         [b_sgm, b_c], [b_cs])
                    cs3 = cs.rearrange("p (c t) -> p c t", c=4)
                    tot_bc = cs3[:, :, 127:128].to_broadcast([128, 4, 128])
                    tt("pool", ex, cs, sgm, ALU.subtract, [b_cs, b_sgm], [b_ex])
                    tt("dve", rs_.rearrange("p (c t) -> p c t", c=4), tot_bc, cs3, ALU.subtract, [b_cs], [b_rs])
                    if d == 0:
                        e1s, e2s, e4s = cs, ex, rs_
                        br1, br2, br4 = b_cs, b_ex, b_rs
                    else:
                        tt("pool", ri, rs_, sgm, ALU.add, [b_rs, b_sgm], [b_ri])
                        e1s, e2s, e4s = ri, rs_, ex
                        br1, br2, br4 = b_ri, b_rs, b_ex
                    act(E[0], e1s, AF.Exp, [br1], [b_E[0]], scale=CDEC)
                    act(E[1], e2s, AF.Exp, [br2], [b_E[1]], scale=CDEC)
                    act(E[2], e1s, AF.Exp, [br1], [b_E[2]], scale=-CDEC)
                    act(E[3], e4s, AF.Exp, [br4], [b_E[3]], scale=CDEC)
                    act(gc_s[:, :, fc], cs3[:, :, 127], AF.Exp, [b_cs], [b_gc], scale=CDEC)
                    ts("dve", t2, av, vec[:, V_KA + fc:V_KA + fc + 1], vec[:, V_1MKA + fc:V_1MKA + fc + 1],
                       ALU.mult, ALU.add, [b_av, b_vec], [b_t2])
                    tt("dve", kd, t2, k_t[:, fc, :], ALU.mult, [b_t2, b_k], [b_kd])
                    tt("pool", bb, kk[:, fc, :], av, ALU.mult, [b_kk, b_av], [b_bb])
                    F_ = fmst[fc % 2]
                    bF = b_fm[fc % 2]

                    def v4(ap):
                        return ap.rearrange("p (c t) -> p c t", c=4)

                    tt("pool", F_[:, :, 0, :], v4(kk[:, fc, :]), v4(E[1]), ALU.mult, [b_kk, b_E[1]], [bF])
                    tt("dve", F_[:, :, 1, :], v4(bb), v4(E[2]), ALU.mult, [b_bb, b_E[2]], [bF])
                    tt("dve", F_[:, :, 2, :], v4(kd), v4(E[2]), ALU.mult, [b_kd, b_E[2]], [bF])
                    tt("pool", F_[:, :, 3, :], v4(r_t[:, fc, :]), v4(E[0]), ALU.mult, [b_r, b_E[0]], [bF])
                    tt("dve", sc_s[:, 0, fc, :], kd, E[3], ALU.mult, [b_kd, b_E[3]], [b_sc])
                    tt("pool", sc_s[:, 1, fc, :], bb, E[3], ALU.mult, [b_bb, b_E[3]], [b_sc])
                    for half in range(2):
                        hp = 2 * fc + half
                        dst = fmS[d, t0 // 128:t0 // 128 + 4].rearrange("c k (h x) -> k c h x", h=8)[:, :, hp, :]
                        dma(dst, F_[half * 64:(half + 1) * 64].rearrange("p c a t -> p c (a t)"), [bF], ())
                dma(gcS[d, t0 // 128:t0 // 128 + 4].rearrange("c p f -> p c f"), gc_s, [b_gc], ())
                for a_ in range(2):
                    for tb in range(4):
                        ps, pb = nextps()
                        psT = ps[:].bitcast(BF16)
                        for fc in range(4):
                            transpose(psT[:, fc * 128:(fc + 1) * 128], sc_s[:, a_, fc, tb * 128:(tb + 1) * 128],
                                      [b_sc], [pb])
                        cp(("act", "dve")[tb % 2], tm_s[:, tb, a_, :], psT[:, 0:512], [pb], [b_tm])
                dma(tmS[d, t0:t0 + 512, :].rearrange("(b p) x -> p b x", p=128),
                    tm_s.rearrange("p b a f -> p b (a f)"), [b_tm], ())
        S.barrier()

    def pass_P2scan(l):
        A.off = 0
        NG = NCH * 8
        gall = [A.alloc(NG, F32).rearrange("p (c a f) -> p c a f", a=2, f=4) for _ in range(2)]
        b_gall = Buf()
        for d in range(2):
            for half in range(2):
                for c0 in range(0, NCH, 16):
                    c1 = min(NCH, c0 + 16)
                    dma(gall[d][0:64, c0:c1, half, :],
                        gcS[d, c0:c1, half * 64:(half + 1) * 64, :].rearrange("c p f -> p c f"), (), [b_gall])

        class Ctx:
            pass

        def h4(n=1):
            return [A.alloc(512, BF16).rearrange("p (h t) -> p h t", h=4) for _ in range(n)]

        ctxs = []
        for d in range(2):
            cx = Ctx()
            cx.d = d
            cx.fm = [A.alloc(8 * 4 * 128, BF16).rearrange("p (h a t) -> p h a t", h=8, a=4) for _ in range(3)]
            cx.tm = [A.alloc(1024, BF16) for _ in range(3)]
            cx.v = [A.alloc(512, BF16) for _ in range(3)]
            cx.b_in = [Buf() for _ in range(3)]
            cx.N = [h4(2) for _ in range(2)]
            cx.NT = [h4(2) for _ in range(2)]
            cx.bN = [[Buf(), Buf()] for _ in range(2)]
            cx.bNT = [[Buf(), Buf()] for _ in range(2)]
            cx.P = [h4(2) for _ in range(2)]
            cx.ARB = [h4(2) for _ in range(2)]
            cx.AKD = [h4(2) for _ in range(2)]
            cx.ARKD = [h4(2) for _ in range(2)]
            cx.bP = [[Buf(), Buf()] for _ in range(2)]
            cx.bARB = [[Buf(), Buf()] for _ in range(2)]
            cx.bAKD = [[Buf(), Buf()] for _ in range(2)]
            cx.bARKD = [[Buf(), Buf()] for _ in range(2)]
            cx.Xn = A.alloc(512, BF16)
            cx.bXn = Buf()
            cx.U = A.alloc(512, BF16)
            cx.bU = Buf()
            cx.ybuf = [A.alloc(512, F32) for _ in range(2)]
            cx.bY = [Buf(), Buf()]
            cx.S32 = [A.alloc(512, F32) for _ in range(2)]
            cx.Sbf = [A.alloc(512, BF16) for _ in range(2)]
            cx.bS32 = [Buf(), Buf()]
            cx.bSbf = [Buf(), Buf()]
            cx.t1 = A.alloc(512, F32)
            cx.bt1 = Buf()
            cx.cur = 0
            ctxs.append(cx)

        ident4 = cb[:, 0:128].rearrange("p (o t) -> p o t", o=1).to_broadcast([128, 4, 128])

        def m4(m):
            return m.rearrange("p (o t) -> p o t", o=1).to_broadcast([128, 4, 128])

        def ps4(ps):
            return ps[:].rearrange("p (h t) -> p h t", h=4)

        def chunk_of(cx, it):
            return it if cx.d == 0 else NCH - 1 - it

        def load(cx, it):
            c = chunk_of(cx, it)
            i3 = it % 3
            d = cx.d
            dma(cx.fm[i3][0:64].rearrange("p h a t -> p (h a t)"), fmS[d, c], (), [cx.b_in[i3]])
            dma(cx.tm[i3], tmS[d, c * 128:(c + 1) * 128, :], (), [cx.b_in[i3]])
            dma(cx.v[i3], vS[c * 128:(c + 1) * 128, :], (), [cx.b_in[i3]])

        def prod4(g, lf, rf, rbufs):
            ps, pb = nextps()
            for i in range(4):
                h = g * 4 + i
                mm(ps[:, i * 128:(i + 1) * 128], lf(h), rf(h), True, True, rbufs, [pb])
            return ps, pb

        def local(cx, it, g):
            d = cx.d
            i3 = it % 3
            par = it % 2
            fm_ = cx.fm[i3]
            b_in = cx.b_in[i3]
            mS, mSt, mR = (SU, SL, UI) if d == 0 else (SL, SU, LI)
            KK = lambda h: fm_[0:64, h, 0, :]
            BH = lambda h: fm_[0:64, h, 1, :]
            KD = lambda h: fm_[0:64, h, 2, :]
            RT = lambda h: fm_[0:64, h, 3, :]
            cn = 0
            N, NT_ = cx.N[g], cx.NT[g]
            bN, bNT = cx.bN[g], cx.bNT[g]
            P, bP = cx.P[par][g], cx.bP[par][g]
            ps, pb = prod4(g, BH, KK, [b_in])
            tt("dve", N[cn], ps4(ps), m4(mS), ALU.mult, [pb, b_c], [bN[cn]])
            yield
            ps, pb = prod4(g, KK, BH, [b_in])
            tt("dve", NT_[cn], ps4(ps), m4(mSt), ALU.mult, [pb, b_c], [bNT[cn]])
            tt("pool", P, ident4, N[cn], ALU.subtract, [b_c, bN[cn]], [bP])
            yield
            for lvl in range(1, 7):
                nn = 1 - cn
                if lvl < 6:
                    ps, pb = prod4(g, lambda h: NT_[cn][:, h % 4, :], lambda h: N[cn][:, h % 4, :], [bN[cn], bNT[cn]])
                    cp("act", N[nn], ps4(ps), [pb], [bN[nn]])
                ps, pb = prod4(g, lambda h: N[cn][:, h % 4, :], lambda h: NT_[cn][:, h % 4, :], [bN[cn], bNT[cn]])
                cp("act", NT_[nn], ps4(ps), [pb], [bNT[nn]])
                yield
                ps, pb = prod4(g, lambda h: NT_[nn][:, h % 4, :], lambda h: P[:, h % 4, :], [bNT[nn], bP])
                tt("dve", P, ps4(ps), P, ALU.add, [pb, bP], [bP])
                cn = nn
                yield
                if lvl == 1:
                    ps, pb = prod4(g, BH, RT, [b_in])
                    tt("dve", cx.ARB[par][g], ps4(ps), m4(mR), ALU.mult, [pb, b_c], [cx.bARB[par][g]])
                    yield
                elif lvl == 2:
                    ps, pb = prod4(g, KD, KK, [b_in])
                    tt("dve", cx.AKD[par][g], ps4(ps), m4(mS), ALU.mult, [pb, b_c], [cx.bAKD[par][g]])
                    yield
                elif lvl == 3:
                    ps, pb = prod4(g, KD, RT, [b_in])
                    tt("dve", cx.ARKD[par][g], ps4(ps), m4(mR), ALU.mult, [pb, b_c], [cx.bARKD[par][g]])
                    yield

        def chain(cx, it):
            d = cx.d
            c = chunk_of(cx, it)
            i3 = it % 3
            par = it % 2
            fm_, tm_, v_ = cx.fm[i3], cx.tm[i3], cx.v[i3]
            b_in = cx.b_in[i3]
            KK = lambda h: fm_[0:64, h, 0, :]
            RT = lambda h: fm_[0:64, h, 3, :]
            Vh = lambda h: v_[:, h * 64:(h + 1) * 64]
            KDs = lambda h: tm_[:, h * 64:(h + 1) * 64]
            Bs = lambda h: tm_[:, 512 + h * 64:512 + (h + 1) * 64]
            P, bP = cx.P[par], cx.bP[par]
            ARB, bARB = cx.ARB[par], cx.bARB[par]
            AKD, bAKD = cx.AKD[par], cx.bAKD[par]
            ARKD, bARKD = cx.ARKD[par], cx.bARKD[par]
            cur = cx.cur
            nxt = 1 - cur
            S32, Sbf = cx.S32[cur], cx.Sbf[cur]
            bS32, bSbf = cx.bS32[cur], cx.bSbf[cur]
            first = (c % CPS == 0) if d == 0 else ((c + 1) % CPS == 0)
            if first:
                b = c // CPS if d == 0 else (c + 1) // CPS
                halo_fix("pool", S32[0:64, :], b, [bS32], [bS32])
                halo_fix("pool", Sbf[0:64, :], b, [bSbf], [bSbf])
            S32v = S32[0:64, :].rearrange("p (f a v) -> p a f v", f=4, a=2)
            t1v = cx.t1[0:64, :].rearrange("p (f a v) -> p a f v", f=4, a=2)
            gCv = gall[d][0:64, c].rearrange("p a (f o) -> p a f o", o=1).to_broadcast([64, 2, 4, 64])
            tt("pool", t1v, S32v, gCv, ALU.mult, [bS32, b_gall], [cx.bt1])
            ps, pb = nextps()
            for h in range(8):
                g = h // 4
                mm(ps[:, h * 64:(h + 1) * 64], KK(h), Sbf[0:64, h * 64:(h + 1) * 64], True, False, [b_in, bSbf], [pb])
                mm(ps[:, h * 64:(h + 1) * 64], AKD[g][:, h % 4, :], Vh(h), False, True, [bAKD[g], b_in], [pb])
            amul(cx.Xn, ps[:], -1.0, [pb], [cx.bXn])
            yield
            ps, pb = nextps()
            for h in range(8):
                g = h // 4
                mm(ps[:, h * 64:(h + 1) * 64], P[g][:, h % 4, :], cx.Xn[:, h * 64:(h + 1) * 64], True, True,
                   [bP[g], cx.bXn], [pb])
            cp("dve", cx.U, ps[:], [pb], [cx.bU])
            yield
            ps, pb = nextps()
            for h in range(8):
                o = ps[0:64, h * 64:(h + 1) * 64]
                mm(o, KDs(h), Vh(h), True, False, [b_in], [pb])
                mm(o, Bs(h), cx.U[:, h * 64:(h + 1) * 64], False, True, [b_in, cx.bU], [pb])
            tt("dve", cx.Sbf[nxt][0:64, :], ps[0:64, :], cx.t1[0:64, :], ALU.add, [pb, cx.bt1], [cx.bSbf[nxt]])
            tt("dve", cx.S32[nxt][0:64, :], ps[0:64, :], cx.t1[0:64, :], ALU.add, [pb, cx.bt1], [cx.bS32[nxt]])
            cx.cur = nxt
            yield
            ps, pb = nextps()
            for h in range(8):
                g = h // 4
                o = ps[:, h * 64:(h + 1) * 64]
                mm(o, RT(h), Sbf[0:64, h * 64:(h + 1) * 64], True, False, [b_in, bSbf], [pb])
                mm(o, ARKD[g][:, h % 4, :], Vh(h), False, False, [bARKD[g], b_in], [pb])
                mm(o, ARB[g][:, h % 4, :], cx.U[:, h * 64:(h + 1) * 64], False, True, [bARB[g], cx.bU], [pb])
            yb_, bY = cx.ybuf[par], cx.bY[par]
            cp("act", yb_, ps[:], [pb], [bY])
            dma(yS[d, c * 128:(c + 1) * 128, :], yb_, [bY], ())
            yield

        for cx in ctxs:
            load(cx, 0)
        for it in range(NCH + 1):
            gens = []
            if it >= 1:
                gens += [chain(ctxs[0], it - 1), chain(ctxs[1], it - 1)]
            if it < NCH:
                if it + 1 < NCH:
                    for cx in ctxs:
                        load(cx, it + 1)
                for cx in ctxs:
                    for g in range(2):
                        gens.append(local(cx, it, g))
            while gens:
                alive = []
                for gen in gens:
                    try:
                        next(gen)
                        alive.append(gen)
                    except StopIteration:
                        pass
                gens = alive
        S.barrier()

    def pass_P3a(l, xin):
        A.off = 0
        wa = A.alloc(4 * D, BF16).rearrange("p (k n) -> p k n", k=4)
        wb = A.alloc(4 * D, BF16).rearrange("p (k n) -> p k n", k=4)
        wo = A.alloc(8 * D, BF16).rearrange("p (k n) -> p k n", k=8)
        g2 = A.alloc(512, BF16)
        gnw = A.alloc(512, F32)
        gnb = A.alloc(512, F32)
        stg = [A.alloc(D, F32) for _ in range(2)]
        bst = [Buf(), Buf()]
        b_w = Buf()
        for kc in range(4):
            load_weight(wa[:, kc, :], w_a[l, kc * 128:(kc + 1) * 128, :], stg, bst, b_w)
            load_weight(wb[:, kc, :], w_b[l, kc * 128:(kc + 1) * 128, :], stg, bst, b_w)
        for kc in range(8):
            load_weight(wo[:, kc, :], w_out[l, kc * 128:(kc + 1) * 128, :], stg, bst, b_w)
        load_weight(g2, g2_d[l], [s[:, 0:512] for s in stg], bst, b_w)
        dma(gnw, gnwb_d[l][:, 0:512], (), [b_w])
        dma(gnb, gnwb_d[l][:, 512:1024], (), [b_w])

        def a3(n, c, dt):
            return A.alloc(c * n, dt).rearrange("p (c t) -> p c t", c=c)

        q_t = a3(514, 4, BF16)
        bg_t = a3(512, 4, BF16)
        sgd_t = A.alloc(512, BF16)
        sgc_t = a3(512, 8, BF16)
        sgr_t = a3(512, 8, BF16)
        yf = [A.alloc(512, F32) for _ in range(2)]
        ybk = [A.alloc(512, F32) for _ in range(2)]
        v_t = [A.alloc(512, BF16) for _ in range(2)]
        rk_t = [A.alloc(8, F32) for _ in range(2)]
        class TS:
            pass

        tsets = []
        for _i in range(2):
            X = TS()
            X.y32, X.tmp, X.tmp2 = A.alloc(512, F32), A.alloc(512, F32), A.alloc(512, F32)
            X.stat, X.o_bf = A.alloc(64, F32), A.alloc(512, BF16)
            X.b_y32, X.b_tmp, X.b_tmp2, X.b_stat, X.b_o = [Buf() for _ in range(5)]
            tsets.append(X)
        oT = a3(512, 4, BF16)
        cqs = [A.alloc(512, F32) for _ in range(2)]
        b_cqs = [Buf(), Buf()]
        ca = a3(512, 4, BF16)
        mg1 = a3(512, 8, F32)
        x_t = mg1
        merged = a3(512, 8, BF16)
        m32 = a3(512, 8, F32)
        sqm = a3(512, 8, BF16)
        rtmp = A.alloc(512, F32)
        rstd = A.alloc(512, F32)
        x1_t = m32
        (b_q, b_bg, b_sgd, b_sgc, b_sgr, b_x_unused, b_y32, b_tmp, b_tmp2, b_stat, b_o, b_oT, b_cq, b_ca, b_mg1, b_mer,
         b_m32, b_sqm, b_rt, b_rs, b_x1) = [Buf() for _ in range(21)]
        b_x1 = b_m32
        b_tb = [Buf(), Buf()]
        b_x = b_mg1

        def fm(ap):
            return ap.rearrange("(c p) t -> p c t", p=128)

        def h8(ap):
            return ap.rearrange("p (h v) -> p h v", h=8)

        def treduce(out, in_, r, w):
            S.op("dve", lambda e: e.tensor_reduce(out=out, in_=in_, axis=AX.X, op=ALU.add), r, w)

        for ti in range(NT // 512):
            t0 = ti * 512
            dma(q_t, fm(qS)[:, :, t0:t0 + 514], (), [b_q])
            dma(bg_t, fm(bgS)[:, :, t0:t0 + 512], (), [b_bg])
            dma(sgd_t, sgdS[:, t0:t0 + 512], (), [b_sgd])
            dma(sgc_t, fm(sgcS)[:, :, t0:t0 + 512], (), [b_sgc])
            dma(sgr_t, fm(sgrS)[:, :, t0:t0 + 512], (), [b_sgr])
            import os
            P3LVL = int(os.environ.get("P3LVL", "20"))
            if P3LVL < 2:
                continue
            if t0 % SEG == 0:
                halo_fix("dve", q_t[:, :, 0:1], t0 // SEG, [b_q], [b_q])
            if (t0 + 512) % SEG == 0:
                halo_fix("dve", q_t[:, :, 513:514], (t0 + 512) // SEG, [b_q], [b_q])
            for tb in range(4):
                if P3LVL < 3:
                    continue
                pp = tb % 2
                r0 = t0 + tb * 128
                dma(yf[pp], yS[0, r0:r0 + 128, :], (), [b_tb[pp]])
                dma(ybk[pp], yS[1, r0:r0 + 128, :], (), [b_tb[pp]])
                dma(v_t[pp], vS[r0:r0 + 128, :], (), [b_tb[pp]])
                dma(rk_t[pp], rkS[r0:r0 + 128, :], (), [b_tb[pp]])
                bt = b_tb[pp]
                X = tsets[pp]
                y32, tmp, tmp2, stat, o_bf = X.y32, X.tmp, X.tmp2, X.stat, X.o_bf
                b_y32, b_tmp, b_tmp2, b_stat, b_o = X.b_y32, X.b_tmp, X.b_tmp2, X.b_stat, X.b_o

                def st8(i, stat=stat):
                    return stat[:, i * 8:(i + 1) * 8]

                def bc8(i, stat=stat):
                    return stat[:, i * 8:(i + 1) * 8].rearrange("p (h o) -> p h o", o=1).to_broadcast([128, 8, 64])

                tt("pool", y32, yf[pp], ybk[pp], ALU.add, [bt], [b_y32])
                treduce(st8(0), h8(y32), [b_y32], [b_stat])
                tt("pool", tmp, y32, y32, ALU.mult, [b_y32], [b_tmp])
                treduce(st8(1), h8(tmp), [b_tmp], [b_stat])
                ts("dve", st8(0), st8(0), 1.0 / 64, None, ALU.mult, None, [b_stat], [b_stat])
                tt("dve", st8(2), st8(0), st8(0), ALU.mult, [b_stat], [b_stat])
                stt(st8(3), st8(1), 1.0 / 64, st8(2), ALU.mult, ALU.subtract, [b_stat], [b_stat])
                act(st8(4), st8(3), AF.Sqrt, [b_stat, b_c], [b_stat], bias=cf[:, C_GNEPS:C_GNEPS + 1], scale=1.0)
                recip(st8(5), st8(4), [b_stat], [b_stat])
                if P3LVL < 4:
                    continue
                tt("dve", h8(tmp), h8(y32), bc8(0), ALU.subtract, [b_y32, b_stat], [b_tmp])
                tt("dve", h8(tmp2), h8(tmp), bc8(5), ALU.mult, [b_tmp, b_stat], [b_tmp2])
                tt("pool", tmp, tmp2, gnw, ALU.mult, [b_tmp2, b_w], [b_tmp])
                tt("pool", tmp2, tmp, gnb, ALU.add, [b_tmp, b_w], [b_tmp2])
                rkb = rk_t[pp].rearrange("p (h o) -> p h o", o=1).to_broadcast([128, 8, 64])
                tt("pool", h8(tmp), h8(v_t[pp]), rkb, ALU.mult, [bt], [b_tmp])
                tt("dve", y32, tmp2, tmp, ALU.add, [b_tmp2, b_tmp], [b_y32])
                if P3LVL < 5:
                    continue
                ps, pb = nextps()
                mm(ps[:], sgd_t[:, tb * 128:(tb + 1) * 128], g2, True, True, [b_sgd, b_w], [pb])
                tt("dve", o_bf, ps[:], y32, ALU.mult, [pb, b_y32], [b_o])
                ps, pb = nextps()
                psT = ps[:].bitcast(BF16)
                for fc in range(4):
                    transpose(psT[:, fc * 128:(fc + 1) * 128], o_bf[:, fc * 128:(fc + 1) * 128], [b_o], [pb])
                cp("act", oT[:, :, tb * 128:(tb + 1) * 128], psT[:, 0:512].rearrange("p (c t) -> p c t", c=4),
                   [pb], [b_oT])
            if P3LVL < 6:
                continue
            for fc in range(4):
                cq, b_cq = cqs[fc % 2], b_cqs[fc % 2]
                cw = lambda j: vec[:, V_CONVW + j * 4 + fc:V_CONVW + j * 4 + fc + 1]
                ts("dve", cq, q_t[:, fc, 1:513], cw(1), vec[:, V_CONVB + fc:V_CONVB + fc + 1], ALU.mult, ALU.add,
                   [b_q, b_vec], [b_cq])
                stt(cq, q_t[:, fc, 0:512], cw(0), cq, ALU.mult, ALU.add, [b_q, b_vec, b_cq], [b_cq])
                stt(cq, q_t[:, fc, 2:514], cw(2), cq, ALU.mult, ALU.add, [b_q, b_vec, b_cq], [b_cq])
                tt("pool", ca[:, fc, :], cq, bg_t[:, fc, :], ALU.mult, [b_cq, b_bg], [b_ca])
            if P3LVL < 7:
                continue
            for mc in range(8):
                ps, pb = nextps()
                for kc in range(4):
                    mm(ps[:], wa[:, kc, mc * 128:(mc + 1) * 128], ca[:, kc, :], kc == 0, kc == 3, [b_w, b_ca], [pb])
                tt("dve", mg1[:, mc, :], ps[:], sgc_t[:, mc, :], ALU.mult, [pb, b_sgc], [b_mg1])
            if P3LVL < 8:
                continue
            for mc in range(8):
                ps, pb = nextps()
                for kc in range(4):
                    mm(ps[:], wb[:, kc, mc * 128:(mc + 1) * 128], oT[:, kc, :], kc == 0, kc == 3, [b_w, b_oT], [pb])
                X = tsets[mc % 2]
                tt("dve", X.tmp, ps[:], sgr_t[:, mc, :], ALU.mult, [pb, b_sgr], [X.b_tmp])
                tt("pool", merged[:, mc, :], X.tmp, mg1[:, mc, :], ALU.add, [X.b_tmp, b_mg1], [b_mer])
            dma(x_t, fm(xin)[:, :, t0:t0 + 512], (), [b_x])
            if P3LVL < 9:
                continue
            for mc in range(8):
                ps, pb = nextps()
                for kc in range(8):
                    mm(ps[:], wo[:, kc, mc * 128:(mc + 1) * 128], merged[:, kc, :], kc == 0, kc == 7, [b_w, b_mer], [pb])
                cp("dve", m32[:, mc, :], ps[:], [pb], [b_m32])
                act(sqm[:, mc, :], m32[:, mc, :], AF.Square, [b_m32], [b_sqm])
            if P3LVL < 10:
                continue
            rms_rstd(lambda c: sqm[:, c, :], 8, 512, rtmp, rstd, b_sqm, b_rt, b_rs)
            if P3LVL < 11:
                continue
            for mc in range(8):
                X = tsets[mc % 2]
                tt("pool", X.tmp, m32[:, mc, :], rstd, ALU.mult, [b_m32, b_rs], [X.b_tmp])
                stt(x1_t[:, mc, :], X.tmp, vec[:, V_NMPOST + mc:V_NMPOST + mc + 1], x_t[:, mc, :], ALU.mult, ALU.add,
                    [X.b_tmp, b_vec, b_x], [b_x1])
            if P3LVL < 12:
                continue
            dma(fm(x1S)[:, :, 1 + t0:1 + t0 + 512], x1_t, [b_x1], ())
        S.barrier()

    def pass_P3b(l, xout):
        A.off = 0
        wd = A.alloc(22 * D, BF16).rearrange("p (k n) -> p k n", k=22)
        wpg = A.alloc(8 * D, BF16).rearrange("p (k n) -> p k n", k=8)
        wpl = A.alloc(2 * D, BF16).rearrange("p (k n) -> p k n", k=2)
        mark = A.off
        stg = [A.alloc(D, F32) for _ in range(2)]
        bst = [Buf(), Buf()]
        b_w = Buf()
        for kc in range(22):
            load_weight(wd[:, kc, :], w_down[l, kc * 128:(kc + 1) * 128, :], stg, bst, b_w)
        for kc in range(8):
            load_weight(wpg[:, kc, :], w_pg[l, kc * 128:(kc + 1) * 128, :], stg, bst, b_w)
        for kc in range(2):
            load_weight(wpl[:, kc, :], w_ple[l, kc * 128:(kc + 1) * 128, :], stg, bst, b_w)
        S.barrier()
        A.off = mark
        WM = 412

        def a3(n, c, dt):
            return A.alloc(c * n, dt).rearrange("p (c t) -> p c t", c=c)

        x1w = a3(WM, 8, F32)
        sq = a3(WM, 8, BF16)
        u = a3(WM, 8, BF16)
        rtmp = A.alloc(WM, F32)
        rstd = A.alloc(WM, F32)
        p_t = a3(WM, 2, F32)
        p_b = a3(WM, 2, BF16)
        wj = [A.alloc(8 * 2 * 128, BF16).rearrange("p (k g m) -> p k g m", k=8, g=2) for _ in range(3)]
        b_wj = [Buf() for _ in range(3)]
        class TS:
            pass

        jsets = []
        for _i in range(2):
            X = TS()
            for nm in ("cg", "cv", "g1", "g2_", "g3", "gate", "tmp"):
                setattr(X, nm, A.alloc(WM, F32))
                setattr(X, "b_" + nm, Buf())
            jsets.append(X)
        actb = a3(WM, 22, BF16)
        m32 = a3(WM, 8, F32)
        sqm = sq
        x2_t = a3(WM, 8, F32)
        x2b = u
        (b_x1, b_sq, b_u, b_rt, b_rs, b_p, b_pb, b_cg_u, b_cv_u, b_g1_u, b_g2_u, b_g3_u, b_act, b_m32, b_sqm, b_x2, b_x2b,
         b_gate_u, b_tmp_u) = [Buf() for _ in range(19)]
        b_sqm = b_sq
        b_x2b = b_u

        def fm(ap):
            return ap.rearrange("(c p) t -> p c t", p=128)

        tiles = []
        for sgi in range(NSEG):
            a = sgi * SEG
            npc = (SEG + 409) // 410
            base = SEG // npc
            rem = SEG - base * npc
            for i in range(npc):
                n = base + (1 if i < rem else 0)
                tiles.append((a, n))
                a += n
        wctr = 0
        for (a0, n) in tiles:
            W = n + 2
            dma(x1w[:, :, 0:W], fm(x1S)[:, :, a0:a0 + W], (), [b_x1])
            dma(p_t[:, :, 0:n], pT[l].rearrange("(c p) t -> p c t", p=128)[:, :, a0:a0 + n], (), [b_p])
            cp("pool", p_b[:, :, 0:n], p_t[:, :, 0:n], [b_p], [b_pb])
            act(sq[:, :, 0:W], x1w[:, :, 0:W], AF.Square, [b_x1], [b_sq])
            rms_rstd(lambda c: sq[:, c, 0:W], 8, W, rtmp[:, 0:W], rstd[:, 0:W], b_sq, b_rt, b_rs)
            for c in range(8):
                stt(u[:, c, 0:W], x1w[:, c, 0:W], vec[:, V_NFP + c:V_NFP + c + 1], rstd[:, 0:W], ALU.mult, ALU.mult,
                    [b_x1, b_rs, b_vec], [b_u])
            if a0 % SEG == 0:
                halo_fix("dve", u[:, :, 0:1], a0 // SEG, [b_u], [b_u])
            if (a0 + n) % SEG == 0:
                halo_fix("dve", u[:, :, W - 1:W], (a0 + n) // SEG, [b_u], [b_u])
            for j in range(22):
                X = jsets[j % 2]
                cg, cv, g1, g2_, g3 = X.cg, X.cv, X.g1, X.g2_, X.g3
                b_cg, b_cv, b_g1, b_g2, b_g3 = X.b_cg, X.b_cv, X.b_g1, X.b_g2_, X.b_g3
                wi = wctr % 3
                wctr += 1
                dma(wj[wi].rearrange("p k g m -> p (k g m)"), wupS[l, j], (), [b_wj[wi]])
                res = []
                for gv in range(2):
                    ps, pb = nextps()
                    for kc in range(8):
                        mm(ps[:, 0:W], wj[wi][:, kc, gv, :], u[:, kc, 0:W], kc == 0, kc == 7, [b_wj[wi], b_u], [pb])
                    res.append((ps, pb))
                for gv in range(2):
                    ps, pb = res[gv]
                    c_ = gv * 22 + j
                    dst, bd = (cg, b_cg) if gv == 0 else (cv, b_cv)
                    fw = lambda jj: vec[:, V_FCW + jj * 44 + c_:V_FCW + jj * 44 + c_ + 1]
                    act(dst[:, 0:n], ps[:, 1:W - 1], AF.Identity, [pb, b_vec], [bd],
                        bias=vec[:, V_FCB + c_:V_FCB + c_ + 1], scale=fw(1))
                    stt(dst[:, 0:n], ps[:, 0:n], fw(0), dst[:, 0:n], ALU.mult, ALU.add, [pb, b_vec, bd], [bd])
                    stt(dst[:, 0:n], ps[:, 2:W], fw(2), dst[:, 0:n], ALU.mult, ALU.add, [pb, b_vec, bd], [bd])
                tt("pool", g1[:, 0:n], cg[:, 0:n], cg[:, 0:n], ALU.mult, [b_cg], [b_g1])
                ts("pool", g1[:, 0:n], g1[:, 0:n], 0.044715, 1.0, ALU.mult, ALU.add, [b_g1], [b_g1])
                tt("pool", g2_[:, 0:n], g1[:, 0:n], cg[:, 0:n], ALU.mult, [b_g1, b_cg], [b_g2])
                act(g3[:, 0:n], g2_[:, 0:n], AF.Sigmoid, [b_g2], [b_g3], scale=GELU_C)
                tt("pool", g1[:, 0:n], cg[:, 0:n], cv[:, 0:n], ALU.mult, [b_cg, b_cv, b_g2], [b_g1])
                tt("dve", actb[:, j, 0:n], g1[:, 0:n], g3[:, 0:n], ALU.mult, [b_g1, b_g3], [b_act])
            for mc in range(8):
                ps, pb = nextps()
                for kc in range(22):
                    mm(ps[:, 0:n], wd[:, kc, mc * 128:(mc + 1) * 128], actb[:, kc, 0:n], kc == 0, kc == 21,
                       [b_w, b_act], [pb])
                cp("dve", m32[:, mc, 0:n], ps[:, 0:n], [pb], [b_m32])
                act(sqm[:, mc, 0:n], m32[:, mc, 0:n], AF.Square, [b_m32], [b_sqm])
            rms_rstd(lambda c: sqm[:, c, 0:n], 8, n, rtmp[:, 0:n], rstd[:, 0:n], b_sqm, b_rt, b_rs)
            for mc in range(8):
                tmp, b_tmp = jsets[mc % 2].tmp, jsets[mc % 2].b_tmp
                tt("pool", tmp[:, 0:n], m32[:, mc, 0:n], rstd[:, 0:n], ALU.mult, [b_m32, b_rs], [b_tmp])
                stt(x2_t[:, mc, 0:n], tmp[:, 0:n], vec[:, V_NFPOST + mc:V_NFPOST + mc + 1], x1w[:, mc, 1:W - 1],
                    ALU.mult, ALU.add, [b_tmp, b_vec, b_x1], [b_x2])
            cp("pool", x2b[:, :, 0:n], x2_t[:, :, 0:n], [b_x2], [b_x2b])
            for mc in range(8):
                ps, pb = nextps()
                for kc in range(8):
                    mm(ps[:, 0:n], wpg[:, kc, mc * 128:(mc + 1) * 128], x2b[:, kc, 0:n], kc == 0, kc == 7,
                       [b_w, b_x2b], [pb])
                gate, b_gate = jsets[mc % 2].gate, jsets[mc % 2].b_gate
                act(gate[:, 0:n], ps[:, 0:n], AF.Sigmoid, [pb], [b_gate])
                ps2, pb2 = nextps()
                for kc in range(2):
                    mm(ps2[:, 0:n], wpl[:, kc, mc * 128:(mc + 1) * 128], p_b[:, kc, 0:n], kc == 0, kc == 1,
                       [b_w, b_pb], [pb2])
                tt("dve", m32[:, mc, 0:n], ps2[:, 0:n], gate[:, 0:n], ALU.mult, [pb2, b_gate], [b_m32])
                act(sqm[:, mc, 0:n], m32[:, mc, 0:n], AF.Square, [b_m32], [b_sqm])
            rms_rstd(lambda c: sqm[:, c, 0:n], 8, n, rtmp[:, 0:n], rstd[:, 0:n], b_sqm, b_rt, b_rs)
            for mc in range(8):
                tmp, b_tmp = jsets[mc % 2].tmp, jsets[mc % 2].b_tmp
                tt("pool", tmp[:, 0:n], m32[:, mc, 0:n], rstd[:, 0:n], ALU.mult, [b_m32, b_rs], [b_tmp])
                stt(x1w[:, mc, 0:n], tmp[:, 0:n], vec[:, V_NPLE + mc:V_NPLE + mc + 1], x2_t[:, mc, 0:n],
                    ALU.mult, ALU.add, [b_tmp, b_vec, b_x2], [b_x1])
            dma(fm(xout)[:, :, a0:a0 + n], x1w[:, :, 0:n], [b_x1], ())
        S.barrier()

    pass_W0()
    for l in range(L):
        xin = xT if l == 0 else xL
        xout = yT if l == L - 1 else xL
        if upto >= 1:
            pass_P1(l, xin)
        if upto >= 2:
            pass_P2pre(l)
        if upto >= 3:
            pass_P2scan(l)
        if upto >= 4:
            pass_P3a(l, xin)
        if upto >= 5:
            pass_P3b(l, xout)
    S.finish()
    S.emit(nc)
    st.close()
    return nc


def make_consts():
    c = np.zeros((128, NCON), np.float32)
    i = np.arange(128)
    c[:, C_IDENT:C_IDENT + 128] = np.eye(128)
    c[:, C_SU:C_SU + 128] = (i[:, None] < i[None, :])
    c[:, C_SL:C_SL + 128] = (i[:, None] > i[None, :])
    c[:, C_UI:C_UI + 128] = (i[:, None] <= i[None, :])
    c[:, C_LI:C_LI + 128] = (i[:, None] >= i[None, :])
    c[:, C_BLK:C_BLK + 128] = ((i[:, None] // 64) == (i[None, :] // 64))
    c[:, C_ONES:C_ONES + 128] = 1.0
    for fc in range(4):
        for h in range(8):
            c[:, C_HSEL + fc * 8 + h] = (h == 2 * fc + i // 64)
    r = np.ones(512, np.float32)
    r[::128] = 0.0
    c[:, C_RESET:C_RESET + 512] = r[None, :]
    c[:, C_EPS] = NORM_EPS
    c[:, C_GNEPS] = GN_EPS
    return c


def make_vecs(inp, L):
    v = np.zeros((L, 128, NV), np.float32)

    def fmaj(a):
        return np.ascontiguousarray(a.reshape(-1, 128).T)

    for l in range(L):
        v[l, :, V_NMP:V_NMP + 8] = fmaj(inp["norm_mix_pre"][l])
        v[l, :, V_NMPOST:V_NMPOST + 8] = fmaj(inp["norm_mix_post"][l])
        v[l, :, V_NFP:V_NFP + 8] = fmaj(inp["norm_ffn_pre"][l])
        v[l, :, V_NFPOST:V_NFPOST + 8] = fmaj(inp["norm_ffn_post"][l])
        v[l, :, V_NPLE:V_NPLE + 8] = fmaj(inp["norm_ple_post"][l])
        for j in range(3):
            v[l, :, V_CONVW + j * 4:V_CONVW + j * 4 + 4] = fmaj(inp["conv_w"][l, j])
            v[l, :, V_FCW + j * 44:V_FCW + j * 44 + 44] = fmaj(inp["ffn_conv_w"][l, j])
        v[l, :, V_CONVB:V_CONVB + 4] = fmaj(inp["conv_b"][l])
        v[l, :, V_FCB:V_FCB + 44] = fmaj(inp["ffn_conv_b"][l])
        v[l, :, V_KK:V_KK + 4] = fmaj(inp["k_k"][l])
        v[l, :, V_KA:V_KA + 4] = fmaj(inp["k_a"][l])
        v[l, :, V_RK:V_RK + 4] = fmaj(inp["r_k"][l].reshape(-1))
        for d in range(2):
            v[l, :, V_W0 + d * 4:V_W0 + d * 4 + 4] = fmaj(inp["decay_w0"][l, d])
            v[l, :, V_A0 + d * 4:V_A0 + d * 4 + 4] = fmaj(inp["iclr_a0"][l, d])
            v[l, 0:64, V_MU + d * 2] = inp["shift_mu"][l, d, 0:64]
            v[l, 0:64, V_MU + d * 2 + 1] = inp["shift_mu"][l, d, 64:128]
    return v


_PROG_CACHE = {}


def run_cores(seqs_per_core, carry_per_core, inp, NSEG, SEG, L, debug=False, upto=99):
    key = (NSEG, SEG, L, debug, upto)
    if key not in _PROG_CACHE:
        _PROG_CACHE[key] = build_program(NSEG, SEG, L, debug, upto)
    nc = _PROG_CACHE[key]
    consts = make_consts()
    vecs = make_vecs(inp, L)
    gnwb = np.zeros((L, 128, 1024), np.float32)
    for l in range(L):
        gnwb[l, :, 0:512] = inp["gn_w"][l][None, :]
        gnwb[l, :, 512:1024] = inp["gn_b"][l][None, :]
    shared = {
        "consts": consts, "vecs": vecs, "gnwb": gnwb,
        "w_in": inp["w_in"], "w_branch_a": inp["w_branch_a"], "w_branch_b": inp["w_branch_b"],
        "w_out": inp["w_out"], "w_up": inp["w_up"], "w_down": inp["w_down"], "w_ple": inp["w_ple"],
        "w_ple_gate": inp["w_ple_gate"], "decay_w2": inp["decay_w2"], "iclr_a2": inp["iclr_a2"],
        "gate_g2": inp["gate_g2"],
    }
    shared = {k: np.ascontiguousarray(np.asarray(v, np.float32)) for k, v in shared.items()}
    in_maps = []
    for (x, p), carry in zip(seqs_per_core, carry_per_core):
        m = dict(shared)
        m["xT"] = np.ascontiguousarray(x.T)
        m["pT"] = np.ascontiguousarray(np.transpose(p, (0, 2, 1)))
        mk = np.zeros((128, NSEG + 1), np.float32)
        mk[:, :] = np.asarray(carry, np.float32)[None, :]
        m["masks"] = mk
        in_maps.append(m)
    res = run_bass_kernel_spmd(nc, in_maps, core_ids=list(range(len(in_maps))))
    return res.results


def kernel(**inp):
    inp = {k: np.asarray(v) for k, v in inp.items()}
    xp, xs = inp["x_prompt"], inp["x_sample"]
    pp, psm = inp["p_prompt"], inp["p_sample"]
    L = pp.shape[0]
    SEG, NSEG = 2048, 6
    per_core = []
    carries = []
    plan = []
    for c in range(8):
        if c < 4:
            segs = [("p", c), ("s", 2 * c), ("s", 2 * c + 1)]
            carry = [0, 1, 1, 1, 0, 0, 0]
        else:
            segs = [("s", 8 + 6 * (c - 4) + i) for i in range(6)]
            carry = [0] * 7
        xs_l, ps_l = [], []
        for kind, i in segs:
            if kind == "p":
                xs_l.append(xp[i])
                ps_l.append(pp[:, i])
            else:
                xs_l.append(xs[i])
                ps_l.append(psm[:, i])
        per_core.append((np.concatenate(xs_l, axis=0), np.concatenate(ps_l, axis=1)))
        carries.append(carry)
        plan.append(segs)
    results = run_cores(per_core, carries, inp, NSEG, SEG, L)
    y_p = np.empty(xp.shape, np.float32)
    y_s = np.empty(xs.shape, np.float32)
    for c in range(8):
        y = np.ascontiguousarray(results[c]["yT"].T)
        off = 0
        for kind, i in plan[c]:
            if kind == "p":
                y_p[i] = y[off:off + 8192]
                off += 8192
            else:
                y_s[i] = y[off:off + 2048]
                off += 2048
    return (y_p, y_s)
```

```python
import numpy as np
from contextlib import ExitStack
import concourse.bass as bass
import concourse.mybir as mybir
from concourse.bass_utils import run_bass_kernel_spmd

F32, BF16 = mybir.dt.float32, mybir.dt.bfloat16
AF = mybir.ActivationFunctionType
ALU = mybir.AluOpType
AX = mybir.AxisListType

D = 1024
INC = 5504
DFF = 2816
CDEC = -0.6065306597126334
NORM_EPS = 1e-6
GN_EPS = 64 * 1e-5
GELU_C = 1.5957691216057308

C_IDENT, C_SU, C_SL, C_UI, C_LI, C_BLK, C_ONES, C_HSEL, C_RESET, C_EPS, C_GNEPS = (
    0, 128, 256, 384, 512, 640, 768, 896, 928, 1440, 1441)
NCON = 1442
V_NMP, V_NMPOST, V_NFP, V_NFPOST, V_NPLE = 0, 8, 16, 24, 32
V_CONVW, V_CONVB, V_KK, V_KA, V_RK, V_W0, V_A0 = 40, 52, 56, 60, 64, 68, 76
V_FCW, V_FCB, V_MU, V_1MKA = 84, 216, 260, 264
NV = 268


class Buf:
    __slots__ = ("lw", "rd")

    def __init__(self):
        self.lw = None
        self.rd = []


ENGS = ("pe", "act", "dve", "pool", "sp")


class Sched:
    def __init__(self, ring=8):
        self.prog = {e: [] for e in ENGS}
        self.cnt = {e: 0 for e in ENGS}
        self.known = {e: {} for e in ENGS}
        self.ring = ring
        self.dma_next = {e: 0 for e in ENGS}
        self.dma_cnt = {e: [0] * ring for e in ENGS}

    def _deps(self, eng, reads, writes):
        deps = {}

        def add(p):
            k, v = p
            if deps.get(k, 0) < v:
                deps[k] = v

        for r in reads:
            if r.lw is not None and not (r.lw[0] == eng and eng == "pe"):
                add(r.lw)
        for w in writes:
            if w.lw is not None and w.lw[0] != eng:
                add(w.lw)
            for p in w.rd:
                if p[0] != eng:
                    add(p)
        return deps

    def _filter(self, eng, deps):
        kn = self.known[eng]
        out = []
        for k, v in deps.items():
            if kn.get(k, 0) >= v:
                continue
            kn[k] = v
            out.append((k, v))
        return out

    def _commit(self, me, reads, writes):
        for r in reads:
            r.rd.append(me)
        for w in writes:
            w.lw = me
            w.rd = []

    def op(self, eng, fn, reads=(), writes=()):
        waits = self._filter(eng, self._deps(eng, reads, writes))
        self.cnt[eng] += 1
        self.prog[eng].append((waits, fn, (eng, 1)))
        self._commit((eng, self.cnt[eng]), reads, writes)

    def dma(self, eng, fn, reads=(), writes=()):
        deps = self._deps(eng, reads, writes)
        slot = self.dma_next[eng] % self.ring
        self.dma_next[eng] += 1
        key = ("dma", eng, slot)
        c = self.dma_cnt[eng][slot]
        if c > 0 and deps.get(key, 0) < 16 * c:
            deps[key] = 16 * c
        waits = self._filter(eng, deps)
        self.dma_cnt[eng][slot] = c + 1
        self.prog[eng].append((waits, fn, (key, 16)))
        self._commit((key, 16 * (c + 1)), reads, writes)

    def _all(self):
        deps = {}
        for e in ENGS:
            if e != "sp" and self.cnt[e] > 0:
                deps[e] = self.cnt[e]
            for slot in range(self.ring):
                c = self.dma_cnt[e][slot]
                if c > 0:
                    deps[("dma", e, slot)] = 16 * c
        return deps

    def barrier(self):
        deps = self._all()
        for e in ENGS:
            d = {k: v for k, v in deps.items() if k != e}
            waits = self._filter(e, d)
            if waits:
                self.prog[e].append((waits, None, None))

    def finish(self):
        self.barrier()

    def emit(self, nc):
        keys = [e for e in ENGS if e != "sp" and self.cnt[e] > 0]
        for e in ENGS:
            for slot in range(self.ring):
                if self.dma_cnt[e][slot] > 0:
                    keys.append(("dma", e, slot))
        with ExitStack() as st:
            sems = {}
            for i, k in enumerate(keys):
                sems[k] = st.enter_context(nc.semaphore("s%d" % i))
            block = st.enter_context(nc.Block())

            def run(engname):
                def body(e):
                    for waits, fn, inc in self.prog[engname]:
                        for k, v in waits:
                            e.wait_ge(sems[k], v)
                        if fn is not None:
                            fn(e).then_inc(sems[inc[0]], inc[1])
                return body

            block.sync(run("sp"))
            block.tensor(run("pe"))
            block.scalar(run("act"))
            block.vector(run("dve"))
            block.gpsimd(run("pool"))


class Arena:
    def __init__(self, ap, size):
        self.ap, self.size, self.off = ap, size, 0

    def alloc(self, n, dtype):
        n16 = n * (2 if dtype == F32 else 1)
        start = (self.off + 15) // 16 * 16
        assert start + n16 <= self.size, ("arena overflow", start + n16, self.size)
        v = self.ap[:, start:start + n16]
        if dtype == F32:
            v = v.bitcast(F32)
        self.off = start + n16
        return v


def build_program(NSEG, SEG, DEPTH, debug=False, upto=99):
    NT = NSEG * SEG
    NCH = NT // 128
    CPS = SEG // 128
    assert NT % 512 == 0 and SEG % 128 == 0
    nc = bass.Bass("TRN2", target_bir_lowering=False)
    L = DEPTH

    def din(name, shape, dt=F32):
        return nc.dram_tensor(name, list(shape), dt, kind="ExternalInput").ap()

    def dscr(name, shape, dt):
        kind = "ExternalOutput" if debug else "Internal"
        return nc.dram_tensor(name, list(shape), dt, kind=kind).ap()

    xT = din("xT", [D, NT])
    pT = din("pT", [L, 256, NT])
    masks_d = din("masks", [128, NSEG + 1])
    consts_d = din("consts", [128, NCON])
    vecs_d = din("vecs", [L, 128, NV])
    gnwb_d = din("gnwb", [L, 128, 1024])
    w_in = din("w_in", [L, D, INC])
    w_a = din("w_branch_a", [L, 512, D])
    w_b = din("w_branch_b", [L, 512, D])
    w_out = din("w_out", [L, D, D])
    w_up = din("w_up", [L, D, 2 * DFF])
    w_down = din("w_down", [L, DFF, D])
    w_ple = din("w_ple", [L, 256, D])
    w_pg = din("w_ple_gate", [L, D, D])
    dw2_d = din("decay_w2", [L, 2, 64, 512])
    a2_d = din("iclr_a2", [L, 2, 64, 512])
    g2_d = din("gate_g2", [L, 128, 512])
    yT = nc.dram_tensor("yT", [D, NT], F32, kind="ExternalOutput").ap()

    qS = dscr("qS", [512, NT + 2], BF16)
    bgS = dscr("bgS", [512, NT], BF16)
    rS = dscr("rS", [512, NT], BF16)
    kS = dscr("kS", [512, NT], BF16)
    vS = dscr("vS", [NT, 512], BF16)
    zS = dscr("zS", [256, NT + 2], F32)
    sgdS = dscr("sgdS", [128, NT], BF16)
    sgcS = dscr("sgcS", [D, NT], BF16)
    sgrS = dscr("sgrS", [D, NT], BF16)
    rkS = dscr("rkS", [NT, 8], F32)
    fmS = dscr("fmS", [2, NCH, 64, 8 * 4 * 128], BF16)
    tmS = dscr("tmS", [2, NT, 1024], BF16)
    gcS = dscr("gcS", [2, NCH, 128, 4], F32)
    yS = dscr("yS", [2, NT, 512], F32)
    x1S = dscr("x1S", [D, NT + 2], F32)
    xL = dscr("xL", [D, NT], F32)
    wupS = dscr("wupS", [L, 22, 128, 8 * 2 * 128], BF16)

    S = Sched()
    st = ExitStack()
    ARENA_N = 90 * 1024
    arena_t = st.enter_context(nc.sbuf_tensor("arena", [128, ARENA_N], BF16))
    cf = st.enter_context(nc.sbuf_tensor("cf", [128, NCON], F32))
    cb = st.enter_context(nc.sbuf_tensor("cb", [128, 256], BF16))
    mk = st.enter_context(nc.sbuf_tensor("mk", [128, NSEG + 1], F32))
    vec = st.enter_context(nc.sbuf_tensor("vec", [128, NV], F32))
    psum = [st.enter_context(nc.psum_tensor("ps%d" % i, [128, 512], F32)) for i in range(8)]
    psb = [Buf() for _ in range(8)]
    A = Arena(arena_t, ARENA_N)
    b_c = Buf()
    b_vec = Buf()
    ident_bf = cb[:, 0:128]
    ones_bf = cb[:, 128:256]
    pctr = [0]

    def nextps():
        i = pctr[0] % 8
        pctr[0] += 1
        return psum[i], psb[i]

    def dma(out, in_, r=(), w=()):
        S.dma("sp", lambda e: e.dma_start(out=out, in_=in_), r, w)

    def mm(out, lhsT, rhs, start, stop, r, w):
        S.op("pe", lambda e: e.matmul(out, lhsT=lhsT, rhs=rhs, start=start, stop=stop), r, w)

    def transpose(out, in_, r, w):
        S.op("pe", lambda e: e.transpose(out, in_, ident_bf), list(r) + [b_c], w)

    def act(out, in_, func, r, w, bias=None, scale=None):
        kw = {}
        if bias is not None:
            kw["bias"] = bias
        if scale is not None:
            kw["scale"] = scale
        S.op("act", lambda e: e.activation(out=out, in_=in_, func=func, **kw), r, w)

    def tt(eng, out, in0, in1, op, r, w):
        S.op(eng, lambda e: e.tensor_tensor(out=out, in0=in0, in1=in1, op=op), r, w)

    def ts(eng, out, in0, s1, s2, op0, op1, r, w):
        if op1 is None:
            S.op(eng, lambda e: e.tensor_scalar(out=out, in0=in0, scalar1=s1, scalar2=None, op0=op0), r, w)
        else:
            S.op(eng, lambda e: e.tensor_scalar(out=out, in0=in0, scalar1=s1, scalar2=s2, op0=op0, op1=op1), r, w)

    def stt(out, in0, scalar, in1, op0, op1, r, w):
        S.op("dve", lambda e: e.scalar_tensor_tensor(out=out, in0=in0, scalar=scalar, in1=in1, op0=op0, op1=op1), r, w)

    def cp(eng, out, in_, r, w):
        if eng == "act":
            S.op("act", lambda e: e.copy(out=out, in_=in_), r, w)
        else:
            S.op(eng, lambda e: e.tensor_copy(out=out, in_=in_), r, w)

    def amul(out, in_, m, r, w):
        S.op("act", lambda e: e.mul(out=out, in_=in_, mul=m), r, w)

    def memset(eng, ap, val, w):
        S.op(eng, lambda e: e.memset(ap, val), (), w)

    def recip(out, in_, r, w):
        S.op("dve", lambda e: e.reciprocal(out=out, in_=in_), r, w)

    cast_rr = [0]

    def load_weight(dst, src, stages, bst, bdst):
        n = dst.shape[-1]
        i = cast_rr[0]
        cast_rr[0] += 1
        sg = stages[i % len(stages)][:, 0:n]
        bs = bst[i % len(stages)]
        dma(sg, src, (), [bs])
        cp(("dve", "act", "pool")[i % 3], dst, sg, [bs], [bdst])

    def halo_fix(eng, ap, b, rbufs, wbufs):
        if b == 0 or b == NSEG:
            memset(eng, ap, 0.0, wbufs)
        else:
            np_ = ap.shape[0]
            ts(eng, ap, ap, mk[0:np_, b:b + 1], None, ALU.mult, None, list(rbufs) + [b_c], wbufs)

    def rms_rstd(sq_chunks, nchunk, W, rstd_tmp, rstd, b_sq, b_tmp, b_rstd):
        ps, pb = nextps()
        for c in range(nchunk):
            mm(ps[:, 0:W], ones_bf, sq_chunks(c), c == 0, c == nchunk - 1, [b_sq, b_c], [pb])
        act(rstd_tmp, ps[:, 0:W], AF.Ln, [pb, b_c], [b_tmp], bias=cf[:, C_EPS:C_EPS + 1], scale=1.0 / D)
        act(rstd, rstd_tmp, AF.Exp, [b_tmp], [b_rstd], scale=-0.5)

    dma(cf[:], consts_d[:, :], (), [b_c])
    dma(mk[:], masks_d[:, :], (), [b_c])
    cp("dve", ident_bf, cf[:, C_IDENT:C_IDENT + 128], [b_c], [b_c])
    cp("dve", ones_bf, cf[:, C_ONES:C_ONES + 128], [b_c], [b_c])
    SU = cf[:, C_SU:C_SU + 128]
    SL = cf[:, C_SL:C_SL + 128]
    UI = cf[:, C_UI:C_UI + 128]
    LI = cf[:, C_LI:C_LI + 128]
    BLK = cf[:, C_BLK:C_BLK + 128]
    RESET = cf[:, C_RESET:C_RESET + 512]

    def pass_W0():
        A.off = 0
        stg = [A.alloc(2 * DFF, F32) for _ in range(2)]
        s16 = [A.alloc(2 * DFF, BF16) for _ in range(2)]
        bs = [Buf(), Buf()]
        b16 = [Buf(), Buf()]
        i = 0
        for l in range(L):
            for kc in range(8):
                s = i % 2
                dma(stg[s], w_up[l, kc * 128:(kc + 1) * 128, :], (), [bs[s]])
                cp(("dve", "act", "pool")[i % 3], s16[s], stg[s], [bs[s]], [b16[s]])
                for gv in range(2):
                    dst = wupS[l].rearrange("j p (k g m) -> p j k g m", k=8, g=2)[:, :, kc, gv, :]
                    src = s16[s][:, gv * DFF:(gv + 1) * DFF].rearrange("p (j m) -> p j m", m=128)
                    dma(dst, src, [b16[s]], ())
                i += 1
        S.barrier()

    def pass_P1(l, xin):
        A.off = 0
        W1 = A.alloc(8 * INC, BF16).rearrange("p (k n) -> p k n", k=8)
        bW1 = Buf()
        mark = A.off
        stg = [A.alloc(INC, F32) for _ in range(2)]
        bst = [Buf(), Buf()]
        dma(vec[:], vecs_d[l], (), [b_vec])
        for kc in range(8):
            load_weight(W1[:, kc, :], w_in[l, kc * 128:(kc + 1) * 128, :], stg, bst, bW1)
        S.barrier()
        A.off = mark
        xt = A.alloc(8 * 512, F32).rearrange("p (c t) -> p c t", c=8)
        sq = A.alloc(8 * 512, BF16).rearrange("p (c t) -> p c t", c=8)
        ub = A.alloc(8 * 512, BF16).rearrange("p (c t) -> p c t", c=8)
        rtmp = A.alloc(512, F32)
        rstd = A.alloc(512, F32)
        hc_s = A.alloc(4 * 512, BF16).rearrange("p (c t) -> p c t", c=4)
        bg_s = A.alloc(4 * 512, BF16).rearrange("p (c t) -> p c t", c=4)
        q_s = A.alloc(4 * 512, BF16).rearrange("p (c t) -> p c t", c=4)
        r_s = A.alloc(4 * 512, BF16).rearrange("p (c t) -> p c t", c=4)
        k_s = A.alloc(4 * 512, BF16).rearrange("p (c t) -> p c t", c=4)
        v_s = A.alloc(4 * 512, BF16).rearrange("p (c t) -> p c t", c=4)
        z_s = A.alloc(2 * 512, F32).rearrange("p (c t) -> p c t", c=2)
        sgd_s = A.alloc(512, BF16)
        sgc_s = A.alloc(8 * 512, BF16).rearrange("p (c t) -> p c t", c=8)
        sgr_s = A.alloc(8 * 512, BF16).rearrange("p (c t) -> p c t", c=8)
        b_xt, b_sq, b_ub, b_rt, b_rs = Buf(), Buf(), Buf(), Buf(), Buf()
        b_hc, b_bg, b_q, b_r, b_k, b_v, b_z, b_sgd, b_sgc, b_sgr = [Buf() for _ in range(10)]

        def fm(ap):
            return ap.rearrange("(c p) t -> p c t", p=128)

        import os
        P1LVL = int(os.environ.get("P1LVL", "20"))
        for ti in range(NT // 512):
            if P1LVL < 1:
                break
            t0 = ti * 512
            dma(xt, fm(xin)[:, :, t0:t0 + 512], (), [b_xt])
            act(sq, xt, AF.Square, [b_xt], [b_sq])
            if P1LVL < 2:
                continue
            rms_rstd(lambda c: sq[:, c, :], 8, 512, rtmp, rstd, b_sq, b_rt, b_rs)
            if P1LVL < 3:
                continue
            for c in range(8):
                stt(ub[:, c, :], xt[:, c, :], vec[:, V_NMP + c:V_NMP + c + 1], rstd, ALU.mult, ALU.mult,
                    [b_xt, b_rs, b_vec], [b_ub])

            def proj(f):
                ps, pb = nextps()
                for kc in range(8):
                    mm(ps[:], W1[:, kc, f * 128:(f + 1) * 128], ub[:, kc, :], kc == 0, kc == 7, [bW1, b_ub], [pb])
                return ps, pb

            if P1LVL < 4:
                continue
            for f in range(43):
                if 20 <= f < 24:
                    continue
                if P1LVL < 20 and f >= {4: 4, 5: 8, 6: 12, 7: 20, 8: 27, 9: 28, 10: 34, 11: 35, 12: 42, 13: 43}[P1LVL]:
                    continue
                ps, pb = proj(f)
                if f < 4:
                    cp("act", hc_s[:, f, :], ps[:], [pb], [b_hc])
                elif f < 8:
                    cp("act", bg_s[:, f - 4, :], ps[:], [pb], [b_bg])
                    if f == 7:
                        dma(fm(bgS)[:, :, t0:t0 + 512], bg_s, [b_bg], ())
                elif f < 12:
                    tt("dve", q_s[:, f - 8, :], ps[:], hc_s[:, f - 8, :], ALU.mult, [pb, b_hc], [b_q])
                    if f == 11:
                        dma(fm(qS)[:, :, 1 + t0:1 + t0 + 512], q_s, [b_q], ())
                elif f < 16:
                    cp("act", r_s[:, f - 12, :], ps[:], [pb], [b_r])
                    if f == 15:
                        dma(fm(rS)[:, :, t0:t0 + 512], r_s, [b_r], ())
                elif f < 20:
                    cp("dve", k_s[:, f - 16, :], ps[:], [pb], [b_k])
                    if f == 19:
                        dma(fm(kS)[:, :, t0:t0 + 512], k_s, [b_k], ())
                elif f < 26:
                    cp("act", z_s[:, f - 24, :], ps[:], [pb], [b_z])
                    if f == 25:
                        dma(fm(zS)[:, :, 1 + t0:1 + t0 + 512], z_s, [b_z], ())
                elif f == 26:
                    act(sgd_s, ps[:], AF.Sigmoid, [pb], [b_sgd])
                    dma(sgdS[:, t0:t0 + 512], sgd_s, [b_sgd], ())
                elif f < 35:
                    act(sgc_s[:, f - 27, :], ps[:], AF.Sigmoid, [pb], [b_sgc])
                    if f == 34:
                        dma(fm(sgcS)[:, :, t0:t0 + 512], sgc_s, [b_sgc], ())
                else:
                    act(sgr_s[:, f - 35, :], ps[:], AF.Sigmoid, [pb], [b_sgr])
                    if f == 42:
                        dma(fm(sgrS)[:, :, t0:t0 + 512], sgr_s, [b_sgr], ())
            for tb in range(4):
                ps, pb = nextps()
                for kc in range(8):
                    mm(ps[:], ub[:, kc, tb * 128:(tb + 1) * 128], W1[:, kc, 2560:3072], kc == 0, kc == 7,
                       [bW1, b_ub], [pb])
                cp(("dve", "act")[tb % 2], v_s[:, tb, :], ps[:], [pb], [b_v])
            dma(vS[t0:t0 + 512, :].rearrange("(b p) f -> p b f", p=128), v_s, [b_v], ())
        S.barrier()

    def pass_P2pre(l):
        A.off = 0
        dw2 = A.alloc(2 * 512, BF16).rearrange("p (d n) -> p d n", d=2)
        a2 = A.alloc(2 * 512, BF16).rearrange("p (d n) -> p d n", d=2)
        stg = [A.alloc(512, F32) for _ in range(2)]
        bst = [Buf(), Buf()]
        b_w = Buf()
        for d in range(2):
            load_weight(dw2[0:64, d, :], dw2_d[l, d], [s[0:64] for s in stg], bst, b_w)
            load_weight(a2[0:64, d, :], a2_d[l, d], [s[0:64] for s in stg], bst, b_w)
        ts("dve", vec[:, V_1MKA:V_1MKA + 4], vec[:, V_KA:V_KA + 4], -1.0, 1.0, ALU.mult, ALU.add, [b_vec], [b_vec])
        r_t = A.alloc(4 * 512, BF16).rearrange("p (c t) -> p c t", c=4)
        k_t = A.alloc(4 * 512, BF16).rearrange("p (c t) -> p c t", c=4)
        zt = [A.alloc(514, F32) for _ in range(4)]
        kk = A.alloc(4 * 512, F32).rearrange("p (c t) -> p c t", c=4)
        prod_raw = A.alloc(4 * 512 * 2, BF16)
        prod = prod_raw.bitcast(F32).rearrange("p (c t) -> p c t", c=4)
        rk_s = A.alloc(32, F32)
        zs = [A.alloc(512, F32) for _ in range(2)]
        tz = A.alloc(512, BF16)
        zab = A.alloc(512, BF16)

        class TS:
            pass

        sets = []
        for _i in range(4):
            X = TS()
            for nm in ("t2", "sgm", "av", "cs", "ex", "rs_", "ri", "kd", "bb"):
                setattr(X, nm, A.alloc(512, F32))
                setattr(X, "b_" + nm, Buf())
            X.E = [A.alloc(512, F32) for _ in range(4)]
            X.b_E = [Buf() for _ in range(4)]
            X.t1, X.b_t1 = X.E[0], X.b_E[0]
            X.t3, X.b_t3 = X.E[1], X.b_E[1]
            sets.append(X)
        fmst = [A.alloc(4 * 4 * 128, BF16).rearrange("p (c a t) -> p c a t", c=4, a=4) for _ in range(4)]
        sc_s = prod_raw.rearrange("p (a c t) -> p a c t", a=2, c=4)
        tm_s = A.alloc(4 * 2 * 512, BF16).rearrange("p (b a f) -> p b a f", b=4, a=2)
        gc_s = A.alloc(16, F32).rearrange("p (c f) -> p c f", c=4)
        b_r, b_k, b_kk, b_prod, b_rk = [Buf() for _ in range(5)]
        b_z = [Buf() for _ in range(4)]
        b_zs = [Buf(), Buf()]
        b_tz, b_zab = Buf(), Buf()
        b_fm = [Buf() for _ in range(4)]
        b_sc, b_tm, b_gc = b_prod, Buf(), Buf()

        def fm(ap):
            return ap.rearrange("(c p) t -> p c t", p=128)

        for ti in range(NT // 512):
            t0 = ti * 512
            dma(r_t, fm(rS)[:, :, t0:t0 + 512], (), [b_r])
            dma(k_t, fm(kS)[:, :, t0:t0 + 512], (), [b_k])
            for i in range(4):
                dma(zt[i][0:64, :], zS[i * 64:(i + 1) * 64, t0:t0 + 514], (), [b_z[i]])
            if t0 % SEG == 0:
                for i in (0, 1):
                    halo_fix("pool", zt[i][0:64, 0:1], t0 // SEG, [b_z[i]], [b_z[i]])
            if (t0 + 512) % SEG == 0:
                for i in (2, 3):
                    halo_fix("pool", zt[i][0:64, 513:514], (t0 + 512) // SEG, [b_z[i]], [b_z[i]])
            for fc in range(4):
                X = sets[fc]
                ts("dve", X.t1, k_t[:, fc, :], vec[:, V_KK + fc:V_KK + fc + 1], None, ALU.mult, None, [b_k, b_vec], [X.b_t1])
                tt("dve", X.t2, X.t1, X.t1, ALU.mult, [X.b_t1], [X.b_t2])
            for fc in range(4):
                X = sets[fc]
                ps, pb = nextps()
                mm(ps[:], BLK, X.t2, True, True, [X.b_t2, b_c], [pb])
                ts("dve", X.t3, ps[:], 1e-24, None, ALU.max, None, [pb], [X.b_t3])
                stt(prod[:, fc, :], r_t[:, fc, :], vec[:, V_RK + fc:V_RK + fc + 1], k_t[:, fc, :], ALU.mult, ALU.mult,
                    [b_r, b_k, b_vec], [b_prod])
            for fc in range(4):
                X = sets[fc]
                act(X.t3, X.t3, AF.Ln, [X.b_t3], [X.b_t3])
            for fc in range(4):
                X = sets[fc]
                act(X.t3, X.t3, AF.Exp, [X.b_t3], [X.b_t3], scale=-0.5)
            for fc in range(4):
                X = sets[fc]
                tt("dve", kk[:, fc, :], X.t1, X.t3, ALU.mult, [X.b_t1, X.b_t3], [b_kk])
            ps, pb = nextps()
            for tb in range(4):
                for fc in range(4):
                    mm(ps[:, tb * 8:(tb + 1) * 8], prod[:, fc, tb * 128:(tb + 1) * 128],
                       cf[:, C_HSEL + fc * 8:C_HSEL + (fc + 1) * 8], fc == 0, fc == 3, [b_prod, b_c], [pb])
            cp("act", rk_s, ps[:, 0:32], [pb], [b_rk])
            dma(rkS[t0:t0 + 512, :].rearrange("(b p) h -> p b h", p=128), rk_s.rearrange("p (b h) -> p b h", b=4),
                [b_rk], ())
            for d in range(2):
                for part in range(2):
                    t1, b_t1 = sets[part].t1, sets[part].b_t1
                    Z = zt[d * 2 + part]
                    cur = Z[0:64, 1:513]
                    sh = Z[0:64, 0:512] if d == 0 else Z[0:64, 2:514]
                    bz = b_z[d * 2 + part]
                    tt("dve", t1[0:64, :], sh, cur, ALU.subtract, [bz], [b_t1])
                    stt(zs[part][0:64, :], t1[0:64, :], vec[0:64, V_MU + d * 2 + part:V_MU + d * 2 + part + 1], cur,
                        ALU.mult, ALU.add, [b_t1, bz, b_vec], [b_zs[part]])
                act(tz[0:64, :], zs[0][0:64, :], AF.Tanh, [b_zs[0]], [b_tz])
                cp("dve", zab[0:64, :], zs[1][0:64, :], [b_zs[1]], [b_zab])
                def fc_gen(d, fc):
                        X = sets[fc]
                        t2, sgm, av, cs, ex, rs_, ri, kd, bb, E = X.t2, X.sgm, X.av, X.cs, X.ex, X.rs_, X.ri, X.kd, X.bb, X.E
                        b_t2, b_sgm, b_av, b_cs, b_ex, b_rs, b_ri, b_kd, b_bb, b_E = (
                            X.b_t2, X.b_sgm, X.b_av, X.b_cs, X.b_ex, X.b_rs_, X.b_ri, X.b_kd, X.b_bb, X.b_E)
                        ps, pb = nextps()
                        mm(ps[:], dw2[0:64, d, fc * 128:(fc + 1) * 128], tz[0:64, :], True, True, [b_w, b_tz], [pb])
                        act(sgm, ps[:], AF.Sigmoid, [pb, b_vec], [b_sgm],
                            bias=vec[:, V_W0 + d * 4 + fc:V_W0 + d * 4 + fc + 1], scale=1.0)
                        ps2, pb2 = nextps()
                        mm(ps2[:], a2[0:64, d, fc * 128:(fc + 1) * 128], zab[0:64, :], True, True, [b_w, b_zab], [pb2])
                        act(av, ps2[:], AF.Sigmoid, [pb2, b_vec], [b_av],
                            bias=vec[:, V_A0 + d * 4 + fc:V_A0 + d * 4 + fc + 1], scale=1.0)
                        yield
                        S.op("dve", lambda e, cs=cs, sgm=sgm: e.tensor_tensor_scan(out=cs, data0=RESET, data1=sgm, initial=0.0,
                                                                                   op0=ALU.mult, op1=ALU.add),
                             [b_sgm, b_c], [b_cs])
                        cs3 = cs.rearrange("p (c t) -> p c t", c=4)
                        tot_bc = cs3[:, :, 127:128].to_broadcast([128, 4, 128])
                        tt("pool", ex, cs, sgm, ALU.subtract, [b_cs, b_sgm], [b_ex])
                        tt("dve", rs_.rearrange("p (c t) -> p c t", c=4), tot_bc, cs3, ALU.subtract, [b_cs], [b_rs])
                        if d == 0:
                            e1s, e2s, e4s = cs, ex, rs_
                            br1, br2, br4 = b_cs, b_ex, b_rs
                        else:
                            tt("dve", ri, rs_, sgm, ALU.add, [b_rs, b_sgm], [b_ri])
                            e1s, e2s, e4s = ri, rs_, ex
                            br1, br2, br4 = b_ri, b_rs, b_ex
                        ts("dve", t2, av, vec[:, V_KA + fc:V_KA + fc + 1], vec[:, V_1MKA + fc:V_1MKA + fc + 1],
                           ALU.mult, ALU.add, [b_av, b_vec], [b_t2])
                        tt("dve", kd, t2, k_t[:, fc, :], ALU.mult, [b_t2, b_k], [b_kd])
                        tt("pool", bb, kk[:, fc, :], av, ALU.mult, [b_kk, b_av], [b_bb])
                        yield
                        act(E[0], e1s, AF.Exp, [br1], [b_E[0]], scale=CDEC)
                        act(E[1], e2s, AF.Exp, [br2], [b_E[1]], scale=CDEC)
                        act(E[2], e1s, AF.Exp, [br1], [b_E[2]], scale=-CDEC)
                        act(E[3], e4s, AF.Exp, [br4], [b_E[3]], scale=CDEC)
                        act(gc_s[:, :, fc], cs3[:, :, 127], AF.Exp, [b_cs], [b_gc], scale=CDEC)
                        yield
                        F_ = fmst[fc]
                        bF = b_fm[fc]

                        def v4(ap):
                            return ap.rearrange("p (c t) -> p c t", c=4)

                        tt("dve", F_[:, :, 0, :], v4(kk[:, fc, :]), v4(E[1]), ALU.mult, [b_kk, b_E[1]], [bF])
                        tt("dve", F_[:, :, 1, :], v4(bb), v4(E[2]), ALU.mult, [b_bb, b_E[2]], [bF])
                        tt("dve", F_[:, :, 2, :], v4(kd), v4(E[2]), ALU.mult, [b_kd, b_E[2]], [bF])
                        tt("dve", F_[:, :, 3, :], v4(r_t[:, fc, :]), v4(E[0]), ALU.mult, [b_r, b_E[0]], [bF])
                        tt("dve", sc_s[:, 0, fc, :], kd, E[3], ALU.mult, [b_kd, b_E[3]], [b_sc])
                        tt("pool", sc_s[:, 1, fc, :], bb, E[3], ALU.mult, [b_bb, b_E[3]], [b_sc])
                        for half in range(2):
                            hp = 2 * fc + half
                            dst = fmS[d, t0 // 128:t0 // 128 + 4].rearrange("c k (h x) -> k c h x", h=8)[:, :, hp, :]
                            dma(dst, F_[half * 64:(half + 1) * 64].rearrange("p c a t -> p c (a t)"), [bF], ())
                gens = [fc_gen(d, fc) for fc in range(4)]
                while gens:
                    alive = []
                    for gen in gens:
                        try:
                            next(gen)
                            alive.append(gen)
                        except StopIteration:
                            pass
                    gens = alive
                dma(gcS[d, t0 // 128:t0 // 128 + 4].rearrange("c p f -> p c f"), gc_s, [b_gc], ())
                for a_ in range(2):
                    for tb in range(4):
                        ps, pb = nextps()
                        psT = ps[:].bitcast(BF16)
                        for fc in range(4):
                            transpose(psT[:, fc * 128:(fc + 1) * 128], sc_s[:, a_, fc, tb * 128:(tb + 1) * 128],
                                      [b_sc], [pb])
                        cp("act", tm_s[:, tb, a_, :], psT[:, 0:512], [pb], [b_tm])
                dma(tmS[d, t0:t0 + 512, :].rearrange("(b p) x -> p b x", p=128),
                    tm_s.rearrange("p b a f -> p b (a f)"), [b_tm], ())
        S.barrier()

    def pass_P2scan(l):
        A.off = 0
        NG = NCH * 8
        gall = [A.alloc(NG, F32).rearrange("p (c a f) -> p c a f", a=2, f=4) for _ in range(2)]
        b_gall = Buf()
        for d in range(2):
            for half in range(2):
                for c0 in range(0, NCH, 16):
                    c1 = min(NCH, c0 + 16)
                    dma(gall[d][0:64, c0:c1, half, :],
                        gcS[d, c0:c1, half * 64:(half + 1) * 64, :].rearrange("c p f -> p c f"), (), [b_gall])

        class Ctx:
            pass

        def h4(n=1):
            return [A.alloc(512, BF16).rearrange("p (h t) -> p h t", h=4) for _ in range(n)]

        ctxs = []
        for d in range(2):
            cx = Ctx()
            cx.d = d
            cx.fm = [A.alloc(8 * 4 * 128, BF16).rearrange("p (h a t) -> p h a t", h=8, a=4) for _ in range(3)]
            cx.tm = [A.alloc(1024, BF16) for _ in range(3)]
            cx.v = [A.alloc(512, BF16) for _ in range(3)]
            cx.b_in = [Buf() for _ in range(3)]
            cx.N = [h4(2) for _ in range(2)]
            cx.NT = [h4(2) for _ in range(2)]
            cx.bN = [[Buf(), Buf()] for _ in range(2)]
            cx.bNT = [[Buf(), Buf()] for _ in range(2)]
            cx.P = [h4(2) for _ in range(2)]
            cx.ARB = [h4(2) for _ in range(2)]
            cx.AKD = [h4(2) for _ in range(2)]
            cx.ARKD = [h4(2) for _ in range(2)]
            cx.bP = [[Buf(), Buf()] for _ in range(2)]
            cx.bARB = [[Buf(), Buf()] for _ in range(2)]
            cx.bAKD = [[Buf(), Buf()] for _ in range(2)]
            cx.bARKD = [[Buf(), Buf()] for _ in range(2)]
            cx.Xn = A.alloc(512, BF16)
            cx.bXn = Buf()
            cx.U = A.alloc(512, BF16)
            cx.bU = Buf()
            cx.ybuf = [A.alloc(512, F32) for _ in range(2)]
            cx.bY = [Buf(), Buf()]
            cx.S32 = [A.alloc(512, F32) for _ in range(2)]
            cx.Sbf = [A.alloc(512, BF16) for _ in range(2)]
            cx.bS32 = [Buf(), Buf()]
            cx.bSbf = [Buf(), Buf()]
            cx.t1 = A.alloc(512, F32)
            cx.bt1 = Buf()
            cx.cur = 0
            ctxs.append(cx)

        ident4 = cb[:, 0:128].rearrange("p (o t) -> p o t", o=1).to_broadcast([128, 4, 128])

        def m4(m):
            return m.rearrange("p (o t) -> p o t", o=1).to_broadcast([128, 4, 128])

        def ps4(ps):
            return ps[:].rearrange("p (h t) -> p h t", h=4)

        def chunk_of(cx, it):
            return it if cx.d == 0 else NCH - 1 - it

        def load(cx, it):
            c = chunk_of(cx, it)
            i3 = it % 3
            d = cx.d
            dma(cx.fm[i3][0:64].rearrange("p h a t -> p (h a t)"), fmS[d, c], (), [cx.b_in[i3]])
            dma(cx.tm[i3], tmS[d, c * 128:(c + 1) * 128, :], (), [cx.b_in[i3]])
            dma(cx.v[i3], vS[c * 128:(c + 1) * 128, :], (), [cx.b_in[i3]])

        def prod4(g, lf, rf, rbufs):
            ps, pb = nextps()
            for i in range(4):
                h = g * 4 + i
                mm(ps[:, i * 128:(i + 1) * 128], lf(h), rf(h), True, True, rbufs, [pb])
            return ps, pb

        def local(cx, it, g):
            d = cx.d
            i3 = it % 3
            par = it % 2
            fm_ = cx.fm[i3]
            b_in = cx.b_in[i3]
            mS, mSt, mR = (SU, SL, UI) if d == 0 else (SL, SU, LI)
            KK = lambda h: fm_[0:64, h, 0, :]
            BH = lambda h: fm_[0:64, h, 1, :]
            KD = lambda h: fm_[0:64, h, 2, :]
            RT = lambda h: fm_[0:64, h, 3, :]
            cn = 0
            N, NT_ = cx.N[g], cx.NT[g]
            bN, bNT = cx.bN[g], cx.bNT[g]
            P, bP = cx.P[par][g], cx.bP[par][g]
            ps, pb = prod4(g, BH, KK, [b_in])
            tt("dve", N[cn], ps4(ps), m4(mS), ALU.mult, [pb, b_c], [bN[cn]])
            yield
            ps, pb = prod4(g, KK, BH, [b_in])
            tt("dve", NT_[cn], ps4(ps), m4(mSt), ALU.mult, [pb, b_c], [bNT[cn]])
            tt("pool", P, ident4, N[cn], ALU.subtract, [b_c, bN[cn]], [bP])
            yield
            for lvl in range(1, 7):
                nn = 1 - cn
                if lvl < 6:
                    ps, pb = prod4(g, lambda h: NT_[cn][:, h % 4, :], lambda h: N[cn][:, h % 4, :], [bN[cn], bNT[cn]])
                    cp("act", N[nn], ps4(ps), [pb], [bN[nn]])
                ps, pb = prod4(g, lambda h: N[cn][:, h % 4, :], lambda h: NT_[cn][:, h % 4, :], [bN[cn], bNT[cn]])
                cp("act", NT_[nn], ps4(ps), [pb], [bNT[nn]])
                yield
                ps, pb = prod4(g, lambda h: NT_[nn][:, h % 4, :], lambda h: P[:, h % 4, :], [bNT[nn], bP])
                tt("dve", P, ps4(ps), P, ALU.add, [pb, bP], [bP])
                cn = nn
                yield
                if lvl == 1:
                    ps, pb = prod4(g, BH, RT, [b_in])
                    tt("dve", cx.ARB[par][g], ps4(ps), m4(mR), ALU.mult, [pb, b_c], [cx.bARB[par][g]])
                    yield
                elif lvl == 2:
                    ps, pb = prod4(g, KD, KK, [b_in])
                    tt("dve", cx.AKD[par][g], ps4(ps), m4(mS), ALU.mult, [pb, b_c], [cx.bAKD[par][g]])
                    yield
                elif lvl == 3:
                    ps, pb = prod4(g, KD, RT, [b_in])
                    tt("dve", cx.ARKD[par][g], ps4(ps), m4(mR), ALU.mult, [pb, b_c], [cx.bARKD[par][g]])
                    yield

        def chain(cx, it):
            d = cx.d
            c = chunk_of(cx, it)
            i3 = it % 3
            par = it % 2
            fm_, tm_, v_ = cx.fm[i3], cx.tm[i3], cx.v[i3]
            b_in = cx.b_in[i3]
            KK = lambda h: fm_[0:64, h, 0, :]
            RT = lambda h: fm_[0:64, h, 3, :]
            Vh = lambda h: v_[:, h * 64:(h + 1) * 64]
            KDs = lambda h: tm_[:, h * 64:(h + 1) * 64]
            Bs = lambda h: tm_[:, 512 + h * 64:512 + (h + 1) * 64]
            P, bP = cx.P[par], cx.bP[par]
            ARB, bARB = cx.ARB[par], cx.bARB[par]
            AKD, bAKD = cx.AKD[par], cx.bAKD[par]
            ARKD, bARKD = cx.ARKD[par], cx.bARKD[par]
            cur = cx.cur
            nxt = 1 - cur
            S32, Sbf = cx.S32[cur], cx.Sbf[cur]
            bS32, bSbf = cx.bS32[cur], cx.bSbf[cur]
            first = (c % CPS == 0) if d == 0 else ((c + 1) % CPS == 0)
            if first:
                b = c // CPS if d == 0 else (c + 1) // CPS
                halo_fix("pool", S32[0:64, :], b, [bS32], [bS32])
                halo_fix("pool", Sbf[0:64, :], b, [bSbf], [bSbf])
            S32v = S32[0:64, :].rearrange("p (f a v) -> p a f v", f=4, a=2)
            t1v = cx.t1[0:64, :].rearrange("p (f a v) -> p a f v", f=4, a=2)
            gCv = gall[d][0:64, c].rearrange("p a (f o) -> p a f o", o=1).to_broadcast([64, 2, 4, 64])
            tt("pool", t1v, S32v, gCv, ALU.mult, [bS32, b_gall], [cx.bt1])
            ps, pb = nextps()
            for h in range(8):
                g = h // 4
                mm(ps[:, h * 64:(h + 1) * 64], KK(h), Sbf[0:64, h * 64:(h + 1) * 64], True, False, [b_in, bSbf], [pb])
                mm(ps[:, h * 64:(h + 1) * 64], AKD[g][:, h % 4, :], Vh(h), False, True, [bAKD[g], b_in], [pb])
            amul(cx.Xn, ps[:], -1.0, [pb], [cx.bXn])
            yield
            ps, pb = nextps()
            for h in range(8):
                g = h // 4
                mm(ps[:, h * 64:(h + 1) * 64], P[g][:, h % 4, :], cx.Xn[:, h * 64:(h + 1) * 64], True, True,
                   [bP[g], cx.bXn], [pb])
            cp("dve", cx.U, ps[:], [pb], [cx.bU])
            yield
            ps, pb = nextps()
            for h in range(8):
                o = ps[0:64, h * 64:(h + 1) * 64]
                mm(o, KDs(h), Vh(h), True, False, [b_in], [pb])
                mm(o, Bs(h), cx.U[:, h * 64:(h + 1) * 64], False, True, [b_in, cx.bU], [pb])
            tt("dve", cx.Sbf[nxt][0:64, :], ps[0:64, :], cx.t1[0:64, :], ALU.add, [pb, cx.bt1], [cx.bSbf[nxt]])
            tt("dve", cx.S32[nxt][0:64, :], ps[0:64, :], cx.t1[0:64, :], ALU.add, [pb, cx.bt1], [cx.bS32[nxt]])
            cx.cur = nxt
            yield
            ps, pb = nextps()
            for h in range(8):
                g = h // 4
                o = ps[:, h * 64:(h + 1) * 64]
                mm(o, RT(h), Sbf[0:64, h * 64:(h + 1) * 64], True, False, [b_in, bSbf], [pb])
                mm(o, ARKD[g][:, h % 4, :], Vh(h), False, False, [bARKD[g], b_in], [pb])
                mm(o, ARB[g][:, h % 4, :], cx.U[:, h * 64:(h + 1) * 64], False, True, [bARB[g], cx.bU], [pb])
            yb_, bY = cx.ybuf[par], cx.bY[par]
            cp("act", yb_, ps[:], [pb], [bY])
            dma(yS[d, c * 128:(c + 1) * 128, :], yb_, [bY], ())
            yield

        for cx in ctxs:
            load(cx, 0)
        for it in range(NCH + 1):
            gens = []
            if it >= 1:
                gens += [chain(ctxs[0], it - 1), chain(ctxs[1], it - 1)]
            if it < NCH:
                if it + 1 < NCH:
                    for cx in ctxs:
                        load(cx, it + 1)
                for cx in ctxs:
                    for g in range(2):
                        gens.append(local(cx, it, g))
            while gens:
                alive = []
                for gen in gens:
                    try:
                        next(gen)
                        alive.append(gen)
                    except StopIteration:
                        pass
                gens = alive
        S.barrier()

    def pass_P3a(l, xin):
        A.off = 0
        wa = A.alloc(4 * D, BF16).rearrange("p (k n) -> p k n", k=4)
        wb = A.alloc(4 * D, BF16).rearrange("p (k n) -> p k n", k=4)
        wo = A.alloc(8 * D, BF16).rearrange("p (k n) -> p k n", k=8)
        g2 = A.alloc(512, BF16)
        gnw = A.alloc(512, F32)
        gnb = A.alloc(512, F32)
        stg = [A.alloc(D, F32) for _ in range(2)]
        bst = [Buf(), Buf()]
        b_w = Buf()
        for kc in range(4):
            load_weight(wa[:, kc, :], w_a[l, kc * 128:(kc + 1) * 128, :], stg, bst, b_w)
            load_weight(wb[:, kc, :], w_b[l, kc * 128:(kc + 1) * 128, :], stg, bst, b_w)
        for kc in range(8):
            load_weight(wo[:, kc, :], w_out[l, kc * 128:(kc + 1) * 128, :], stg, bst, b_w)
        load_weight(g2, g2_d[l], [s[:, 0:512] for s in stg], bst, b_w)
        dma(gnw, gnwb_d[l][:, 0:512], (), [b_w])
        dma(gnb, gnwb_d[l][:, 512:1024], (), [b_w])

        def a3(n, c, dt):
            return A.alloc(c * n, dt).rearrange("p (c t) -> p c t", c=c)

        q_t = a3(514, 4, BF16)
        bg_t = a3(512, 4, BF16)
        sgd_t = A.alloc(512, BF16)
        sgc_t = a3(512, 8, BF16)
        sgr_t = a3(512, 8, BF16)
        yf = [A.alloc(512, F32) for _ in range(2)]
        ybk = [A.alloc(512, F32) for _ in range(2)]
        v_t = [A.alloc(512, BF16) for _ in range(2)]
        rk_t = [A.alloc(8, F32) for _ in range(2)]
        class TS:
            pass

        tsets = []
        for _i in range(2):
            X = TS()
            X.y32, X.tmp, X.tmp2 = A.alloc(512, F32), A.alloc(512, F32), A.alloc(512, F32)
            X.stat, X.o_bf = A.alloc(64, F32), A.alloc(512, BF16)
            X.b_y32, X.b_tmp, X.b_tmp2, X.b_stat, X.b_o = [Buf() for _ in range(5)]
            tsets.append(X)
        oT = a3(512, 4, BF16)
        cqs = [A.alloc(512, F32) for _ in range(2)]
        b_cqs = [Buf(), Buf()]
        ca = a3(512, 4, BF16)
        mg1 = a3(512, 8, F32)
        x_t = mg1
        merged = a3(512, 8, BF16)
        m32 = a3(512, 8, F32)
        sqm = a3(512, 8, BF16)
        rtmp = A.alloc(512, F32)
        rstd = A.alloc(512, F32)
        x1_t = m32
        (b_q, b_bg, b_sgd, b_sgc, b_sgr, b_x_unused, b_y32, b_tmp, b_tmp2, b_stat, b_o, b_oT, b_cq, b_ca, b_mg1, b_mer,
         b_m32, b_sqm, b_rt, b_rs, b_x1) = [Buf() for _ in range(21)]
        b_x1 = b_m32
        b_tb = [Buf(), Buf()]
        b_x = b_mg1

        def fm(ap):
            return ap.rearrange("(c p) t -> p c t", p=128)

        def h8(ap):
            return ap.rearrange("p (h v) -> p h v", h=8)

        def treduce(out, in_, r, w):
            S.op("dve", lambda e: e.tensor_reduce(out=out, in_=in_, axis=AX.X, op=ALU.add), r, w)

        for ti in range(NT // 512):
            t0 = ti * 512
            dma(q_t, fm(qS)[:, :, t0:t0 + 514], (), [b_q])
            dma(bg_t, fm(bgS)[:, :, t0:t0 + 512], (), [b_bg])
            dma(sgd_t, sgdS[:, t0:t0 + 512], (), [b_sgd])
            dma(sgc_t, fm(sgcS)[:, :, t0:t0 + 512], (), [b_sgc])
            dma(sgr_t, fm(sgrS)[:, :, t0:t0 + 512], (), [b_sgr])
            import os
            P3LVL = int(os.environ.get("P3LVL", "20"))
            if P3LVL < 2:
                continue
            if t0 % SEG == 0:
                halo_fix("dve", q_t[:, :, 0:1], t0 // SEG, [b_q], [b_q])
            if (t0 + 512) % SEG == 0:
                halo_fix("dve", q_t[:, :, 513:514], (t0 + 512) // SEG, [b_q], [b_q])
            for tb in range(4):
                if P3LVL < 3:
                    continue
                pp = tb % 2
                r0 = t0 + tb * 128
                dma(yf[pp], yS[0, r0:r0 + 128, :], (), [b_tb[pp]])
                dma(ybk[pp], yS[1, r0:r0 + 128, :], (), [b_tb[pp]])
                dma(v_t[pp], vS[r0:r0 + 128, :], (), [b_tb[pp]])
                dma(rk_t[pp], rkS[r0:r0 + 128, :], (), [b_tb[pp]])
                bt = b_tb[pp]
                X = tsets[pp]
                y32, tmp, tmp2, stat, o_bf = X.y32, X.tmp, X.tmp2, X.stat, X.o_bf
                b_y32, b_tmp, b_tmp2, b_stat, b_o = X.b_y32, X.b_tmp, X.b_tmp2, X.b_stat, X.b_o

                def st8(i, stat=stat):
                    return stat[:, i * 8:(i + 1) * 8]

                def bc8(i, stat=stat):
                    return stat[:, i * 8:(i + 1) * 8].rearrange("p (h o) -> p h o", o=1).to_broadcast([128, 8, 64])

                tt("pool", y32, yf[pp], ybk[pp], ALU.add, [bt], [b_y32])
                treduce(st8(0), h8(y32), [b_y32], [b_stat])
                act(tmp, y32, AF.Square, [b_y32], [b_tmp])
                treduce(st8(1), h8(tmp), [b_tmp], [b_stat])
                ts("dve", st8(0), st8(0), 1.0 / 64, None, ALU.mult, None, [b_stat], [b_stat])
                tt("dve", st8(2), st8(0), st8(0), ALU.mult, [b_stat], [b_stat])
                stt(st8(3), st8(1), 1.0 / 64, st8(2), ALU.mult, ALU.subtract, [b_stat], [b_stat])
                act(st8(4), st8(3), AF.Ln, [b_stat, b_c], [b_stat], bias=cf[:, C_GNEPS:C_GNEPS + 1], scale=1.0)
                act(st8(5), st8(4), AF.Exp, [b_stat], [b_stat], scale=-0.5)
                if P3LVL < 4:
                    continue
                tt("dve", h8(tmp), h8(y32), bc8(0), ALU.subtract, [b_y32, b_stat], [b_tmp])
                tt("dve", h8(tmp2), h8(tmp), bc8(5), ALU.mult, [b_tmp, b_stat], [b_tmp2])
                tt("dve", tmp, tmp2, gnw, ALU.mult, [b_tmp2, b_w], [b_tmp])
                tt("dve", tmp2, tmp, gnb, ALU.add, [b_tmp, b_w], [b_tmp2])
                rkb = rk_t[pp].rearrange("p (h o) -> p h o", o=1).to_broadcast([128, 8, 64])
                tt("pool", h8(tmp), h8(v_t[pp]), rkb, ALU.mult, [bt], [b_tmp])
                tt("dve", y32, tmp2, tmp, ALU.add, [b_tmp2, b_tmp], [b_y32])
                if P3LVL < 5:
                    continue
                ps, pb = nextps()
                mm(ps[:], sgd_t[:, tb * 128:(tb + 1) * 128], g2, True, True, [b_sgd, b_w], [pb])
                tt("dve", o_bf, ps[:], y32, ALU.mult, [pb, b_y32], [b_o])
                ps, pb = nextps()
                psT = ps[:].bitcast(BF16)
                for fc in range(4):
                    transpose(psT[:, fc * 128:(fc + 1) * 128], o_bf[:, fc * 128:(fc + 1) * 128], [b_o], [pb])
                cp("act", oT[:, :, tb * 128:(tb + 1) * 128], psT[:, 0:512].rearrange("p (c t) -> p c t", c=4),
                   [pb], [b_oT])
            if P3LVL < 6:
                continue
            for fc in range(4):
                cq, b_cq = cqs[fc % 2], b_cqs[fc % 2]
                cw = lambda j: vec[:, V_CONVW + j * 4 + fc:V_CONVW + j * 4 + fc + 1]
                ts("dve", cq, q_t[:, fc, 1:513], cw(1), vec[:, V_CONVB + fc:V_CONVB + fc + 1], ALU.mult, ALU.add,
                   [b_q, b_vec], [b_cq])
                stt(cq, q_t[:, fc, 0:512], cw(0), cq, ALU.mult, ALU.add, [b_q, b_vec, b_cq], [b_cq])
                stt(cq, q_t[:, fc, 2:514], cw(2), cq, ALU.mult, ALU.add, [b_q, b_vec, b_cq], [b_cq])
                tt("pool", ca[:, fc, :], cq, bg_t[:, fc, :], ALU.mult, [b_cq, b_bg], [b_ca])
            if P3LVL < 7:
                continue
            for mc in range(8):
                ps, pb = nextps()
                for kc in range(4):
                    mm(ps[:], wa[:, kc, mc * 128:(mc + 1) * 128], ca[:, kc, :], kc == 0, kc == 3, [b_w, b_ca], [pb])
                tt("dve", mg1[:, mc, :], ps[:], sgc_t[:, mc, :], ALU.mult, [pb, b_sgc], [b_mg1])
            if P3LVL < 8:
                continue
            for mc in range(8):
                ps, pb = nextps()
                for kc in range(4):
                    mm(ps[:], wb[:, kc, mc * 128:(mc + 1) * 128], oT[:, kc, :], kc == 0, kc == 3, [b_w, b_oT], [pb])
                X = tsets[mc % 2]
                tt("dve", X.tmp, ps[:], sgr_t[:, mc, :], ALU.mult, [pb, b_sgr], [X.b_tmp])
                tt(("pool", "dve")[mc % 2], merged[:, mc, :], X.tmp, mg1[:, mc, :], ALU.add, [X.b_tmp, b_mg1], [b_mer])
            dma(x_t, fm(xin)[:, :, t0:t0 + 512], (), [b_x])
            if P3LVL < 9:
                continue
            for mc in range(8):
                ps, pb = nextps()
                for kc in range(8):
                    mm(ps[:], wo[:, kc, mc * 128:(mc + 1) * 128], merged[:, kc, :], kc == 0, kc == 7, [b_w, b_mer], [pb])
                cp("dve", m32[:, mc, :], ps[:], [pb], [b_m32])
                act(sqm[:, mc, :], m32[:, mc, :], AF.Square, [b_m32], [b_sqm])
            if P3LVL < 10:
                continue
            rms_rstd(lambda c: sqm[:, c, :], 8, 512, rtmp, rstd, b_sqm, b_rt, b_rs)
            if P3LVL < 11:
                continue
            for mc in range(8):
                X = tsets[mc % 2]
                tt(("pool", "dve")[mc % 2], X.tmp, m32[:, mc, :], rstd, ALU.mult, [b_m32, b_rs], [X.b_tmp])
                stt(x1_t[:, mc, :], X.tmp, vec[:, V_NMPOST + mc:V_NMPOST + mc + 1], x_t[:, mc, :], ALU.mult, ALU.add,
                    [X.b_tmp, b_vec, b_x], [b_x1])
            if P3LVL < 12:
                continue
            dma(fm(x1S)[:, :, 1 + t0:1 + t0 + 512], x1_t, [b_x1], ())
        S.barrier()

    def pass_P3b(l, xout):
        A.off = 0
        wd = A.alloc(22 * D, BF16).rearrange("p (k n) -> p k n", k=22)
        wpg = A.alloc(8 * D, BF16).rearrange("p (k n) -> p k n", k=8)
        wpl = A.alloc(2 * D, BF16).rearrange("p (k n) -> p k n", k=2)
        mark = A.off
        stg = [A.alloc(D, F32) for _ in range(2)]
        bst = [Buf(), Buf()]
        b_w = Buf()
        for kc in range(22):
            load_weight(wd[:, kc, :], w_down[l, kc * 128:(kc + 1) * 128, :], stg, bst, b_w)
        for kc in range(8):
            load_weight(wpg[:, kc, :], w_pg[l, kc * 128:(kc + 1) * 128, :], stg, bst, b_w)
        for kc in range(2):
            load_weight(wpl[:, kc, :], w_ple[l, kc * 128:(kc + 1) * 128, :], stg, bst, b_w)
        S.barrier()
        A.off = mark
        WM = 412

        def a3(n, c, dt):
            return A.alloc(c * n, dt).rearrange("p (c t) -> p c t", c=c)

        x1w = a3(WM, 8, F32)
        sq = a3(WM, 8, BF16)
        u = a3(WM, 8, BF16)
        rtmp = A.alloc(WM, F32)
        rstd = A.alloc(WM, F32)
        p_t = a3(WM, 2, F32)
        p_b = a3(WM, 2, BF16)
        wj = [A.alloc(8 * 2 * 128, BF16).rearrange("p (k g m) -> p k g m", k=8, g=2) for _ in range(3)]
        b_wj = [Buf() for _ in range(3)]
        class TS:
            pass

        jsets = []
        for _i in range(2):
            X = TS()
            for nm in ("cg", "cv", "g1", "g2_", "g3", "gate", "tmp"):
                setattr(X, nm, A.alloc(WM, F32))
                setattr(X, "b_" + nm, Buf())
            jsets.append(X)
        actb = a3(WM, 22, BF16)
        m32 = a3(WM, 8, F32)
        sqm = sq
        x2_t = a3(WM, 8, F32)
        x2b = u
        (b_x1, b_sq, b_u, b_rt, b_rs, b_p, b_pb, b_cg_u, b_cv_u, b_g1_u, b_g2_u, b_g3_u, b_act, b_m32, b_sqm, b_x2, b_x2b,
         b_gate_u, b_tmp_u) = [Buf() for _ in range(19)]
        b_sqm = b_sq
        b_x2b = b_u

        def fm(ap):
            return ap.rearrange("(c p) t -> p c t", p=128)

        tiles = []
        for sgi in range(NSEG):
            a = sgi * SEG
            npc = (SEG + 409) // 410
            base = SEG // npc
            rem = SEG - base * npc
            for i in range(npc):
                n = base + (1 if i < rem else 0)
                tiles.append((a, n))
                a += n
        wctr = 0
        for (a0, n) in tiles:
            W = n + 2
            dma(x1w[:, :, 0:W], fm(x1S)[:, :, a0:a0 + W], (), [b_x1])
            dma(p_t[:, :, 0:n], pT[l].rearrange("(c p) t -> p c t", p=128)[:, :, a0:a0 + n], (), [b_p])
            cp("pool", p_b[:, :, 0:n], p_t[:, :, 0:n], [b_p], [b_pb])
            act(sq[:, :, 0:W], x1w[:, :, 0:W], AF.Square, [b_x1], [b_sq])
            rms_rstd(lambda c: sq[:, c, 0:W], 8, W, rtmp[:, 0:W], rstd[:, 0:W], b_sq, b_rt, b_rs)
            for c in range(8):
                stt(u[:, c, 0:W], x1w[:, c, 0:W], vec[:, V_NFP + c:V_NFP + c + 1], rstd[:, 0:W], ALU.mult, ALU.mult,
                    [b_x1, b_rs, b_vec], [b_u])
            if a0 % SEG == 0:
                halo_fix("dve", u[:, :, 0:1], a0 // SEG, [b_u], [b_u])
            if (a0 + n) % SEG == 0:
                halo_fix("dve", u[:, :, W - 1:W], (a0 + n) // SEG, [b_u], [b_u])
            for j in range(22):
                X = jsets[j % 2]
                cg, cv, g1, g2_, g3 = X.cg, X.cv, X.g1, X.g2_, X.g3
                b_cg, b_cv, b_g1, b_g2, b_g3 = X.b_cg, X.b_cv, X.b_g1, X.b_g2_, X.b_g3
                wi = wctr % 3
                wctr += 1
                dma(wj[wi].rearrange("p k g m -> p (k g m)"), wupS[l, j], (), [b_wj[wi]])
                res = []
                for gv in range(2):
                    ps, pb = nextps()
                    for kc in range(8):
                        mm(ps[:, 0:W], wj[wi][:, kc, gv, :], u[:, kc, 0:W], kc == 0, kc == 7, [b_wj[wi], b_u], [pb])
                    res.append((ps, pb))
                for gv in range(2):
                    ps, pb = res[gv]
                    c_ = gv * 22 + j
                    dst, bd = (cg, b_cg) if gv == 0 else (cv, b_cv)
                    fw = lambda jj: vec[:, V_FCW + jj * 44 + c_:V_FCW + jj * 44 + c_ + 1]
                    act(dst[:, 0:n], ps[:, 1:W - 1], AF.Identity, [pb, b_vec], [bd],
                        bias=vec[:, V_FCB + c_:V_FCB + c_ + 1], scale=fw(1))
                    stt(dst[:, 0:n], ps[:, 0:n], fw(0), dst[:, 0:n], ALU.mult, ALU.add, [pb, b_vec, bd], [bd])
                    stt(dst[:, 0:n], ps[:, 2:W], fw(2), dst[:, 0:n], ALU.mult, ALU.add, [pb, b_vec, bd], [bd])
                act(g1[:, 0:n], cg[:, 0:n], AF.Square, [b_cg], [b_g1])
                ts("pool", g1[:, 0:n], g1[:, 0:n], 0.044715, 1.0, ALU.mult, ALU.add, [b_g1], [b_g1])
                tt("pool", g2_[:, 0:n], g1[:, 0:n], cg[:, 0:n], ALU.mult, [b_g1, b_cg], [b_g2])
                act(g3[:, 0:n], g2_[:, 0:n], AF.Sigmoid, [b_g2], [b_g3], scale=GELU_C)
                tt("pool", g1[:, 0:n], cg[:, 0:n], cv[:, 0:n], ALU.mult, [b_cg, b_cv, b_g2], [b_g1])
                tt("dve", actb[:, j, 0:n], g1[:, 0:n], g3[:, 0:n], ALU.mult, [b_g1, b_g3], [b_act])
            for mc in range(8):
                ps, pb = nextps()
                for kc in range(22):
                    mm(ps[:, 0:n], wd[:, kc, mc * 128:(mc + 1) * 128], actb[:, kc, 0:n], kc == 0, kc == 21,
                       [b_w, b_act], [pb])
                cp("dve", m32[:, mc, 0:n], ps[:, 0:n], [pb], [b_m32])
                act(sqm[:, mc, 0:n], m32[:, mc, 0:n], AF.Square, [b_m32], [b_sqm])
            rms_rstd(lambda c: sqm[:, c, 0:n], 8, n, rtmp[:, 0:n], rstd[:, 0:n], b_sqm, b_rt, b_rs)
            for mc in range(8):
                tmp, b_tmp = jsets[mc % 2].tmp, jsets[mc % 2].b_tmp
                tt(("pool", "dve")[mc % 2], tmp[:, 0:n], m32[:, mc, 0:n], rstd[:, 0:n], ALU.mult, [b_m32, b_rs], [b_tmp])
                stt(x2_t[:, mc, 0:n], tmp[:, 0:n], vec[:, V_NFPOST + mc:V_NFPOST + mc + 1], x1w[:, mc, 1:W - 1],
                    ALU.mult, ALU.add, [b_tmp, b_vec, b_x1], [b_x2])
                cp("act", x2b[:, mc, 0:n], x2_t[:, mc, 0:n], [b_x2], [b_x2b])
            for mc in range(8):
                ps, pb = nextps()
                for kc in range(8):
                    mm(ps[:, 0:n], wpg[:, kc, mc * 128:(mc + 1) * 128], x2b[:, kc, 0:n], kc == 0, kc == 7,
                       [b_w, b_x2b], [pb])
                gate, b_gate = jsets[mc % 2].gate, jsets[mc % 2].b_gate
                act(gate[:, 0:n], ps[:, 0:n], AF.Sigmoid, [pb], [b_gate])
                ps2, pb2 = nextps()
                for kc in range(2):
                    mm(ps2[:, 0:n], wpl[:, kc, mc * 128:(mc + 1) * 128], p_b[:, kc, 0:n], kc == 0, kc == 1,
                       [b_w, b_pb], [pb2])
                tt("dve", m32[:, mc, 0:n], ps2[:, 0:n], gate[:, 0:n], ALU.mult, [pb2, b_gate], [b_m32])
                act(sqm[:, mc, 0:n], m32[:, mc, 0:n], AF.Square, [b_m32], [b_sqm])
            rms_rstd(lambda c: sqm[:, c, 0:n], 8, n, rtmp[:, 0:n], rstd[:, 0:n], b_sqm, b_rt, b_rs)
            for mc in range(8):
                tmp, b_tmp = jsets[mc % 2].tmp, jsets[mc % 2].b_tmp
                tt(("pool", "dve")[mc % 2], tmp[:, 0:n], m32[:, mc, 0:n], rstd[:, 0:n], ALU.mult, [b_m32, b_rs], [b_tmp])
                stt(x1w[:, mc, 0:n], tmp[:, 0:n], vec[:, V_NPLE + mc:V_NPLE + mc + 1], x2_t[:, mc, 0:n],
                    ALU.mult, ALU.add, [b_tmp, b_vec, b_x2], [b_x1])
            dma(fm(xout)[:, :, a0:a0 + n], x1w[:, :, 0:n], [b_x1], ())
        S.barrier()

    pass_W0()
    for l in range(L):
        xin = xT if l == 0 else xL
        xout = yT if l == L - 1 else xL
        if upto >= 1:
            pass_P1(l, xin)
        if upto >= 2:
            pass_P2pre(l)
        if upto >= 3:
            pass_P2scan(l)
        if upto >= 4:
            pass_P3a(l, xin)
        if upto >= 5:
            pass_P3b(l, xout)
    S.finish()
    S.emit(nc)
    st.close()
    return nc


def make_consts():
    c = np.zeros((128, NCON), np.float32)
    i = np.arange(128)
    c[:, C_IDENT:C_IDENT + 128] = np.eye(128)
    c[:, C_SU:C_SU + 128] = (i[:, None] < i[None, :])
    c[:, C_SL:C_SL + 128] = (i[:, None] > i[None, :])
    c[:, C_UI:C_UI + 128] = (i[:, None] <= i[None, :])
    c[:, C_LI:C_LI + 128] = (i[:, None] >= i[None, :])
    c[:, C_BLK:C_BLK + 128] = ((i[:, None] // 64) == (i[None, :] // 64))
    c[:, C_ONES:C_ONES + 128] = 1.0
    for fc in range(4):
        for h in range(8):
            c[:, C_HSEL + fc * 8 + h] = (h == 2 * fc + i // 64)
    r = np.ones(512, np.float32)
    r[::128] = 0.0
    c[:, C_RESET:C_RESET + 512] = r[None, :]
    c[:, C_EPS] = NORM_EPS
    c[:, C_GNEPS] = GN_EPS
    return c


def make_vecs(inp, L):
    v = np.zeros((L, 128, NV), np.float32)

    def fmaj(a):
        return np.ascontiguousarray(a.reshape(-1, 128).T)

    for l in range(L):
        v[l, :, V_NMP:V_NMP + 8] = fmaj(inp["norm_mix_pre"][l])
        v[l, :, V_NMPOST:V_NMPOST + 8] = fmaj(inp["norm_mix_post"][l])
        v[l, :, V_NFP:V_NFP + 8] = fmaj(inp["norm_ffn_pre"][l])
        v[l, :, V_NFPOST:V_NFPOST + 8] = fmaj(inp["norm_ffn_post"][l])
        v[l, :, V_NPLE:V_NPLE + 8] = fmaj(inp["norm_ple_post"][l])
        for j in range(3):
            v[l, :, V_CONVW + j * 4:V_CONVW + j * 4 + 4] = fmaj(inp["conv_w"][l, j])
            v[l, :, V_FCW + j * 44:V_FCW + j * 44 + 44] = fmaj(inp["ffn_conv_w"][l, j])
        v[l, :, V_CONVB:V_CONVB + 4] = fmaj(inp["conv_b"][l])
        v[l, :, V_FCB:V_FCB + 44] = fmaj(inp["ffn_conv_b"][l])
        v[l, :, V_KK:V_KK + 4] = fmaj(inp["k_k"][l])
        v[l, :, V_KA:V_KA + 4] = fmaj(inp["k_a"][l])
        v[l, :, V_RK:V_RK + 4] = fmaj(inp["r_k"][l].reshape(-1))
        for d in range(2):
            v[l, :, V_W0 + d * 4:V_W0 + d * 4 + 4] = fmaj(inp["decay_w0"][l, d])
            v[l, :, V_A0 + d * 4:V_A0 + d * 4 + 4] = fmaj(inp["iclr_a0"][l, d])
            v[l, 0:64, V_MU + d * 2] = inp["shift_mu"][l, d, 0:64]
            v[l, 0:64, V_MU + d * 2 + 1] = inp["shift_mu"][l, d, 64:128]
    return v


_PROG_CACHE = {}


def run_cores(seqs_per_core, carry_per_core, inp, NSEG, SEG, L, debug=False, upto=99):
    key = (NSEG, SEG, L, debug, upto)
    if key not in _PROG_CACHE:
        _PROG_CACHE[key] = build_program(NSEG, SEG, L, debug, upto)
    nc = _PROG_CACHE[key]
    consts = make_consts()
    vecs = make_vecs(inp, L)
    gnwb = np.zeros((L, 128, 1024), np.float32)
    for l in range(L):
        gnwb[l, :, 0:512] = inp["gn_w"][l][None, :]
        gnwb[l, :, 512:1024] = inp["gn_b"][l][None, :]
    shared = {
        "consts": consts, "vecs": vecs, "gnwb": gnwb,
        "w_in": inp["w_in"], "w_branch_a": inp["w_branch_a"], "w_branch_b": inp["w_branch_b"],
        "w_out": inp["w_out"], "w_up": inp["w_up"], "w_down": inp["w_down"], "w_ple": inp["w_ple"],
        "w_ple_gate": inp["w_ple_gate"], "decay_w2": inp["decay_w2"], "iclr_a2": inp["iclr_a2"],
        "gate_g2": inp["gate_g2"],
    }
    shared = {k: np.ascontiguousarray(np.asarray(v, np.float32)) for k, v in shared.items()}
    in_maps = []
    for (x, p), carry in zip(seqs_per_core, carry_per_core):
        m = dict(shared)
        m["xT"] = np.ascontiguousarray(x.T)
        m["pT"] = np.ascontiguousarray(np.transpose(p, (0, 2, 1)))
        mk = np.zeros((128, NSEG + 1), np.float32)
        mk[:, :] = np.asarray(carry, np.float32)[None, :]
        m["masks"] = mk
        in_maps.append(m)
    res = run_bass_kernel_spmd(nc, in_maps, core_ids=list(range(len(in_maps))))
    return res.results


def kernel(**inp):
    inp = {k: np.asarray(v) for k, v in inp.items()}
    xp, xs = inp["x_prompt"], inp["x_sample"]
    pp, psm = inp["p_prompt"], inp["p_sample"]
    L = pp.shape[0]
    SEG, NSEG = 2048, 6
    per_core = []
    carries = []
    plan = []
    for c in range(8):
        if c < 4:
            segs = [("p", c), ("s", 2 * c), ("s", 2 * c + 1)]
            carry = [0, 1, 1, 1, 0, 0, 0]
        else:
            segs = [("s", 8 + 6 * (c - 4) + i) for i in range(6)]
            carry = [0] * 7
        xs_l, ps_l = [], []
        for kind, i in segs:
            if kind == "p":
                xs_l.append(xp[i])
                ps_l.append(pp[:, i])
            else:
                xs_l.append(xs[i])
                ps_l.append(psm[:, i])
        per_core.append((np.concatenate(xs_l, axis=0), np.concatenate(ps_l, axis=1)))
        carries.append(carry)
        plan.append(segs)
    results = run_cores(per_core, carries, inp, NSEG, SEG, L)
    y_p = np.empty(xp.shape, np.float32)
    y_s = np.empty(xs.shape, np.float32)
    for c in range(8):
        y = np.ascontiguousarray(results[c]["yT"].T)
        off = 0
        for kind, i in plan[c]:
            if kind == "p":
                y_p[i] = y[off:off + 8192]
                off += 8192
            else:
                y_s[i] = y[off:off + 2048]
                off += 2048
    return (y_p, y_s)
```

```python
import numpy as np
from contextlib import ExitStack
import concourse.bass as bass
import concourse.mybir as mybir
from concourse.bass_utils import run_bass_kernel_spmd

F32, BF16 = mybir.dt.float32, mybir.dt.bfloat16
AF = mybir.ActivationFunctionType
ALU = mybir.AluOpType
AX = mybir.AxisListType

D = 1024
INC = 5504
DFF = 2816
CDEC = -0.6065306597126334
NORM_EPS = 1e-6
GN_EPS = 64 * 1e-5
GELU_C = 1.5957691216057308

C_IDENT, C_SU, C_SL, C_UI, C_LI, C_BLK, C_ONES, C_HSEL, C_RESET, C_EPS, C_GNEPS = (
    0, 128, 256, 384, 512, 640, 768, 896, 928, 1440, 1441)
NCON = 1442
V_NMP, V_NMPOST, V_NFP, V_NFPOST, V_NPLE = 0, 8, 16, 24, 32
V_CONVW, V_CONVB, V_KK, V_KA, V_RK, V_W0, V_A0 = 40, 52, 56, 60, 64, 68, 76
V_FCW, V_FCB, V_MU, V_1MKA = 84, 216, 260, 264
NV = 268


class Buf:
    __slots__ = ("lw", "rd")

    def __init__(self):
        self.lw = None
        self.rd = []


ENGS = ("pe", "act", "dve", "pool", "sp")


class Sched:
    def __init__(self, ring=8):
        self.prog = {e: [] for e in ENGS}
        self.cnt = {e: 0 for e in ENGS}
        self.known = {e: {} for e in ENGS}
        self.ring = ring
        self.dma_next = {e: 0 for e in ENGS}
        self.dma_cnt = {e: [0] * ring for e in ENGS}

    def _deps(self, eng, reads, writes):
        deps = {}

        def add(p):
            k, v = p
            if deps.get(k, 0) < v:
                deps[k] = v

        for r in reads:
            if r.lw is not None and not (r.lw[0] == eng and eng == "pe"):
                add(r.lw)
        for w in writes:
            if w.lw is not None and w.lw[0] != eng:
                add(w.lw)
            for p in w.rd:
                if p[0] != eng:
                    add(p)
        return deps

    def _filter(self, eng, deps):
        kn = self.known[eng]
        out = []
        for k, v in deps.items():
            if kn.get(k, 0) >= v:
                continue
            kn[k] = v
            out.append((k, v))
        return out

    def _commit(self, me, reads, writes):
        for r in reads:
            r.rd.append(me)
        for w in writes:
            w.lw = me
            w.rd = []

    def op(self, eng, fn, reads=(), writes=()):
        waits = self._filter(eng, self._deps(eng, reads, writes))
        self.cnt[eng] += 1
        self.prog[eng].append((waits, fn, (eng, 1)))
        self._commit((eng, self.cnt[eng]), reads, writes)

    def dma(self, eng, fn, reads=(), writes=()):
        deps = self._deps(eng, reads, writes)
        slot = self.dma_next[eng] % self.ring
        self.dma_next[eng] += 1
        key = ("dma", eng, slot)
        c = self.dma_cnt[eng][slot]
        if c > 0 and deps.get(key, 0) < 16 * c:
            deps[key] = 16 * c
        waits = self._filter(eng, deps)
        self.dma_cnt[eng][slot] = c + 1
        self.prog[eng].append((waits, fn, (key, 16)))
        self._commit((key, 16 * (c + 1)), reads, writes)

    def _all(self):
        deps = {}
        for e in ENGS:
            if e != "sp" and self.cnt[e] > 0:
                deps[e] = self.cnt[e]
            for slot in range(self.ring):
                c = self.dma_cnt[e][slot]
                if c > 0:
                    deps[("dma", e, slot)] = 16 * c
        return deps

    def barrier(self):
        deps = self._all()
        for e in ENGS:
            d = {k: v for k, v in deps.items() if k != e}
            waits = self._filter(e, d)
            if waits:
                self.prog[e].append((waits, None, None))

    def finish(self):
        self.barrier()

    def emit(self, nc):
        keys = [e for e in ENGS if e != "sp" and self.cnt[e] > 0]
        for e in ENGS:
            for slot in range(self.ring):
                if self.dma_cnt[e][slot] > 0:
                    keys.append(("dma", e, slot))
        with ExitStack() as st:
            sems = {}
            for i, k in enumerate(keys):
                sems[k] = st.enter_context(nc.semaphore("s%d" % i))
            block = st.enter_context(nc.Block())

            def run(engname):
                def body(e):
                    for waits, fn, inc in self.prog[engname]:
                        for k, v in waits:
                            e.wait_ge(sems[k], v)
                        if fn is not None:
                            fn(e).then_inc(sems[inc[0]], inc[1])
                return body

            block.sync(run("sp"))
            block.tensor(run("pe"))
            block.scalar(run("act"))
            block.vector(run("dve"))
            block.gpsimd(run("pool"))


class Arena:
    def __init__(self, ap, size):
        self.ap, self.size, self.off = ap, size, 0

    def alloc(self, n, dtype):
        n16 = n * (2 if dtype == F32 else 1)
        start = (self.off + 15) // 16 * 16
        assert start + n16 <= self.size, ("arena overflow", start + n16, self.size)
        v = self.ap[:, start:start + n16]
        if dtype == F32:
            v = v.bitcast(F32)
        self.off = start + n16
        return v


def build_program(NSEG, SEG, DEPTH, debug=False, upto=99):
    NT = NSEG * SEG
    NCH = NT // 128
    CPS = SEG // 128
    assert NT % 512 == 0 and SEG % 128 == 0
    nc = bass.Bass("TRN2", target_bir_lowering=False)
    L = DEPTH

    def din(name, shape, dt=F32):
        return nc.dram_tensor(name, list(shape), dt, kind="ExternalInput").ap()

    def dscr(name, shape, dt):
        kind = "ExternalOutput" if debug else "Internal"
        return nc.dram_tensor(name, list(shape), dt, kind=kind).ap()

    xT = din("xT", [D, NT])
    pT = din("pT", [L, 256, NT])
    masks_d = din("masks", [128, NSEG + 1])
    consts_d = din("consts", [128, NCON])
    vecs_d = din("vecs", [L, 128, NV])
    gnwb_d = din("gnwb", [L, 128, 1024])
    w_in = din("w_in", [L, D, INC])
    w_a = din("w_branch_a", [L, 512, D])
    w_b = din("w_branch_b", [L, 512, D])
    w_out = din("w_out", [L, D, D])
    w_up = din("w_up", [L, D, 2 * DFF])
    w_down = din("w_down", [L, DFF, D])
    w_ple = din("w_ple", [L, 256, D])
    w_pg = din("w_ple_gate", [L, D, D])
    dw2_d = din("decay_w2", [L, 2, 64, 512])
    a2_d = din("iclr_a2", [L, 2, 64, 512])
    g2_d = din("gate_g2", [L, 128, 512])
    yT = nc.dram_tensor("yT", [D, NT], F32, kind="ExternalOutput").ap()

    qS = dscr("qS", [512, NT + 2], BF16)
    bgS = dscr("bgS", [512, NT], BF16)
    rS = dscr("rS", [512, NT], BF16)
    kS = dscr("kS", [512, NT], BF16)
    vS = dscr("vS", [NT, 512], BF16)
    zS = dscr("zS", [256, NT + 2], F32)
    sgdS = dscr("sgdS", [128, NT], BF16)
    sgcS = dscr("sgcS", [D, NT], BF16)
    sgrS = dscr("sgrS", [D, NT], BF16)
    rkS = dscr("rkS", [NT, 8], F32)
    fmS = dscr("fmS", [2, NCH, 64, 8 * 4 * 128], BF16)
    tmS = dscr("tmS", [2, NT, 1024], BF16)
    gcS = dscr("gcS", [2, NCH, 128, 4], F32)
    yS = dscr("yS", [2, NT, 512], F32)
    x1S = dscr("x1S", [D, NT + 2], F32)
    xL = dscr("xL", [D, NT], F32)
    wupS = dscr("wupS", [L, 22, 128, 8 * 2 * 128], BF16)

    S = Sched()
    st = ExitStack()
    ARENA_N = 90 * 1024
    arena_t = st.enter_context(nc.sbuf_tensor("arena", [128, ARENA_N], BF16))
    cf = st.enter_context(nc.sbuf_tensor("cf", [128, NCON], F32))
    cb = st.enter_context(nc.sbuf_tensor("cb", [128, 256], BF16))
    mk = st.enter_context(nc.sbuf_tensor("mk", [128, NSEG + 1], F32))
    vec = st.enter_context(nc.sbuf_tensor("vec", [128, NV], F32))
    psum = [st.enter_context(nc.psum_tensor("ps%d" % i, [128, 512], F32)) for i in range(8)]
    psb = [Buf() for _ in range(8)]
    A = Arena(arena_t, ARENA_N)
    b_c = Buf()
    b_vec = Buf()
    ident_bf = cb[:, 0:128]
    ones_bf = cb[:, 128:256]
    pctr = [0]

    def nextps():
        i = pctr[0] % 8
        pctr[0] += 1
        return psum[i], psb[i]

    def dma(out, in_, r=(), w=()):
        S.dma("sp", lambda e: e.dma_start(out=out, in_=in_), r, w)

    def mm(out, lhsT, rhs, start, stop, r, w):
        S.op("pe", lambda e: e.matmul(out, lhsT=lhsT, rhs=rhs, start=start, stop=stop), r, w)

    def transpose(out, in_, r, w):
        S.op("pe", lambda e: e.transpose(out, in_, ident_bf), list(r) + [b_c], w)

    def act(out, in_, func, r, w, bias=None, scale=None):
        kw = {}
        if bias is not None:
            kw["bias"] = bias
        if scale is not None:
            kw["scale"] = scale
        S.op("act", lambda e: e.activation(out=out, in_=in_, func=func, **kw), r, w)

    def tt(eng, out, in0, in1, op, r, w):
        S.op(eng, lambda e: e.tensor_tensor(out=out, in0=in0, in1=in1, op=op), r, w)

    def ts(eng, out, in0, s1, s2, op0, op1, r, w):
        if op1 is None:
            S.op(eng, lambda e: e.tensor_scalar(out=out, in0=in0, scalar1=s1, scalar2=None, op0=op0), r, w)
        else:
            S.op(eng, lambda e: e.tensor_scalar(out=out, in0=in0, scalar1=s1, scalar2=s2, op0=op0, op1=op1), r, w)

    def stt(out, in0, scalar, in1, op0, op1, r, w):
        S.op("dve", lambda e: e.scalar_tensor_tensor(out=out, in0=in0, scalar=scalar, in1=in1, op0=op0, op1=op1), r, w)

    def cp(eng, out, in_, r, w):
        if eng == "act":
            S.op("act", lambda e: e.copy(out=out, in_=in_), r, w)
        else:
            S.op(eng, lambda e: e.tensor_copy(out=out, in_=in_), r, w)

    def amul(out, in_, m, r, w):
        S.op("act", lambda e: e.mul(out=out, in_=in_, mul=m), r, w)

    def memset(eng, ap, val, w):
        S.op(eng, lambda e: e.memset(ap, val), (), w)

    def recip(out, in_, r, w):
        S.op("dve", lambda e: e.reciprocal(out=out, in_=in_), r, w)

    cast_rr = [0]

    def load_weight(dst, src, stages, bst, bdst):
        n = dst.shape[-1]
        i = cast_rr[0]
        cast_rr[0] += 1
        sg = stages[i % len(stages)][:, 0:n]
        bs = bst[i % len(stages)]
        dma(sg, src, (), [bs])
        cp(("dve", "act", "pool")[i % 3], dst, sg, [bs], [bdst])

    def halo_fix(eng, ap, b, rbufs, wbufs):
        if b == 0 or b == NSEG:
            memset(eng, ap, 0.0, wbufs)
        else:
            np_ = ap.shape[0]
            ts(eng, ap, ap, mk[0:np_, b:b + 1], None, ALU.mult, None, list(rbufs) + [b_c], wbufs)

    def rms_rstd(sq_chunks, nchunk, W, rstd_tmp, rstd, b_sq, b_tmp, b_rstd):
        ps, pb = nextps()
        for c in range(nchunk):
            mm(ps[:, 0:W], ones_bf, sq_chunks(c), c == 0, c == nchunk - 1, [b_sq, b_c], [pb])
        act(rstd_tmp, ps[:, 0:W], AF.Ln, [pb, b_c], [b_tmp], bias=cf[:, C_EPS:C_EPS + 1], scale=1.0 / D)
        act(rstd, rstd_tmp, AF.Exp, [b_tmp], [b_rstd], scale=-0.5)

    dma(cf[:], consts_d[:, :], (), [b_c])
    dma(mk[:], masks_d[:, :], (), [b_c])
    cp("dve", ident_bf, cf[:, C_IDENT:C_IDENT + 128], [b_c], [b_c])
    cp("dve", ones_bf, cf[:, C_ONES:C_ONES + 128], [b_c], [b_c])
    SU = cf[:, C_SU:C_SU + 128]
    SL = cf[:, C_SL:C_SL + 128]
    UI = cf[:, C_UI:C_UI + 128]
    LI = cf[:, C_LI:C_LI + 128]
    BLK = cf[:, C_BLK:C_BLK + 128]
    RESET = cf[:, C_RESET:C_RESET + 512]

    def pass_W0():
        A.off = 0
        stg = [A.alloc(2 * DFF, F32) for _ in range(2)]
        s16 = [A.alloc(2 * DFF, BF16) for _ in range(2)]
        bs = [Buf(), Buf()]
        b16 = [Buf(), Buf()]
        i = 0
        for l in range(L):
            for kc in range(8):
                s = i % 2
                dma(stg[s], w_up[l, kc * 128:(kc + 1) * 128, :], (), [bs[s]])
                cp(("dve", "act", "pool")[i % 3], s16[s], stg[s], [bs[s]], [b16[s]])
                for gv in range(2):
                    dst = wupS[l].rearrange("j p (k g m) -> p j k g m", k=8, g=2)[:, :, kc, gv, :]
                    src = s16[s][:, gv * DFF:(gv + 1) * DFF].rearrange("p (j m) -> p j m", m=128)
                    dma(dst, src, [b16[s]], ())
                i += 1
        S.barrier()

    def pass_P1(l, xin):
        A.off = 0
        W1 = A.alloc(8 * INC, BF16).rearrange("p (k n) -> p k n", k=8)
        bW1 = Buf()
        mark = A.off
        stg = [A.alloc(INC, F32) for _ in range(2)]
        bst = [Buf(), Buf()]
        dma(vec[:], vecs_d[l], (), [b_vec])
        for kc in range(8):
            load_weight(W1[:, kc, :], w_in[l, kc * 128:(kc + 1) * 128, :], stg, bst, bW1)
        S.barrier()
        A.off = mark
        xt = A.alloc(8 * 512, F32).rearrange("p (c t) -> p c t", c=8)
        sq = A.alloc(8 * 512, BF16).rearrange("p (c t) -> p c t", c=8)
        ub = A.alloc(8 * 512, BF16).rearrange("p (c t) -> p c t", c=8)
        rtmp = A.alloc(512, F32)
        rstd = A.alloc(512, F32)
        hc_s = A.alloc(4 * 512, BF16).rearrange("p (c t) -> p c t", c=4)
        bg_s = A.alloc(4 * 512, BF16).rearrange("p (c t) -> p c t", c=4)
        q_s = A.alloc(4 * 512, BF16).rearrange("p (c t) -> p c t", c=4)
        r_s = A.alloc(4 * 512, BF16).rearrange("p (c t) -> p c t", c=4)
        k_s = A.alloc(4 * 512, BF16).rearrange("p (c t) -> p c t", c=4)
        v_s = A.alloc(4 * 512, BF16).rearrange("p (c t) -> p c t", c=4)
        z_s = A.alloc(2 * 512, F32).rearrange("p (c t) -> p c t", c=2)
        sgd_s = A.alloc(512, BF16)
        sgc_s = A.alloc(8 * 512, BF16).rearrange("p (c t) -> p c t", c=8)
        sgr_s = A.alloc(8 * 512, BF16).rearrange("p (c t) -> p c t", c=8)
        b_xt, b_sq, b_ub, b_rt, b_rs = Buf(), Buf(), Buf(), Buf(), Buf()
        b_hc, b_bg, b_q, b_r, b_k, b_v, b_z, b_sgd, b_sgc, b_sgr = [Buf() for _ in range(10)]

        def fm(ap):
            return ap.rearrange("(c p) t -> p c t", p=128)

        import os
        P1LVL = int(os.environ.get("P1LVL", "20"))
        for ti in range(NT // 512):
            if P1LVL < 1:
                break
            t0 = ti * 512
            dma(xt, fm(xin)[:, :, t0:t0 + 512], (), [b_xt])
            act(sq, xt, AF.Square, [b_xt], [b_sq])
            if P1LVL < 2:
                continue
            rms_rstd(lambda c: sq[:, c, :], 8, 512, rtmp, rstd, b_sq, b_rt, b_rs)
            if P1LVL < 3:
                continue
            for c in range(8):
                stt(ub[:, c, :], xt[:, c, :], vec[:, V_NMP + c:V_NMP + c + 1], rstd, ALU.mult, ALU.mult,
                    [b_xt, b_rs, b_vec], [b_ub])

            def proj(f):
                ps, pb = nextps()
                for kc in range(8):
                    mm(ps[:], W1[:, kc, f * 128:(f + 1) * 128], ub[:, kc, :], kc == 0, kc == 7, [bW1, b_ub], [pb])
                return ps, pb

            if P1LVL < 4:
                continue
            for f in range(43):
                if 20 <= f < 24:
                    continue
                if P1LVL < 20 and f >= {4: 4, 5: 8, 6: 12, 7: 20, 8: 27, 9: 28, 10: 34, 11: 35, 12: 42, 13: 43}[P1LVL]:
                    continue
                ps, pb = proj(f)
                if f < 4:
                    cp("act", hc_s[:, f, :], ps[:], [pb], [b_hc])
                elif f < 8:
                    cp("act", bg_s[:, f - 4, :], ps[:], [pb], [b_bg])
                    if f == 7:
                        dma(fm(bgS)[:, :, t0:t0 + 512], bg_s, [b_bg], ())
                elif f < 12:
                    tt("dve", q_s[:, f - 8, :], ps[:], hc_s[:, f - 8, :], ALU.mult, [pb, b_hc], [b_q])
                    if f == 11:
                        dma(fm(qS)[:, :, 1 + t0:1 + t0 + 512], q_s, [b_q], ())
                elif f < 16:
                    cp("act", r_s[:, f - 12, :], ps[:], [pb], [b_r])
                    if f == 15:
                        dma(fm(rS)[:, :, t0:t0 + 512], r_s, [b_r], ())
                elif f < 20:
                    cp("dve", k_s[:, f - 16, :], ps[:], [pb], [b_k])
                    if f == 19:
                        dma(fm(kS)[:, :, t0:t0 + 512], k_s, [b_k], ())
                elif f < 26:
                    cp("act", z_s[:, f - 24, :], ps[:], [pb], [b_z])
                    if f == 25:
                        dma(fm(zS)[:, :, 1 + t0:1 + t0 + 512], z_s, [b_z], ())
                elif f == 26:
                    act(sgd_s, ps[:], AF.Sigmoid, [pb], [b_sgd])
                    dma(sgdS[:, t0:t0 + 512], sgd_s, [b_sgd], ())
                elif f < 35:
                    act(sgc_s[:, f - 27, :], ps[:], AF.Sigmoid, [pb], [b_sgc])
                    if f == 34:
                        dma(fm(sgcS)[:, :, t0:t0 + 512], sgc_s, [b_sgc], ())
                else:
                    act(sgr_s[:, f - 35, :], ps[:], AF.Sigmoid, [pb], [b_sgr])
                    if f == 42:
                        dma(fm(sgrS)[:, :, t0:t0 + 512], sgr_s, [b_sgr], ())
            for tb in range(4):
                ps, pb = nextps()
                for kc in range(8):
                    mm(ps[:], ub[:, kc, tb * 128:(tb + 1) * 128], W1[:, kc, 2560:3072], kc == 0, kc == 7,
                       [bW1, b_ub], [pb])
                cp(("dve", "act")[tb % 2], v_s[:, tb, :], ps[:], [pb], [b_v])
            dma(vS[t0:t0 + 512, :].rearrange("(b p) f -> p b f", p=128), v_s, [b_v], ())
        S.barrier()

    def pass_P2pre(l):
        A.off = 0
        dw2 = A.alloc(2 * 512, BF16).rearrange("p (d n) -> p d n", d=2)
        a2 = A.alloc(2 * 512, BF16).rearrange("p (d n) -> p d n", d=2)
        stg = [A.alloc(512, F32) for _ in range(2)]
        bst = [Buf(), Buf()]
        b_w = Buf()
        for d in range(2):
            load_weight(dw2[0:64, d, :], dw2_d[l, d], [s[0:64] for s in stg], bst, b_w)
            load_weight(a2[0:64, d, :], a2_d[l, d], [s[0:64] for s in stg], bst, b_w)
        ts("dve", vec[:, V_1MKA:V_1MKA + 4], vec[:, V_KA:V_KA + 4], -1.0, 1.0, ALU.mult, ALU.add, [b_vec], [b_vec])
        r_t = A.alloc(4 * 512, BF16).rearrange("p (c t) -> p c t", c=4)
        k_t = A.alloc(4 * 512, BF16).rearrange("p (c t) -> p c t", c=4)
        zt = [A.alloc(514, F32) for _ in range(4)]
        kk = A.alloc(4 * 512, F32).rearrange("p (c t) -> p c t", c=4)
        prod_raw = A.alloc(4 * 512 * 2, BF16)
        prod = prod_raw.bitcast(F32).rearrange("p (c t) -> p c t", c=4)
        rk_s = A.alloc(32, F32)
        zs = [A.alloc(512, F32) for _ in range(2)]
        tz = A.alloc(512, BF16)
        zab = A.alloc(512, BF16)

        class TS:
            pass

        sets = []
        for _i in range(4):
            X = TS()
            for nm in ("t2", "sgm", "av", "cs", "ex", "rs_", "ri", "kd", "bb"):
                setattr(X, nm, A.alloc(512, F32))
                setattr(X, "b_" + nm, Buf())
            X.E = [A.alloc(512, F32) for _ in range(4)]
            X.b_E = [Buf() for _ in range(4)]
            X.t1, X.b_t1 = X.E[0], X.b_E[0]
            X.t3, X.b_t3 = X.E[1], X.b_E[1]
            sets.append(X)
        fmst = [A.alloc(4 * 4 * 128, BF16).rearrange("p (c a t) -> p c a t", c=4, a=4) for _ in range(4)]
        sc_s = prod_raw.rearrange("p (a c t) -> p a c t", a=2, c=4)
        tm_s = A.alloc(4 * 2 * 512, BF16).rearrange("p (b a f) -> p b a f", b=4, a=2)
        gc_s = A.alloc(16, F32).rearrange("p (c f) -> p c f", c=4)
        b_r, b_k, b_kk, b_prod, b_rk = [Buf() for _ in range(5)]
        b_z = [Buf() for _ in range(4)]
        b_zs = [Buf(), Buf()]
        b_tz, b_zab = Buf(), Buf()
        b_fm = [Buf() for _ in range(4)]
        b_sc, b_tm, b_gc = b_prod, Buf(), Buf()

        def fm(ap):
            return ap.rearrange("(c p) t -> p c t", p=128)

        for ti in range(NT // 512):
            t0 = ti * 512
            dma(r_t, fm(rS)[:, :, t0:t0 + 512], (), [b_r])
            dma(k_t, fm(kS)[:, :, t0:t0 + 512], (), [b_k])
            for i in range(4):
                dma(zt[i][0:64, :], zS[i * 64:(i + 1) * 64, t0:t0 + 514], (), [b_z[i]])
            if t0 % SEG == 0:
                for i in (0, 1):
                    halo_fix("pool", zt[i][0:64, 0:1], t0 // SEG, [b_z[i]], [b_z[i]])
            if (t0 + 512) % SEG == 0:
                for i in (2, 3):
                    halo_fix("pool", zt[i][0:64, 513:514], (t0 + 512) // SEG, [b_z[i]], [b_z[i]])
            for fc in range(4):
                X = sets[fc]
                ts("dve", X.t1, k_t[:, fc, :], vec[:, V_KK + fc:V_KK + fc + 1], None, ALU.mult, None, [b_k, b_vec], [X.b_t1])
                tt("dve", X.t2, X.t1, X.t1, ALU.mult, [X.b_t1], [X.b_t2])
            for fc in range(4):
                X = sets[fc]
                ps, pb = nextps()
                mm(ps[:], BLK, X.t2, True, True, [X.b_t2, b_c], [pb])
                ts("dve", X.t3, ps[:], 1e-24, None, ALU.max, None, [pb], [X.b_t3])
                stt(prod[:, fc, :], r_t[:, fc, :], vec[:, V_RK + fc:V_RK + fc + 1], k_t[:, fc, :], ALU.mult, ALU.mult,
                    [b_r, b_k, b_vec], [b_prod])
            for fc in range(4):
                X = sets[fc]
                act(X.t3, X.t3, AF.Ln, [X.b_t3], [X.b_t3])
            for fc in range(4):
                X = sets[fc]
                act(X.t3, X.t3, AF.Exp, [X.b_t3], [X.b_t3], scale=-0.5)
            for fc in range(4):
                X = sets[fc]
                tt("dve", kk[:, fc, :], X.t1, X.t3, ALU.mult, [X.b_t1, X.b_t3], [b_kk])
            ps, pb = nextps()
            for tb in range(4):
                for fc in range(4):
                    mm(ps[:, tb * 8:(tb + 1) * 8], prod[:, fc, tb * 128:(tb + 1) * 128],
                       cf[:, C_HSEL + fc * 8:C_HSEL + (fc + 1) * 8], fc == 0, fc == 3, [b_prod, b_c], [pb])
            cp("act", rk_s, ps[:, 0:32], [pb], [b_rk])
            dma(rkS[t0:t0 + 512, :].rearrange("(b p) h -> p b h", p=128), rk_s.rearrange("p (b h) -> p b h", b=4),
                [b_rk], ())
            for d in range(2):
                for part in range(2):
                    t1, b_t1 = sets[part].t1, sets[part].b_t1
                    Z = zt[d * 2 + part]
                    cur = Z[0:64, 1:513]
                    sh = Z[0:64, 0:512] if d == 0 else Z[0:64, 2:514]
                    bz = b_z[d * 2 + part]
                    tt("dve", t1[0:64, :], sh, cur, ALU.subtract, [bz], [b_t1])
                    stt(zs[part][0:64, :], t1[0:64, :], vec[0:64, V_MU + d * 2 + part:V_MU + d * 2 + part + 1], cur,
                        ALU.mult, ALU.add, [b_t1, bz, b_vec], [b_zs[part]])
                act(tz[0:64, :], zs[0][0:64, :], AF.Tanh, [b_zs[0]], [b_tz])
                cp("dve", zab[0:64, :], zs[1][0:64, :], [b_zs[1]], [b_zab])
                def fc_gen(d, fc):
                        X = sets[fc]
                        t2, sgm, av, cs, ex, rs_, ri, kd, bb, E = X.t2, X.sgm, X.av, X.cs, X.ex, X.rs_, X.ri, X.kd, X.bb, X.E
                        b_t2, b_sgm, b_av, b_cs, b_ex, b_rs, b_ri, b_kd, b_bb, b_E = (
                            X.b_t2, X.b_sgm, X.b_av, X.b_cs, X.b_ex, X.b_rs_, X.b_ri, X.b_kd, X.b_bb, X.b_E)
                        ps, pb = nextps()
                        mm(ps[:], dw2[0:64, d, fc * 128:(fc + 1) * 128], tz[0:64, :], True, True, [b_w, b_tz], [pb])
                        act(sgm, ps[:], AF.Sigmoid, [pb, b_vec], [b_sgm],
                            bias=vec[:, V_W0 + d * 4 + fc:V_W0 + d * 4 + fc + 1], scale=1.0)
                        ps2, pb2 = nextps()
                        mm(ps2[:], a2[0:64, d, fc * 128:(fc + 1) * 128], zab[0:64, :], True, True, [b_w, b_zab], [pb2])
                        act(av, ps2[:], AF.Sigmoid, [pb2, b_vec], [b_av],
                            bias=vec[:, V_A0 + d * 4 + fc:V_A0 + d * 4 + fc + 1], scale=1.0)
                        yield
                        S.op("dve", lambda e, cs=cs, sgm=sgm: e.tensor_tensor_scan(out=cs, data0=RESET, data1=sgm, initial=0.0,
                                                                                   op0=ALU.mult, op1=ALU.add),
                             [b_sgm, b_c], [b_cs])
                        cs3 = cs.rearrange("p (c t) -> p c t", c=4)
                        tot_bc = cs3[:, :, 127:128].to_broadcast([128, 4, 128])
                        tt("pool", ex, cs, sgm, ALU.subtract, [b_cs, b_sgm], [b_ex])
                        tt("dve", rs_.rearrange("p (c t) -> p c t", c=4), tot_bc, cs3, ALU.subtract, [b_cs], [b_rs])
                        if d == 0:
                            e1s, e2s, e4s = cs, ex, rs_
                            br1, br2, br4 = b_cs, b_ex, b_rs
                        else:
                            tt("dve", ri, rs_, sgm, ALU.add, [b_rs, b_sgm], [b_ri])
                            e1s, e2s, e4s = ri, rs_, ex
                            br1, br2, br4 = b_ri, b_rs, b_ex
                        ts("dve", t2, av, vec[:, V_KA + fc:V_KA + fc + 1], vec[:, V_1MKA + fc:V_1MKA + fc + 1],
                           ALU.mult, ALU.add, [b_av, b_vec], [b_t2])
                        tt("dve", kd, t2, k_t[:, fc, :], ALU.mult, [b_t2, b_k], [b_kd])
                        tt("pool", bb, kk[:, fc, :], av, ALU.mult, [b_kk, b_av], [b_bb])
                        yield
                        act(E[0], e1s, AF.Exp, [br1], [b_E[0]], scale=CDEC)
                        act(E[1], e2s, AF.Exp, [br2], [b_E[1]], scale=CDEC)
                        act(E[2], e1s, AF.Exp, [br1], [b_E[2]], scale=-CDEC)
                        act(E[3], e4s, AF.Exp, [br4], [b_E[3]], scale=CDEC)
                        act(gc_s[:, :, fc], cs3[:, :, 127], AF.Exp, [b_cs], [b_gc], scale=CDEC)
                        yield
                        F_ = fmst[fc]
                        bF = b_fm[fc]

                        def v4(ap):
                            return ap.rearrange("p (c t) -> p c t", c=4)

                        tt("dve", F_[:, :, 0, :], v4(kk[:, fc, :]), v4(E[1]), ALU.mult, [b_kk, b_E[1]], [bF])
                        tt("dve", F_[:, :, 1, :], v4(bb), v4(E[2]), ALU.mult, [b_bb, b_E[2]], [bF])
                        tt("dve", F_[:, :, 2, :], v4(kd), v4(E[2]), ALU.mult, [b_kd, b_E[2]], [bF])
                        tt("dve", F_[:, :, 3, :], v4(r_t[:, fc, :]), v4(E[0]), ALU.mult, [b_r, b_E[0]], [bF])
                        tt("dve", sc_s[:, 0, fc, :], kd, E[3], ALU.mult, [b_kd, b_E[3]], [b_sc])
                        tt("pool", sc_s[:, 1, fc, :], bb, E[3], ALU.mult, [b_bb, b_E[3]], [b_sc])
                        for half in range(2):
                            hp = 2 * fc + half
                            dst = fmS[d, t0 // 128:t0 // 128 + 4].rearrange("c k (h x) -> k c h x", h=8)[:, :, hp, :]
                            dma(dst, F_[half * 64:(half + 1) * 64].rearrange("p c a t -> p c (a t)"), [bF], ())
                gens = [fc_gen(d, fc) for fc in range(4)]
                while gens:
                    alive = []
                    for gen in gens:
                        try:
                            next(gen)
                            alive.append(gen)
                        except StopIteration:
                            pass
                    gens = alive
                dma(gcS[d, t0 // 128:t0 // 128 + 4].rearrange("c p f -> p c f"), gc_s, [b_gc], ())
                for a_ in range(2):
                    for tb in range(4):
                        ps, pb = nextps()
                        psT = ps[:].bitcast(BF16)
                        for fc in range(4):
                            transpose(psT[:, fc * 128:(fc + 1) * 128], sc_s[:, a_, fc, tb * 128:(tb + 1) * 128],
                                      [b_sc], [pb])
                        cp("act", tm_s[:, tb, a_, :], psT[:, 0:512], [pb], [b_tm])
                dma(tmS[d, t0:t0 + 512, :].rearrange("(b p) x -> p b x", p=128),
                    tm_s.rearrange("p b a f -> p b (a f)"), [b_tm], ())
        S.barrier()

    def pass_P2scan(l):
        A.off = 0
        NG = NCH * 8
        gall = [A.alloc(NG, F32).rearrange("p (c a f) -> p c a f", a=2, f=4) for _ in range(2)]
        b_gall = Buf()
        for d in range(2):
            for half in range(2):
                for c0 in range(0, NCH, 16):
                    c1 = min(NCH, c0 + 16)
                    dma(gall[d][0:64, c0:c1, half, :],
                        gcS[d, c0:c1, half * 64:(half + 1) * 64, :].rearrange("c p f -> p c f"), (), [b_gall])

        class Ctx:
            pass

        def h4(n=1):
            return [A.alloc(512, BF16).rearrange("p (h t) -> p h t", h=4) for _ in range(n)]

        ctxs = []
        for d in range(2):
            cx = Ctx()
            cx.d = d
            cx.fm = [A.alloc(8 * 4 * 128, BF16).rearrange("p (h a t) -> p h a t", h=8, a=4) for _ in range(3)]
            cx.tm = [A.alloc(1024, BF16) for _ in range(3)]
            cx.v = [A.alloc(512, BF16) for _ in range(3)]
            cx.b_in = [Buf() for _ in range(3)]
            cx.N = [h4(2) for _ in range(2)]
            cx.NT = [h4(2) for _ in range(2)]
            cx.bN = [[Buf(), Buf()] for _ in range(2)]
            cx.bNT = [[Buf(), Buf()] for _ in range(2)]
            cx.P = [h4(2) for _ in range(2)]
            cx.ARB = [h4(2) for _ in range(2)]
            cx.AKD = [h4(2) for _ in range(2)]
            cx.ARKD = [h4(2) for _ in range(2)]
            cx.bP = [[Buf(), Buf()] for _ in range(2)]
            cx.bARB = [[Buf(), Buf()] for _ in range(2)]
            cx.bAKD = [[Buf(), Buf()] for _ in range(2)]
            cx.bARKD = [[Buf(), Buf()] for _ in range(2)]
            cx.Xn = A.alloc(512, BF16)
            cx.bXn = Buf()
            cx.U = A.alloc(512, BF16)
            cx.bU = Buf()
            cx.ybuf = [A.alloc(512, F32) for _ in range(2)]
            cx.bY = [Buf(), Buf()]
            cx.S32 = [A.alloc(512, F32) for _ in range(2)]
            cx.Sbf = [A.alloc(512, BF16) for _ in range(2)]
            cx.bS32 = [Buf(), Buf()]
            cx.bSbf = [Buf(), Buf()]
            cx.t1 = A.alloc(512, F32)
            cx.bt1 = Buf()
            cx.cur = 0
            ctxs.append(cx)

        ident4 = cb[:, 0:128].rearrange("p (o t) -> p o t", o=1).to_broadcast([128, 4, 128])

        def m4(m):
            return m.rearrange("p (o t) -> p o t", o=1).to_broadcast([128, 4, 128])

        def ps4(ps):
            return ps[:].rearrange("p (h t) -> p h t", h=4)

        def chunk_of(cx, it):
            return it if cx.d == 0 else NCH - 1 - it

        def load(cx, it):
            c = chunk_of(cx, it)
            i3 = it % 3
            d = cx.d
            dma(cx.fm[i3][0:64].rearrange("p h a t -> p (h a t)"), fmS[d, c], (), [cx.b_in[i3]])
            dma(cx.tm[i3], tmS[d, c * 128:(c + 1) * 128, :], (), [cx.b_in[i3]])
            dma(cx.v[i3], vS[c * 128:(c + 1) * 128, :], (), [cx.b_in[i3]])

        def prod4(g, lf, rf, rbufs):
            ps, pb = nextps()
            for i in range(4):
                h = g * 4 + i
                mm(ps[:, i * 128:(i + 1) * 128], lf(h), rf(h), True, True, rbufs, [pb])
            return ps, pb

        def local(cx, it, g):
            d = cx.d
            i3 = it % 3
            par = it % 2
            fm_ = cx.fm[i3]
            b_in = cx.b_in[i3]
            mS, mSt, mR = (SU, SL, UI) if d == 0 else (SL, SU, LI)
            KK = lambda h: fm_[0:64, h, 0, :]
            BH = lambda h: fm_[0:64, h, 1, :]
            KD = lambda h: fm_[0:64, h, 2, :]
            RT = lambda h: fm_[0:64, h, 3, :]
            cn = 0
            N, NT_ = cx.N[g], cx.NT[g]
            bN, bNT = cx.bN[g], cx.bNT[g]
            P, bP = cx.P[par][g], cx.bP[par][g]
            ps, pb = prod4(g, BH, KK, [b_in])
            tt("dve", N[cn], ps4(ps), m4(mS), ALU.mult, [pb, b_c], [bN[cn]])
            yield
            ps, pb = prod4(g, KK, BH, [b_in])
            tt("dve", NT_[cn], ps4(ps), m4(mSt), ALU.mult, [pb, b_c], [bNT[cn]])
            tt("pool", P, ident4, N[cn], ALU.subtract, [b_c, bN[cn]], [bP])
            yield
            for lvl in range(1, 7):
                nn = 1 - cn
                if lvl < 6:
                    ps, pb = prod4(g, lambda h: NT_[cn][:, h % 4, :], lambda h: N[cn][:, h % 4, :], [bN[cn], bNT[cn]])
                    cp("act", N[nn], ps4(ps), [pb], [bN[nn]])
                ps, pb = prod4(g, lambda h: N[cn][:, h % 4, :], lambda h: NT_[cn][:, h % 4, :], [bN[cn], bNT[cn]])
                cp("act", NT_[nn], ps4(ps), [pb], [bNT[nn]])
                yield
                ps, pb = prod4(g, lambda h: NT_[nn][:, h % 4, :], lambda h: P[:, h % 4, :], [bNT[nn], bP])
                tt("dve", P, ps4(ps), P, ALU.add, [pb, bP], [bP])
                cn = nn
                yield
                if lvl == 1:
                    ps, pb = prod4(g, BH, RT, [b_in])
                    tt("dve", cx.ARB[par][g], ps4(ps), m4(mR), ALU.mult, [pb, b_c], [cx.bARB[par][g]])
                    yield
                elif lvl == 2:
                    ps, pb = prod4(g, KD, KK, [b_in])
                    tt("dve", cx.AKD[par][g], ps4(ps), m4(mS), ALU.mult, [pb, b_c], [cx.bAKD[par][g]])
                    yield
                elif lvl == 3:
                    ps, pb = prod4(g, KD, RT, [b_in])
                    tt("dve", cx.ARKD[par][g], ps4(ps), m4(mR), ALU.mult, [pb, b_c], [cx.bARKD[par][g]])
                    yield

        def chain(cx, it):
            d = cx.d
            c = chunk_of(cx, it)
            i3 = it % 3
            par = it % 2
            fm_, tm_, v_ = cx.fm[i3], cx.tm[i3], cx.v[i3]
            b_in = cx.b_in[i3]
            KK = lambda h: fm_[0:64, h, 0, :]
            RT = lambda h: fm_[0:64, h, 3, :]
            Vh = lambda h: v_[:, h * 64:(h + 1) * 64]
            KDs = lambda h: tm_[:, h * 64:(h + 1) * 64]
            Bs = lambda h: tm_[:, 512 + h * 64:512 + (h + 1) * 64]
            P, bP = cx.P[par], cx.bP[par]
            ARB, bARB = cx.ARB[par], cx.bARB[par]
            AKD, bAKD = cx.AKD[par], cx.bAKD[par]
            ARKD, bARKD = cx.ARKD[par], cx.bARKD[par]
            cur = cx.cur
            nxt = 1 - cur
            S32, Sbf = cx.S32[cur], cx.Sbf[cur]
            bS32, bSbf = cx.bS32[cur], cx.bSbf[cur]
            first = (c % CPS == 0) if d == 0 else ((c + 1) % CPS == 0)
            if first:
                b = c // CPS if d == 0 else (c + 1) // CPS
                halo_fix("pool", S32[0:64, :], b, [bS32], [bS32])
                halo_fix("pool", Sbf[0:64, :], b, [bSbf], [bSbf])
            S32v = S32[0:64, :].rearrange("p (f a v) -> p a f v", f=4, a=2)
            t1v = cx.t1[0:64, :].rearrange("p (f a v) -> p a f v", f=4, a=2)
            gCv = gall[d][0:64, c].rearrange("p a (f o) -> p a f o", o=1).to_broadcast([64, 2, 4, 64])
            tt("pool", t1v, S32v, gCv, ALU.mult, [bS32, b_gall], [cx.bt1])
            ps, pb = nextps()
            for h in range(8):
                g = h // 4
                mm(ps[:, h * 64:(h + 1) * 64], KK(h), Sbf[0:64, h * 64:(h + 1) * 64], True, False, [b_in, bSbf], [pb])
                mm(ps[:, h * 64:(h + 1) * 64], AKD[g][:, h % 4, :], Vh(h), False, True, [bAKD[g], b_in], [pb])
            amul(cx.Xn, ps[:], -1.0, [pb], [cx.bXn])
            yield
            ps, pb = nextps()
            for h in range(8):
                g = h // 4
                mm(ps[:, h * 64:(h + 1) * 64], P[g][:, h % 4, :], cx.Xn[:, h * 64:(h + 1) * 64], True, True,
                   [bP[g], cx.bXn], [pb])
            cp("dve", cx.U, ps[:], [pb], [cx.bU])
            yield
            ps, pb = nextps()
            for h in range(8):
                o = ps[0:64, h * 64:(h + 1) * 64]
                mm(o, KDs(h), Vh(h), True, False, [b_in], [pb])
                mm(o, Bs(h), cx.U[:, h * 64:(h + 1) * 64], False, True, [b_in, cx.bU], [pb])
            tt("dve", cx.Sbf[nxt][0:64, :], ps[0:64, :], cx.t1[0:64, :], ALU.add, [pb, cx.bt1], [cx.bSbf[nxt]])
            tt("dve", cx.S32[nxt][0:64, :], ps[0:64, :], cx.t1[0:64, :], ALU.add, [pb, cx.bt1], [cx.bS32[nxt]])
            cx.cur = nxt
            yield
            ps, pb = nextps()
            for h in range(8):
                g = h // 4
                o = ps[:, h * 64:(h + 1) * 64]
                mm(o, RT(h), Sbf[0:64, h * 64:(h + 1) * 64], True, False, [b_in, bSbf], [pb])
                mm(o, ARKD[g][:, h % 4, :], Vh(h), False, False, [bARKD[g], b_in], [pb])
                mm(o, ARB[g][:, h % 4, :], cx.U[:, h * 64:(h + 1) * 64], False, True, [bARB[g], cx.bU], [pb])
            yb_, bY = cx.ybuf[par], cx.bY[par]
            cp("act", yb_, ps[:], [pb], [bY])
            dma(yS[d, c * 128:(c + 1) * 128, :], yb_, [bY], ())
            yield

        for cx in ctxs:
            load(cx, 0)
        for it in range(NCH + 1):
            gens = []
            if it >= 1:
                gens += [chain(ctxs[0], it - 1), chain(ctxs[1], it - 1)]
            if it < NCH:
                if it + 1 < NCH:
                    for cx in ctxs:
                        load(cx, it + 1)
                for cx in ctxs:
                    for g in range(2):
                        gens.append(local(cx, it, g))
            while gens:
                alive = []
                for gen in gens:
                    try:
                        next(gen)
                        alive.append(gen)
                    except StopIteration:
                        pass
                gens = alive
        S.barrier()

    def pass_P3a(l, xin):
        A.off = 0
        wa = A.alloc(4 * D, BF16).rearrange("p (k n) -> p k n", k=4)
        wb = A.alloc(4 * D, BF16).rearrange("p (k n) -> p k n", k=4)
        wo = A.alloc(8 * D, BF16).rearrange("p (k n) -> p k n", k=8)
        g2 = A.alloc(512, BF16)
        gnw = A.alloc(512, F32)
        gnb = A.alloc(512, F32)
        stg = [A.alloc(D, F32) for _ in range(2)]
        bst = [Buf(), Buf()]
        b_w = Buf()
        for kc in range(4):
            load_weight(wa[:, kc, :], w_a[l, kc * 128:(kc + 1) * 128, :], stg, bst, b_w)
            load_weight(wb[:, kc, :], w_b[l, kc * 128:(kc + 1) * 128, :], stg, bst, b_w)
        for kc in range(8):
            load_weight(wo[:, kc, :], w_out[l, kc * 128:(kc + 1) * 128, :], stg, bst, b_w)
        load_weight(g2, g2_d[l], [s[:, 0:512] for s in stg], bst, b_w)
        dma(gnw, gnwb_d[l][:, 0:512], (), [b_w])
        dma(gnb, gnwb_d[l][:, 512:1024], (), [b_w])

        def a3(n, c, dt):
            return A.alloc(c * n, dt).rearrange("p (c t) -> p c t", c=c)

        q_t = a3(514, 4, BF16)
        bg_t = a3(512, 4, BF16)
        sgd_t = A.alloc(512, BF16)
        sgc_t = a3(512, 8, BF16)
        sgr_t = a3(512, 8, BF16)
        yf = [A.alloc(512, F32) for _ in range(2)]
        ybk = [A.alloc(512, F32) for _ in range(2)]
        v_t = [A.alloc(512, BF16) for _ in range(2)]
        rk_t = [A.alloc(8, F32) for _ in range(2)]
        class TS:
            pass

        tsets = []
        for _i in range(2):
            X = TS()
            X.y32, X.tmp, X.tmp2 = A.alloc(512, F32), A.alloc(512, F32), A.alloc(512, F32)
            X.stat, X.o_bf = A.alloc(64, F32), A.alloc(512, BF16)
            X.b_y32, X.b_tmp, X.b_tmp2, X.b_stat, X.b_o = [Buf() for _ in range(5)]
            tsets.append(X)
        oT = a3(512, 4, BF16)
        cqs = [A.alloc(512, F32) for _ in range(2)]
        b_cqs = [Buf(), Buf()]
        ca = a3(512, 4, BF16)
        mg1 = a3(512, 8, F32)
        x_t = mg1
        merged = a3(512, 8, BF16)
        m32 = a3(512, 8, F32)
        sqm = a3(512, 8, BF16)
        rtmp = A.alloc(512, F32)
        rstd = A.alloc(512, F32)
        x1_t = m32
        (b_q, b_bg, b_sgd, b_sgc, b_sgr, b_x_unused, b_y32, b_tmp, b_tmp2, b_stat, b_o, b_oT, b_cq, b_ca, b_mg1, b_mer,
         b_m32, b_sqm, b_rt, b_rs, b_x1) = [Buf() for _ in range(21)]
        b_x1 = b_m32
        b_tb = [Buf(), Buf()]
        b_x = b_mg1

        def fm(ap):
            return ap.rearrange("(c p) t -> p c t", p=128)

        def h8(ap):
            return ap.rearrange("p (h v) -> p h v", h=8)

        def treduce(out, in_, r, w):
            S.op("dve", lambda e: e.tensor_reduce(out=out, in_=in_, axis=AX.X, op=ALU.add), r, w)

        for ti in range(NT // 512):
            t0 = ti * 512
            dma(q_t, fm(qS)[:, :, t0:t0 + 514], (), [b_q])
            dma(bg_t, fm(bgS)[:, :, t0:t0 + 512], (), [b_bg])
            dma(sgd_t, sgdS[:, t0:t0 + 512], (), [b_sgd])
            dma(sgc_t, fm(sgcS)[:, :, t0:t0 + 512], (), [b_sgc])
            dma(sgr_t, fm(sgrS)[:, :, t0:t0 + 512], (), [b_sgr])
            import os
            P3LVL = int(os.environ.get("P3LVL", "20"))
            if P3LVL < 2:
                continue
            if t0 % SEG == 0:
                halo_fix("dve", q_t[:, :, 0:1], t0 // SEG, [b_q], [b_q])
            if (t0 + 512) % SEG == 0:
                halo_fix("dve", q_t[:, :, 513:514], (t0 + 512) // SEG, [b_q], [b_q])
            for tb in range(4):
                if P3LVL < 3:
                    continue
                pp = tb % 2
                r0 = t0 + tb * 128
                dma(yf[pp], yS[0, r0:r0 + 128, :], (), [b_tb[pp]])
                dma(ybk[pp], yS[1, r0:r0 + 128, :], (), [b_tb[pp]])
                dma(v_t[pp], vS[r0:r0 + 128, :], (), [b_tb[pp]])
                dma(rk_t[pp], rkS[r0:r0 + 128, :], (), [b_tb[pp]])
                bt = b_tb[pp]
                X = tsets[pp]
                y32, tmp, tmp2, stat, o_bf = X.y32, X.tmp, X.tmp2, X.stat, X.o_bf
                b_y32, b_tmp, b_tmp2, b_stat, b_o = X.b_y32, X.b_tmp, X.b_tmp2, X.b_stat, X.b_o

                def st8(i, stat=stat):
                    return stat[:, i * 8:(i + 1) * 8]

                def bc8(i, stat=stat):
                    return stat[:, i * 8:(i + 1) * 8].rearrange("p (h o) -> p h o", o=1).to_broadcast([128, 8, 64])

                tt("pool", y32, yf[pp], ybk[pp], ALU.add, [bt], [b_y32])
                treduce(st8(0), h8(y32), [b_y32], [b_stat])
                act(tmp, y32, AF.Square, [b_y32], [b_tmp])
                treduce(st8(1), h8(tmp), [b_tmp], [b_stat])
                ts("dve", st8(0), st8(0), 1.0 / 64, None, ALU.mult, None, [b_stat], [b_stat])
                tt("dve", st8(2), st8(0), st8(0), ALU.mult, [b_stat], [b_stat])
                stt(st8(3), st8(1), 1.0 / 64, st8(2), ALU.mult, ALU.subtract, [b_stat], [b_stat])
                act(st8(4), st8(3), AF.Ln, [b_stat, b_c], [b_stat], bias=cf[:, C_GNEPS:C_GNEPS + 1], scale=1.0)
                act(st8(5), st8(4), AF.Exp, [b_stat], [b_stat], scale=-0.5)
                if P3LVL < 4:
                    continue
                tt("dve", h8(tmp), h8(y32), bc8(0), ALU.subtract, [b_y32, b_stat], [b_tmp])
                tt("dve", h8(tmp2), h8(tmp), bc8(5), ALU.mult, [b_tmp, b_stat], [b_tmp2])
                tt("dve", tmp, tmp2, gnw, ALU.mult, [b_tmp2, b_w], [b_tmp])
                tt("dve", tmp2, tmp, gnb, ALU.add, [b_tmp, b_w], [b_tmp2])
                rkb = rk_t[pp].rearrange("p (h o) -> p h o", o=1).to_broadcast([128, 8, 64])
                tt("dve", h8(tmp), h8(v_t[pp]), rkb, ALU.mult, [bt], [b_tmp])
                tt("dve", y32, tmp2, tmp, ALU.add, [b_tmp2, b_tmp], [b_y32])
                if P3LVL < 5:
                    continue
                ps, pb = nextps()
                mm(ps[:], sgd_t[:, tb * 128:(tb + 1) * 128], g2, True, True, [b_sgd, b_w], [pb])
                tt("dve", o_bf, ps[:], y32, ALU.mult, [pb, b_y32], [b_o])
                ps, pb = nextps()
                psT = ps[:].bitcast(BF16)
                for fc in range(4):
                    transpose(psT[:, fc * 128:(fc + 1) * 128], o_bf[:, fc * 128:(fc + 1) * 128], [b_o], [pb])
                cp("act", oT[:, :, tb * 128:(tb + 1) * 128], psT[:, 0:512].rearrange("p (c t) -> p c t", c=4),
                   [pb], [b_oT])
            if P3LVL < 6:
                continue
            for fc in range(4):
                cq, b_cq = cqs[fc % 2], b_cqs[fc % 2]
                cw = lambda j: vec[:, V_CONVW + j * 4 + fc:V_CONVW + j * 4 + fc + 1]
                ts("dve", cq, q_t[:, fc, 1:513], cw(1), vec[:, V_CONVB + fc:V_CONVB + fc + 1], ALU.mult, ALU.add,
                   [b_q, b_vec], [b_cq])
                stt(cq, q_t[:, fc, 0:512], cw(0), cq, ALU.mult, ALU.add, [b_q, b_vec, b_cq], [b_cq])
                stt(cq, q_t[:, fc, 2:514], cw(2), cq, ALU.mult, ALU.add, [b_q, b_vec, b_cq], [b_cq])
                tt("dve", ca[:, fc, :], cq, bg_t[:, fc, :], ALU.mult, [b_cq, b_bg], [b_ca])
            if P3LVL < 7:
                continue
            for mc in range(8):
                ps, pb = nextps()
                for kc in range(4):
                    mm(ps[:], wa[:, kc, mc * 128:(mc + 1) * 128], ca[:, kc, :], kc == 0, kc == 3, [b_w, b_ca], [pb])
                tt("dve", mg1[:, mc, :], ps[:], sgc_t[:, mc, :], ALU.mult, [pb, b_sgc], [b_mg1])
            if P3LVL < 8:
                continue
            for mc in range(8):
                ps, pb = nextps()
                for kc in range(4):
                    mm(ps[:], wb[:, kc, mc * 128:(mc + 1) * 128], oT[:, kc, :], kc == 0, kc == 3, [b_w, b_oT], [pb])
                X = tsets[mc % 2]
                tt("dve", X.tmp, ps[:], sgr_t[:, mc, :], ALU.mult, [pb, b_sgr], [X.b_tmp])
                tt("dve", merged[:, mc, :], X.tmp, mg1[:, mc, :], ALU.add, [X.b_tmp, b_mg1], [b_mer])
            dma(x_t, fm(xin)[:, :, t0:t0 + 512], (), [b_x])
            if P3LVL < 9:
                continue
            for mc in range(8):
                ps, pb = nextps()
                for kc in range(8):
                    mm(ps[:], wo[:, kc, mc * 128:(mc + 1) * 128], merged[:, kc, :], kc == 0, kc == 7, [b_w, b_mer], [pb])
                cp("dve", m32[:, mc, :], ps[:], [pb], [b_m32])
                act(sqm[:, mc, :], m32[:, mc, :], AF.Square, [b_m32], [b_sqm])
            if P3LVL < 10:
                continue
            rms_rstd(lambda c: sqm[:, c, :], 8, 512, rtmp, rstd, b_sqm, b_rt, b_rs)
            if P3LVL < 11:
                continue
            for mc in range(8):
                X = tsets[mc % 2]
                tt("dve", X.tmp, m32[:, mc, :], rstd, ALU.mult, [b_m32, b_rs], [X.b_tmp])
                stt(x1_t[:, mc, :], X.tmp, vec[:, V_NMPOST + mc:V_NMPOST + mc + 1], x_t[:, mc, :], ALU.mult, ALU.add,
                    [X.b_tmp, b_vec, b_x], [b_x1])
            if P3LVL < 12:
                continue
            dma(fm(x1S)[:, :, 1 + t0:1 + t0 + 512], x1_t, [b_x1], ())
        S.barrier()

    def pass_P3b(l, xout):
        A.off = 0
        wd = A.alloc(22 * D, BF16).rearrange("p (k n) -> p k n", k=22)
        wpg = A.alloc(8 * D, BF16).rearrange("p (k n) -> p k n", k=8)
        wpl = A.alloc(2 * D, BF16).rearrange("p (k n) -> p k n", k=2)
        mark = A.off
        stg = [A.alloc(D, F32) for _ in range(2)]
        bst = [Buf(), Buf()]
        b_w = Buf()
        for kc in range(22):
            load_weight(wd[:, kc, :], w_down[l, kc * 128:(kc + 1) * 128, :], stg, bst, b_w)
        for kc in range(8):
            load_weight(wpg[:, kc, :], w_pg[l, kc * 128:(kc + 1) * 128, :], stg, bst, b_w)
        for kc in range(2):
            load_weight(wpl[:, kc, :], w_ple[l, kc * 128:(kc + 1) * 128, :], stg, bst, b_w)
        S.barrier()
        A.off = mark
        WM = 412

        def a3(n, c, dt):
            return A.alloc(c * n, dt).rearrange("p (c t) -> p c t", c=c)

        x1w = a3(WM, 8, F32)
        sq = a3(WM, 8, BF16)
        u = a3(WM, 8, BF16)
        rtmp = A.alloc(WM, F32)
        rstd = A.alloc(WM, F32)
        p_t = a3(WM, 2, F32)
        p_b = a3(WM, 2, BF16)
        wj = [A.alloc(8 * 2 * 128, BF16).rearrange("p (k g m) -> p k g m", k=8, g=2) for _ in range(3)]
        b_wj = [Buf() for _ in range(3)]
        class TS:
            pass

        jsets = []
        for _i in range(2):
            X = TS()
            for nm in ("cg", "cv", "g1", "g2_", "g3", "gate", "tmp"):
                setattr(X, nm, A.alloc(WM, F32))
                setattr(X, "b_" + nm, Buf())
            jsets.append(X)
        actb = a3(WM, 22, BF16)
        m32 = a3(WM, 8, F32)
        sqm = sq
        x2_t = a3(WM, 8, F32)
        x2b = u
        (b_x1, b_sq, b_u, b_rt, b_rs, b_p, b_pb, b_cg_u, b_cv_u, b_g1_u, b_g2_u, b_g3_u, b_act, b_m32, b_sqm, b_x2, b_x2b,
         b_gate_u, b_tmp_u) = [Buf() for _ in range(19)]
        b_sqm = b_sq
        b_x2b = b_u

        def fm(ap):
            return ap.rearrange("(c p) t -> p c t", p=128)

        tiles = []
        for sgi in range(NSEG):
            a = sgi * SEG
            npc = (SEG + 409) // 410
            base = SEG // npc
            rem = SEG - base * npc
            for i in range(npc):
                n = base + (1 if i < rem else 0)
                tiles.append((a, n))
                a += n
        wctr = 0
        for (a0, n) in tiles:
            W = n + 2
            dma(x1w[:, :, 0:W], fm(x1S)[:, :, a0:a0 + W], (), [b_x1])
            dma(p_t[:, :, 0:n], pT[l].rearrange("(c p) t -> p c t", p=128)[:, :, a0:a0 + n], (), [b_p])
            cp("pool", p_b[:, :, 0:n], p_t[:, :, 0:n], [b_p], [b_pb])
            act(sq[:, :, 0:W], x1w[:, :, 0:W], AF.Square, [b_x1], [b_sq])
            rms_rstd(lambda c: sq[:, c, 0:W], 8, W, rtmp[:, 0:W], rstd[:, 0:W], b_sq, b_rt, b_rs)
            for c in range(8):
                stt(u[:, c, 0:W], x1w[:, c, 0:W], vec[:, V_NFP + c:V_NFP + c + 1], rstd[:, 0:W], ALU.mult, ALU.mult,
                    [b_x1, b_rs, b_vec], [b_u])
            if a0 % SEG == 0:
                halo_fix("dve", u[:, :, 0:1], a0 // SEG, [b_u], [b_u])
            if (a0 + n) % SEG == 0:
                halo_fix("dve", u[:, :, W - 1:W], (a0 + n) // SEG, [b_u], [b_u])
            for j in range(22):
                X = jsets[j % 2]
                cg, cv, g1, g2_, g3 = X.cg, X.cv, X.g1, X.g2_, X.g3
                b_cg, b_cv, b_g1, b_g2, b_g3 = X.b_cg, X.b_cv, X.b_g1, X.b_g2_, X.b_g3
                wi = wctr % 3
                wctr += 1
                dma(wj[wi].rearrange("p k g m -> p (k g m)"), wupS[l, j], (), [b_wj[wi]])
                res = []
                for gv in range(2):
                    ps, pb = nextps()
                    for kc in range(8):
                        mm(ps[:, 0:W], wj[wi][:, kc, gv, :], u[:, kc, 0:W], kc == 0, kc == 7, [b_wj[wi], b_u], [pb])
                    res.append((ps, pb))
                for gv in range(2):
                    ps, pb = res[gv]
                    c_ = gv * 22 + j
                    dst, bd = (cg, b_cg) if gv == 0 else (cv, b_cv)
                    fw = lambda jj: vec[:, V_FCW + jj * 44 + c_:V_FCW + jj * 44 + c_ + 1]
                    act(dst[:, 0:n], ps[:, 1:W - 1], AF.Identity, [pb, b_vec], [bd],
                        bias=vec[:, V_FCB + c_:V_FCB + c_ + 1], scale=fw(1))
                    stt(dst[:, 0:n], ps[:, 0:n], fw(0), dst[:, 0:n], ALU.mult, ALU.add, [pb, b_vec, bd], [bd])
                    stt(dst[:, 0:n], ps[:, 2:W], fw(2), dst[:, 0:n], ALU.mult, ALU.add, [pb, b_vec, bd], [bd])
                act(g1[:, 0:n], cg[:, 0:n], AF.Square, [b_cg], [b_g1])
                ts("pool", g1[:, 0:n], g1[:, 0:n], 0.044715, 1.0, ALU.mult, ALU.add, [b_g1], [b_g1])
                tt("dve", g2_[:, 0:n], g1[:, 0:n], cg[:, 0:n], ALU.mult, [b_g1, b_cg], [b_g2])
                act(g3[:, 0:n], g2_[:, 0:n], AF.Sigmoid, [b_g2], [b_g3], scale=GELU_C)
                tt("dve", g1[:, 0:n], cg[:, 0:n], cv[:, 0:n], ALU.mult, [b_cg, b_cv, b_g2], [b_g1])
                tt("dve", actb[:, j, 0:n], g1[:, 0:n], g3[:, 0:n], ALU.mult, [b_g1, b_g3], [b_act])
            for mc in range(8):
                ps, pb = nextps()
                for kc in range(22):
                    mm(ps[:, 0:n], wd[:, kc, mc * 128:(mc + 1) * 128], actb[:, kc, 0:n], kc == 0, kc == 21,
                       [b_w, b_act], [pb])
                cp("dve", m32[:, mc, 0:n], ps[:, 0:n], [pb], [b_m32])
                act(sqm[:, mc, 0:n], m32[:, mc, 0:n], AF.Square, [b_m32], [b_sqm])
            rms_rstd(lambda c: sqm[:, c, 0:n], 8, n, rtmp[:, 0:n], rstd[:, 0:n], b_sqm, b_rt, b_rs)
            for mc in range(8):
                tmp, b_tmp = jsets[mc % 2].tmp, jsets[mc % 2].b_tmp
                tt("dve", tmp[:, 0:n], m32[:, mc, 0:n], rstd[:, 0:n], ALU.mult, [b_m32, b_rs], [b_tmp])
                stt(x2_t[:, mc, 0:n], tmp[:, 0:n], vec[:, V_NFPOST + mc:V_NFPOST + mc + 1], x1w[:, mc, 1:W - 1],
                    ALU.mult, ALU.add, [b_tmp, b_vec, b_x1], [b_x2])
                cp("act", x2b[:, mc, 0:n], x2_t[:, mc, 0:n], [b_x2], [b_x2b])
            for mc in range(8):
                ps, pb = nextps()
                for kc in range(8):
                    mm(ps[:, 0:n], wpg[:, kc, mc * 128:(mc + 1) * 128], x2b[:, kc, 0:n], kc == 0, kc == 7,
                       [b_w, b_x2b], [pb])
                gate, b_gate = jsets[mc % 2].gate, jsets[mc % 2].b_gate
                act(gate[:, 0:n], ps[:, 0:n], AF.Sigmoid, [pb], [b_gate])
                ps2, pb2 = nextps()
                for kc in range(2):
                    mm(ps2[:, 0:n], wpl[:, kc, mc * 128:(mc + 1) * 128], p_b[:, kc, 0:n], kc == 0, kc == 1,
                       [b_w, b_pb], [pb2])
                tt("dve", m32[:, mc, 0:n], ps2[:, 0:n], gate[:, 0:n], ALU.mult, [pb2, b_gate], [b_m32])
                act(sqm[:, mc, 0:n], m32[:, mc, 0:n], AF.Square, [b_m32], [b_sqm])
            rms_rstd(lambda c: sqm[:, c, 0:n], 8, n, rtmp[:, 0:n], rstd[:, 0:n], b_sqm, b_rt, b_rs)
            for mc in range(8):
                tmp, b_tmp = jsets[mc % 2].tmp, jsets[mc % 2].b_tmp
                tt("dve", tmp[:, 0:n], m32[:, mc, 0:n], rstd[:, 0:n], ALU.mult, [b_m32, b_rs], [b_tmp])
                stt(x1w[:, mc, 0:n], tmp[:, 0:n], vec[:, V_NPLE + mc:V_NPLE + mc + 1], x2_t[:, mc, 0:n],
                    ALU.mult, ALU.add, [b_tmp, b_vec, b_x2], [b_x1])
            dma(fm(xout)[:, :, a0:a0 + n], x1w[:, :, 0:n], [b_x1], ())
        S.barrier()

    pass_W0()
    for l in range(L):
        xin = xT if l == 0 else xL
        xout = yT if l == L - 1 else xL
        if upto >= 1:
            pass_P1(l, xin)
        if upto >= 2:
            pass_P2pre(l)
        if upto >= 3:
            pass_P2scan(l)
        if upto >= 4:
            pass_P3a(l, xin)
        if upto >= 5:
            pass_P3b(l, xout)
    S.finish()
    S.emit(nc)
    st.close()
    return nc


def make_consts():
    c = np.zeros((128, NCON), np.float32)
    i = np.arange(128)
    c[:, C_IDENT:C_IDENT + 128] = np.eye(128)
    c[:, C_SU:C_SU + 128] = (i[:, None] < i[None, :])
    c[:, C_SL:C_SL + 128] = (i[:, None] > i[None, :])
    c[:, C_UI:C_UI + 128] = (i[:, None] <= i[None, :])
    c[:, C_LI:C_LI + 128] = (i[:, None] >= i[None, :])
    c[:, C_BLK:C_BLK + 128] = ((i[:, None] // 64) == (i[None, :] // 64))
    c[:, C_ONES:C_ONES + 128] = 1.0
    for fc in range(4):
        for h in range(8):
            c[:, C_HSEL + fc * 8 + h] = (h == 2 * fc + i // 64)
    r = np.ones(512, np.float32)
    r[::128] = 0.0
    c[:, C_RESET:C_RESET + 512] = r[None, :]
    c[:, C_EPS] = NORM_EPS
    c[:, C_GNEPS] = GN_EPS
    return c


def make_vecs(inp, L):
    v = np.zeros((L, 128, NV), np.float32)

    def fmaj(a):
        return np.ascontiguousarray(a.reshape(-1, 128).T)

    for l in range(L):
        v[l, :, V_NMP:V_NMP + 8] = fmaj(inp["norm_mix_pre"][l])
        v[l, :, V_NMPOST:V_NMPOST + 8] = fmaj(inp["norm_mix_post"][l])
        v[l, :, V_NFP:V_NFP + 8] = fmaj(inp["norm_ffn_pre"][l])
        v[l, :, V_NFPOST:V_NFPOST + 8] = fmaj(inp["norm_ffn_post"][l])
        v[l, :, V_NPLE:V_NPLE + 8] = fmaj(inp["norm_ple_post"][l])
        for j in range(3):
            v[l, :, V_CONVW + j * 4:V_CONVW + j * 4 + 4] = fmaj(inp["conv_w"][l, j])
            v[l, :, V_FCW + j * 44:V_FCW + j * 44 + 44] = fmaj(inp["ffn_conv_w"][l, j])
        v[l, :, V_CONVB:V_CONVB + 4] = fmaj(inp["conv_b"][l])
        v[l, :, V_FCB:V_FCB + 44] = fmaj(inp["ffn_conv_b"][l])
        v[l, :, V_KK:V_KK + 4] = fmaj(inp["k_k"][l])
        v[l, :, V_KA:V_KA + 4] = fmaj(inp["k_a"][l])
        v[l, :, V_RK:V_RK + 4] = fmaj(inp["r_k"][l].reshape(-1))
        for d in range(2):
            v[l, :, V_W0 + d * 4:V_W0 + d * 4 + 4] = fmaj(inp["decay_w0"][l, d])
            v[l, :, V_A0 + d * 4:V_A0 + d * 4 + 4] = fmaj(inp["iclr_a0"][l, d])
            v[l, 0:64, V_MU + d * 2] = inp["shift_mu"][l, d, 0:64]
            v[l, 0:64, V_MU + d * 2 + 1] = inp["shift_mu"][l, d, 64:128]
    return v


_PROG_CACHE = {}


def run_cores(seqs_per_core, carry_per_core, inp, NSEG, SEG, L, debug=False, upto=99):
    key = (NSEG, SEG, L, debug, upto)
    if key not in _PROG_CACHE:
        _PROG_CACHE[key] = build_program(NSEG, SEG, L, debug, upto)
    nc = _PROG_CACHE[key]
    consts = make_consts()
    vecs = make_vecs(inp, L)
    gnwb = np.zeros((L, 128, 1024), np.float32)
    for l in range(L):
        gnwb[l, :, 0:512] = inp["gn_w"][l][None, :]
        gnwb[l, :, 512:1024] = inp["gn_b"][l][None, :]
    shared = {
        "consts": consts, "vecs": vecs, "gnwb": gnwb,
        "w_in": inp["w_in"], "w_branch_a": inp["w_branch_a"], "w_branch_b": inp["w_branch_b"],
        "w_out": inp["w_out"], "w_up": inp["w_up"], "w_down": inp["w_down"], "w_ple": inp["w_ple"],
        "w_ple_gate": inp["w_ple_gate"], "decay_w2": inp["decay_w2"], "iclr_a2": inp["iclr_a2"],
        "gate_g2": inp["gate_g2"],
    }
    shared = {k: np.ascontiguousarray(np.asarray(v, np.float32)) for k, v in shared.items()}
    in_maps = []
    for (x, p), carry in zip(seqs_per_core, carry_per_core):
        m = dict(shared)
        m["xT"] = np.ascontiguousarray(x.T)
        m["pT"] = np.ascontiguousarray(np.transpose(p, (0, 2, 1)))
        mk = np.zeros((128, NSEG + 1), np.float32)
        mk[:, :] = np.asarray(carry, np.float32)[None, :]
        m["masks"] = mk
        in_maps.append(m)
    res = run_bass_kernel_spmd(nc, in_maps, core_ids=list(range(len(in_maps))))
    return res.results


def kernel(**inp):
    inp = {k: np.asarray(v) for k, v in inp.items()}
    xp, xs = inp["x_prompt"], inp["x_sample"]
    pp, psm = inp["p_prompt"], inp["p_sample"]
    L = pp.shape[0]
    SEG, NSEG = 2048, 6
    per_core = []
    carries = []
    plan = []
    for c in range(8):
        if c < 4:
            segs = [("p", c), ("s", 2 * c), ("s", 2 * c + 1)]
            carry = [0, 1, 1, 1, 0, 0, 0]
        else:
            segs = [("s", 8 + 6 * (c - 4) + i) for i in range(6)]
            carry = [0] * 7
        xs_l, ps_l = [], []
        for kind, i in segs:
            if kind == "p":
                xs_l.append(xp[i])
                ps_l.append(pp[:, i])
            else:
                xs_l.append(xs[i])
                ps_l.append(psm[:, i])
        per_core.append((np.concatenate(xs_l, axis=0), np.concatenate(ps_l, axis=1)))
        carries.append(carry)
        plan.append(segs)
    results = run_cores(per_core, carries, inp, NSEG, SEG, L)
    y_p = np.empty(xp.shape, np.float32)
    y_s = np.empty(xs.shape, np.float32)
    for c in range(8):
        y = np.ascontiguousarray(results[c]["yT"].T)
        off = 0
        for kind, i in plan[c]:
            if kind == "p":
                y_p[i] = y[off:off + 8192]
                off += 8192
            else:
                y_s[i] = y[off:off + 2048]
                off += 2048
    return (y_p, y_s)
```

```python
import numpy as np
from contextlib import ExitStack
import concourse.bass as bass
import concourse.mybir as mybir
from concourse.bass_utils import run_bass_kernel_spmd

F32, BF16 = mybir.dt.float32, mybir.dt.bfloat16
AF = mybir.ActivationFunctionType
ALU = mybir.AluOpType
AX = mybir.AxisListType

D = 1024
INC = 5504
DFF = 2816
CDEC = -0.6065306597126334
NORM_EPS = 1e-6
GN_EPS = 64 * 1e-5
GELU_C = 1.5957691216057308

C_IDENT, C_SU, C_SL, C_UI, C_LI, C_BLK, C_ONES, C_HSEL, C_RESET, C_EPS, C_GNEPS = (
    0, 128, 256, 384, 512, 640, 768, 896, 928, 1440, 1441)
NCON = 1442
V_NMP, V_NMPOST, V_NFP, V_NFPOST, V_NPLE = 0, 8, 16, 24, 32
V_CONVW, V_CONVB, V_KK, V_KA, V_RK, V_W0, V_A0 = 40, 52, 56, 60, 64, 68, 76
V_FCW, V_FCB, V_MU, V_1MKA = 84, 216, 260, 264
NV = 268


class Buf:
    __slots__ = ("lw", "rd")

    def __init__(self):
        self.lw = None
        self.rd = []


ENGS = ("pe", "act", "dve", "pool", "sp")


class Sched:
    def __init__(self, ring=8):
        self.prog = {e: [] for e in ENGS}
        self.cnt = {e: 0 for e in ENGS}
        self.known = {e: {} for e in ENGS}
        self.ring = ring
        self.dma_next = {e: 0 for e in ENGS}
        self.dma_cnt = {e: [0] * ring for e in ENGS}

    def _deps(self, eng, reads, writes):
        deps = {}

        def add(p):
            k, v = p
            if deps.get(k, 0) < v:
                deps[k] = v

        for r in reads:
            if r.lw is not None and not (r.lw[0] == eng and eng == "pe"):
                add(r.lw)
        for w in writes:
            if w.lw is not None and w.lw[0] != eng:
                add(w.lw)
            for p in w.rd:
                if p[0] != eng:
                    add(p)
        return deps

    def _filter(self, eng, deps):
        kn = self.known[eng]
        out = []
        for k, v in deps.items():
            if kn.get(k, 0) >= v:
                continue
            kn[k] = v
            out.append((k, v))
        return out

    def _commit(self, me, reads, writes):
        for r in reads:
            r.rd.append(me)
        for w in writes:
            w.lw = me
            w.rd = []

    def op(self, eng, fn, reads=(), writes=()):
        waits = self._filter(eng, self._deps(eng, reads, writes))
        self.cnt[eng] += 1
        self.prog[eng].append((waits, fn, (eng, 1)))
        self._commit((eng, self.cnt[eng]), reads, writes)

    def dma(self, eng, fn, reads=(), writes=()):
        deps = self._deps(eng, reads, writes)
        slot = self.dma_next[eng] % self.ring
        self.dma_next[eng] += 1
        key = ("dma", eng, slot)
        c = self.dma_cnt[eng][slot]
        if c > 0 and deps.get(key, 0) < 16 * c:
            deps[key] = 16 * c
        waits = self._filter(eng, deps)
        self.dma_cnt[eng][slot] = c + 1
        self.prog[eng].append((waits, fn, (key, 16)))
        self._commit((key, 16 * (c + 1)), reads, writes)

    def _all(self):
        deps = {}
        for e in ENGS:
            if e != "sp" and self.cnt[e] > 0:
                deps[e] = self.cnt[e]
            for slot in range(self.ring):
                c = self.dma_cnt[e][slot]
                if c > 0:
                    deps[("dma", e, slot)] = 16 * c
        return deps

    def barrier(self):
        deps = self._all()
        for e in ENGS:
            d = {k: v for k, v in deps.items() if k != e}
            waits = self._filter(e, d)
            if waits:
                self.prog[e].append((waits, None, None))

    def finish(self):
        self.barrier()

    def emit(self, nc):
        keys = [e for e in ENGS if e != "sp" and self.cnt[e] > 0]
        for e in ENGS:
            for slot in range(self.ring):
                if self.dma_cnt[e][slot] > 0:
                    keys.append(("dma", e, slot))
        with ExitStack() as st:
            sems = {}
            for i, k in enumerate(keys):
                sems[k] = st.enter_context(nc.semaphore("s%d" % i))
            block = st.enter_context(nc.Block())

            def run(engname):
                def body(e):
                    for waits, fn, inc in self.prog[engname]:
                        for k, v in waits:
                            e.wait_ge(sems[k], v)
                        if fn is not None:
                            fn(e).then_inc(sems[inc[0]], inc[1])
                return body

            block.sync(run("sp"))
            block.tensor(run("pe"))
            block.scalar(run("act"))
            block.vector(run("dve"))
            block.gpsimd(run("pool"))


class Arena:
    def __init__(self, ap, size):
        self.ap, self.size, self.off = ap, size, 0

    def alloc(self, n, dtype):
        n16 = n * (2 if dtype == F32 else 1)
        start = (self.off + 15) // 16 * 16
        assert start + n16 <= self.size, ("arena overflow", start + n16, self.size)
        v = self.ap[:, start:start + n16]
        if dtype == F32:
            v = v.bitcast(F32)
        self.off = start + n16
        return v


def build_program(NSEG, SEG, DEPTH, debug=False, upto=99):
    NT = NSEG * SEG
    NCH = NT // 128
    CPS = SEG // 128
    assert NT % 512 == 0 and SEG % 128 == 0
    nc = bass.Bass("TRN2", target_bir_lowering=False)
    L = DEPTH

    def din(name, shape, dt=F32):
        return nc.dram_tensor(name, list(shape), dt, kind="ExternalInput").ap()

    def dscr(name, shape, dt):
        kind = "ExternalOutput" if debug else "Internal"
        return nc.dram_tensor(name, list(shape), dt, kind=kind).ap()

    xT = din("xT", [D, NT])
    pT = din("pT", [L, 256, NT])
    masks_d = din("masks", [128, NSEG + 1])
    consts_d = din("consts", [128, NCON])
    vecs_d = din("vecs", [L, 128, NV])
    gnwb_d = din("gnwb", [L, 128, 1024])
    w_in = din("w_in", [L, D, INC])
    w_a = din("w_branch_a", [L, 512, D])
    w_b = din("w_branch_b", [L, 512, D])
    w_out = din("w_out", [L, D, D])
    w_up = din("w_up", [L, D, 2 * DFF])
    w_down = din("w_down", [L, DFF, D])
    w_ple = din("w_ple", [L, 256, D])
    w_pg = din("w_ple_gate", [L, D, D])
    dw2_d = din("decay_w2", [L, 2, 64, 512])
    a2_d = din("iclr_a2", [L, 2, 64, 512])
    g2_d = din("gate_g2", [L, 128, 512])
    yT = nc.dram_tensor("yT", [D, NT], F32, kind="ExternalOutput").ap()

    qS = dscr("qS", [512, NT + 2], BF16)
    bgS = dscr("bgS", [512, NT], BF16)
    rS = dscr("rS", [512, NT], BF16)
    kS = dscr("kS", [512, NT], BF16)
    vS = dscr("vS", [NT, 512], BF16)
    zS = dscr("zS", [256, NT + 2], F32)
    sgdS = dscr("sgdS", [128, NT], BF16)
    sgcS = dscr("sgcS", [D, NT], BF16)
    sgrS = dscr("sgrS", [D, NT], BF16)
    rkS = dscr("rkS", [NT, 8], F32)
    fmS = dscr("fmS", [2, NCH, 64, 8 * 4 * 128], BF16)
    tmS = dscr("tmS", [2, NT, 1024], BF16)
    gcS = dscr("gcS", [2, NCH, 128, 4], F32)
    yS = dscr("yS", [2, NT, 512], F32)
    x1S = dscr("x1S", [D, NT + 2], F32)
    xL = dscr("xL", [D, NT], F32)
    wupS = dscr("wupS", [L, 22, 128, 8 * 2 * 128], BF16)

    S = Sched()
    st = ExitStack()
    ARENA_N = 90 * 1024
    arena_t = st.enter_context(nc.sbuf_tensor("arena", [128, ARENA_N], BF16))
    cf = st.enter_context(nc.sbuf_tensor("cf", [128, NCON], F32))
    cb = st.enter_context(nc.sbuf_tensor("cb", [128, 256], BF16))
    mk = st.enter_context(nc.sbuf_tensor("mk", [128, NSEG + 1], F32))
    vec = st.enter_context(nc.sbuf_tensor("vec", [128, NV], F32))
    psum = [st.enter_context(nc.psum_tensor("ps%d" % i, [128, 512], F32)) for i in range(8)]
    psb = [Buf() for _ in range(8)]
    A = Arena(arena_t, ARENA_N)
    b_c = Buf()
    b_vec = Buf()
    ident_bf = cb[:, 0:128]
    ones_bf = cb[:, 128:256]
    pctr = [0]

    def nextps():
        i = pctr[0] % 8
        pctr[0] += 1
        return psum[i], psb[i]

    def dma(out, in_, r=(), w=()):
        q = "pool" if (len(r) > 0 and len(w) == 0) else "sp"
        S.dma(q, lambda e: e.dma_start(out=out, in_=in_), r, w)

    def mm(out, lhsT, rhs, start, stop, r, w):
        S.op("pe", lambda e: e.matmul(out, lhsT=lhsT, rhs=rhs, start=start, stop=stop), r, w)

    def transpose(out, in_, r, w):
        S.op("pe", lambda e: e.transpose(out, in_, ident_bf), list(r) + [b_c], w)

    def act(out, in_, func, r, w, bias=None, scale=None):
        kw = {}
        if bias is not None:
            kw["bias"] = bias
        if scale is not None:
            kw["scale"] = scale
        S.op("act", lambda e: e.activation(out=out, in_=in_, func=func, **kw), r, w)

    def tt(eng, out, in0, in1, op, r, w):
        S.op(eng, lambda e: e.tensor_tensor(out=out, in0=in0, in1=in1, op=op), r, w)

    def ts(eng, out, in0, s1, s2, op0, op1, r, w):
        if op1 is None:
            S.op(eng, lambda e: e.tensor_scalar(out=out, in0=in0, scalar1=s1, scalar2=None, op0=op0), r, w)
        else:
            S.op(eng, lambda e: e.tensor_scalar(out=out, in0=in0, scalar1=s1, scalar2=s2, op0=op0, op1=op1), r, w)

    def stt(out, in0, scalar, in1, op0, op1, r, w):
        S.op("dve", lambda e: e.scalar_tensor_tensor(out=out, in0=in0, scalar=scalar, in1=in1, op0=op0, op1=op1), r, w)

    def cp(eng, out, in_, r, w):
        if eng == "act":
            S.op("act", lambda e: e.copy(out=out, in_=in_), r, w)
        else:
            S.op(eng, lambda e: e.tensor_copy(out=out, in_=in_), r, w)

    def amul(out, in_, m, r, w):
        S.op("act", lambda e: e.mul(out=out, in_=in_, mul=m), r, w)

    def memset(eng, ap, val, w):
        S.op(eng, lambda e: e.memset(ap, val), (), w)

    def recip(out, in_, r, w):
        S.op("dve", lambda e: e.reciprocal(out=out, in_=in_), r, w)

    cast_rr = [0]

    def load_weight(dst, src, stages, bst, bdst):
        n = dst.shape[-1]
        i = cast_rr[0]
        cast_rr[0] += 1
        sg = stages[i % len(stages)][:, 0:n]
        bs = bst[i % len(stages)]
        dma(sg, src, (), [bs])
        cp(("dve", "act", "pool")[i % 3], dst, sg, [bs], [bdst])

    def halo_fix(eng, ap, b, rbufs, wbufs):
        if b == 0 or b == NSEG:
            memset(eng, ap, 0.0, wbufs)
        else:
            np_ = ap.shape[0]
            ts(eng, ap, ap, mk[0:np_, b:b + 1], None, ALU.mult, None, list(rbufs) + [b_c], wbufs)

    def rms_rstd(sq_chunks, nchunk, W, rstd_tmp, rstd, b_sq, b_tmp, b_rstd):
        ps, pb = nextps()
        for c in range(nchunk):
            mm(ps[:, 0:W], ones_bf, sq_chunks(c), c == 0, c == nchunk - 1, [b_sq, b_c], [pb])
        act(rstd_tmp, ps[:, 0:W], AF.Ln, [pb, b_c], [b_tmp], bias=cf[:, C_EPS:C_EPS + 1], scale=1.0 / D)
        act(rstd, rstd_tmp, AF.Exp, [b_tmp], [b_rstd], scale=-0.5)

    dma(cf[:], consts_d[:, :], (), [b_c])
    dma(mk[:], masks_d[:, :], (), [b_c])
    cp("dve", ident_bf, cf[:, C_IDENT:C_IDENT + 128], [b_c], [b_c])
    cp("dve", ones_bf, cf[:, C_ONES:C_ONES + 128], [b_c], [b_c])
    SU = cf[:, C_SU:C_SU + 128]
    SL = cf[:, C_SL:C_SL + 128]
    UI = cf[:, C_UI:C_UI + 128]
    LI = cf[:, C_LI:C_LI + 128]
    BLK = cf[:, C_BLK:C_BLK + 128]
    RESET = cf[:, C_RESET:C_RESET + 512]

    def pass_W0():
        A.off = 0
        stg = [A.alloc(2 * DFF, F32) for _ in range(2)]
        s16 = [A.alloc(2 * DFF, BF16) for _ in range(2)]
        bs = [Buf(), Buf()]
        b16 = [Buf(), Buf()]
        i = 0
        for l in range(L):
            for kc in range(8):
                s = i % 2
                dma(stg[s], w_up[l, kc * 128:(kc + 1) * 128, :], (), [bs[s]])
                cp(("dve", "act", "pool")[i % 3], s16[s], stg[s], [bs[s]], [b16[s]])
                for gv in range(2):
                    dst = wupS[l].rearrange("j p (k g m) -> p j k g m", k=8, g=2)[:, :, kc, gv, :]
                    src = s16[s][:, gv * DFF:(gv + 1) * DFF].rearrange("p (j m) -> p j m", m=128)
                    dma(dst, src, [b16[s]], ())
                i += 1
        S.barrier()

    def pass_P1(l, xin):
        A.off = 0
        W1 = A.alloc(8 * INC, BF16).rearrange("p (k n) -> p k n", k=8)
        bW1 = Buf()
        mark = A.off
        stg = [A.alloc(INC, F32) for _ in range(2)]
        bst = [Buf(), Buf()]
        dma(vec[:], vecs_d[l], (), [b_vec])
        for kc in range(8):
            load_weight(W1[:, kc, :], w_in[l, kc * 128:(kc + 1) * 128, :], stg, bst, bW1)
        S.barrier()
        A.off = mark
        xt = A.alloc(8 * 512, F32).rearrange("p (c t) -> p c t", c=8)
        sq = A.alloc(8 * 512, BF16).rearrange("p (c t) -> p c t", c=8)
        ub = A.alloc(8 * 512, BF16).rearrange("p (c t) -> p c t", c=8)
        rtmp = A.alloc(512, F32)
        rstd = A.alloc(512, F32)
        hc_s = A.alloc(4 * 512, BF16).rearrange("p (c t) -> p c t", c=4)
        bg_s = A.alloc(4 * 512, BF16).rearrange("p (c t) -> p c t", c=4)
        q_s = A.alloc(4 * 512, BF16).rearrange("p (c t) -> p c t", c=4)
        r_s = A.alloc(4 * 512, BF16).rearrange("p (c t) -> p c t", c=4)
        k_s = A.alloc(4 * 512, BF16).rearrange("p (c t) -> p c t", c=4)
        v_s = A.alloc(4 * 512, BF16).rearrange("p (c t) -> p c t", c=4)
        z_s = A.alloc(2 * 512, F32).rearrange("p (c t) -> p c t", c=2)
        sgd_s = A.alloc(512, BF16)
        sgc_s = A.alloc(8 * 512, BF16).rearrange("p (c t) -> p c t", c=8)
        sgr_s = A.alloc(8 * 512, BF16).rearrange("p (c t) -> p c t", c=8)
        b_xt, b_sq, b_ub, b_rt, b_rs = Buf(), Buf(), Buf(), Buf(), Buf()
        b_hc, b_bg, b_q, b_r, b_k, b_v, b_z, b_sgd, b_sgc, b_sgr = [Buf() for _ in range(10)]

        def fm(ap):
            return ap.rearrange("(c p) t -> p c t", p=128)

        import os
        P1LVL = int(os.environ.get("P1LVL", "20"))
        for ti in range(NT // 512):
            if P1LVL < 1:
                break
            t0 = ti * 512
            dma(xt, fm(xin)[:, :, t0:t0 + 512], (), [b_xt])
            act(sq, xt, AF.Square, [b_xt], [b_sq])
            if P1LVL < 2:
                continue
            rms_rstd(lambda c: sq[:, c, :], 8, 512, rtmp, rstd, b_sq, b_rt, b_rs)
            if P1LVL < 3:
                continue
            for c in range(8):
                stt(ub[:, c, :], xt[:, c, :], vec[:, V_NMP + c:V_NMP + c + 1], rstd, ALU.mult, ALU.mult,
                    [b_xt, b_rs, b_vec], [b_ub])

            def proj(f):
                ps, pb = nextps()
                for kc in range(8):
                    mm(ps[:], W1[:, kc, f * 128:(f + 1) * 128], ub[:, kc, :], kc == 0, kc == 7, [bW1, b_ub], [pb])
                return ps, pb

            if P1LVL < 4:
                continue
            for f in range(43):
                if 20 <= f < 24:
                    continue
                if P1LVL < 20 and f >= {4: 4, 5: 8, 6: 12, 7: 20, 8: 27, 9: 28, 10: 34, 11: 35, 12: 42, 13: 43}[P1LVL]:
                    continue
                ps, pb = proj(f)
                if f < 4:
                    cp("act", hc_s[:, f, :], ps[:], [pb], [b_hc])
                elif f < 8:
                    cp("act", bg_s[:, f - 4, :], ps[:], [pb], [b_bg])
                    if f == 7:
                        dma(fm(bgS)[:, :, t0:t0 + 512], bg_s, [b_bg], ())
                elif f < 12:
                    tt("dve", q_s[:, f - 8, :], ps[:], hc_s[:, f - 8, :], ALU.mult, [pb, b_hc], [b_q])
                    if f == 11:
                        dma(fm(qS)[:, :, 1 + t0:1 + t0 + 512], q_s, [b_q], ())
                elif f < 16:
                    cp("act", r_s[:, f - 12, :], ps[:], [pb], [b_r])
                    if f == 15:
                        dma(fm(rS)[:, :, t0:t0 + 512], r_s, [b_r], ())
                elif f < 20:
                    cp("dve", k_s[:, f - 16, :], ps[:], [pb], [b_k])
                    if f == 19:
                        dma(fm(kS)[:, :, t0:t0 + 512], k_s, [b_k], ())
                elif f < 26:
                    cp("act", z_s[:, f - 24, :], ps[:], [pb], [b_z])
                    if f == 25:
                        dma(fm(zS)[:, :, 1 + t0:1 + t0 + 512], z_s, [b_z], ())
                elif f == 26:
                    act(sgd_s, ps[:], AF.Sigmoid, [pb], [b_sgd])
                    dma(sgdS[:, t0:t0 + 512], sgd_s, [b_sgd], ())
                elif f < 35:
                    act(sgc_s[:, f - 27, :], ps[:], AF.Sigmoid, [pb], [b_sgc])
                    if f == 34:
                        dma(fm(sgcS)[:, :, t0:t0 + 512], sgc_s, [b_sgc], ())
                else:
                    act(sgr_s[:, f - 35, :], ps[:], AF.Sigmoid, [pb], [b_sgr])
                    if f == 42:
                        dma(fm(sgrS)[:, :, t0:t0 + 512], sgr_s, [b_sgr], ())
            for tb in range(4):
                ps, pb = nextps()
                for kc in range(8):
                    mm(ps[:], ub[:, kc, tb * 128:(tb + 1) * 128], W1[:, kc, 2560:3072], kc == 0, kc == 7,
                       [bW1, b_ub], [pb])
                cp(("dve", "act")[tb % 2], v_s[:, tb, :], ps[:], [pb], [b_v])
            dma(vS[t0:t0 + 512, :].rearrange("(b p) f -> p b f", p=128), v_s, [b_v], ())
        S.barrier()

    def pass_P2pre(l):
        A.off = 0
        dw2 = A.alloc(2 * 512, BF16).rearrange("p (d n) -> p d n", d=2)
        a2 = A.alloc(2 * 512, BF16).rearrange("p (d n) -> p d n", d=2)
        stg = [A.alloc(512, F32) for _ in range(2)]
        bst = [Buf(), Buf()]
        b_w = Buf()
        for d in range(2):
            load_weight(dw2[0:64, d, :], dw2_d[l, d], [s[0:64] for s in stg], bst, b_w)
            load_weight(a2[0:64, d, :], a2_d[l, d], [s[0:64] for s in stg], bst, b_w)
        ts("dve", vec[:, V_1MKA:V_1MKA + 4], vec[:, V_KA:V_KA + 4], -1.0, 1.0, ALU.mult, ALU.add, [b_vec], [b_vec])
        r_t = A.alloc(4 * 512, BF16).rearrange("p (c t) -> p c t", c=4)
        k_t = A.alloc(4 * 512, BF16).rearrange("p (c t) -> p c t", c=4)
        zt = [A.alloc(514, F32) for _ in range(4)]
        kk = A.alloc(4 * 512, F32).rearrange("p (c t) -> p c t", c=4)
        prod_raw = A.alloc(4 * 512 * 2, BF16)
        prod = prod_raw.bitcast(F32).rearrange("p (c t) -> p c t", c=4)
        rk_s = A.alloc(32, F32)
        zs = [A.alloc(512, F32) for _ in range(2)]
        tz = A.alloc(512, BF16)
        zab = A.alloc(512, BF16)

        class TS:
            pass

        sets = []
        for _i in range(4):
            X = TS()
            for nm in ("t2", "sgm", "av", "cs", "ex", "rs_", "ri", "kd", "bb"):
                setattr(X, nm, A.alloc(512, F32))
                setattr(X, "b_" + nm, Buf())
            X.E = [A.alloc(512, F32) for _ in range(4)]
            X.b_E = [Buf() for _ in range(4)]
            X.t1, X.b_t1 = X.E[0], X.b_E[0]
            X.t3, X.b_t3 = X.E[1], X.b_E[1]
            sets.append(X)
        fmst = [A.alloc(4 * 4 * 128, BF16).rearrange("p (c a t) -> p c a t", c=4, a=4) for _ in range(4)]
        sc_s = prod_raw.rearrange("p (a c t) -> p a c t", a=2, c=4)
        tm_s = A.alloc(4 * 2 * 512, BF16).rearrange("p (b a f) -> p b a f", b=4, a=2)
        gc_s = A.alloc(16, F32).rearrange("p (c f) -> p c f", c=4)
        b_r, b_k, b_kk, b_prod, b_rk = [Buf() for _ in range(5)]
        b_z = [Buf() for _ in range(4)]
        b_zs = [Buf(), Buf()]
        b_tz, b_zab = Buf(), Buf()
        b_fm = [Buf() for _ in range(4)]
        b_sc, b_tm, b_gc = b_prod, Buf(), Buf()

        def fm(ap):
            return ap.rearrange("(c p) t -> p c t", p=128)

        for ti in range(NT // 512):
            t0 = ti * 512
            dma(r_t, fm(rS)[:, :, t0:t0 + 512], (), [b_r])
            dma(k_t, fm(kS)[:, :, t0:t0 + 512], (), [b_k])
            for i in range(4):
                dma(zt[i][0:64, :], zS[i * 64:(i + 1) * 64, t0:t0 + 514], (), [b_z[i]])
            if t0 % SEG == 0:
                for i in (0, 1):
                    halo_fix("pool", zt[i][0:64, 0:1], t0 // SEG, [b_z[i]], [b_z[i]])
            if (t0 + 512) % SEG == 0:
                for i in (2, 3):
                    halo_fix("pool", zt[i][0:64, 513:514], (t0 + 512) // SEG, [b_z[i]], [b_z[i]])
            for fc in range(4):
                X = sets[fc]
                ts("dve", X.t1, k_t[:, fc, :], vec[:, V_KK + fc:V_KK + fc + 1], None, ALU.mult, None, [b_k, b_vec], [X.b_t1])
                tt("dve", X.t2, X.t1, X.t1, ALU.mult, [X.b_t1], [X.b_t2])
            for fc in range(4):
                X = sets[fc]
                ps, pb = nextps()
                mm(ps[:], BLK, X.t2, True, True, [X.b_t2, b_c], [pb])
                ts("dve", X.t3, ps[:], 1e-24, None, ALU.max, None, [pb], [X.b_t3])
                stt(prod[:, fc, :], r_t[:, fc, :], vec[:, V_RK + fc:V_RK + fc + 1], k_t[:, fc, :], ALU.mult, ALU.mult,
                    [b_r, b_k, b_vec], [b_prod])
            for fc in range(4):
                X = sets[fc]
                act(X.t3, X.t3, AF.Ln, [X.b_t3], [X.b_t3])
            for fc in range(4):
                X = sets[fc]
                act(X.t3, X.t3, AF.Exp, [X.b_t3], [X.b_t3], scale=-0.5)
            for fc in range(4):
                X = sets[fc]
                tt("dve", kk[:, fc, :], X.t1, X.t3, ALU.mult, [X.b_t1, X.b_t3], [b_kk])
            ps, pb = nextps()
            for tb in range(4):
                for fc in range(4):
                    mm(ps[:, tb * 8:(tb + 1) * 8], prod[:, fc, tb * 128:(tb + 1) * 128],
                       cf[:, C_HSEL + fc * 8:C_HSEL + (fc + 1) * 8], fc == 0, fc == 3, [b_prod, b_c], [pb])
            cp("act", rk_s, ps[:, 0:32], [pb], [b_rk])
            dma(rkS[t0:t0 + 512, :].rearrange("(b p) h -> p b h", p=128), rk_s.rearrange("p (b h) -> p b h", b=4),
                [b_rk], ())
            for d in range(2):
                for part in range(2):
                    t1, b_t1 = sets[part].t1, sets[part].b_t1
                    Z = zt[d * 2 + part]
                    cur = Z[0:64, 1:513]
                    sh = Z[0:64, 0:512] if d == 0 else Z[0:64, 2:514]
                    bz = b_z[d * 2 + part]
                    tt("dve", t1[0:64, :], sh, cur, ALU.subtract, [bz], [b_t1])
                    stt(zs[part][0:64, :], t1[0:64, :], vec[0:64, V_MU + d * 2 + part:V_MU + d * 2 + part + 1], cur,
                        ALU.mult, ALU.add, [b_t1, bz, b_vec], [b_zs[part]])
                act(tz[0:64, :], zs[0][0:64, :], AF.Tanh, [b_zs[0]], [b_tz])
                cp("dve", zab[0:64, :], zs[1][0:64, :], [b_zs[1]], [b_zab])
                def fc_gen(d, fc):
                        X = sets[fc]
                        t2, sgm, av, cs, ex, rs_, ri, kd, bb, E = X.t2, X.sgm, X.av, X.cs, X.ex, X.rs_, X.ri, X.kd, X.bb, X.E
                        b_t2, b_sgm, b_av, b_cs, b_ex, b_rs, b_ri, b_kd, b_bb, b_E = (
                            X.b_t2, X.b_sgm, X.b_av, X.b_cs, X.b_ex, X.b_rs_, X.b_ri, X.b_kd, X.b_bb, X.b_E)
                        ps, pb = nextps()
                        mm(ps[:], dw2[0:64, d, fc * 128:(fc + 1) * 128], tz[0:64, :], True, True, [b_w, b_tz], [pb])
                        act(sgm, ps[:], AF.Sigmoid, [pb, b_vec], [b_sgm],
                            bias=vec[:, V_W0 + d * 4 + fc:V_W0 + d * 4 + fc + 1], scale=1.0)
                        ps2, pb2 = nextps()
                        mm(ps2[:], a2[0:64, d, fc * 128:(fc + 1) * 128], zab[0:64, :], True, True, [b_w, b_zab], [pb2])
                        act(av, ps2[:], AF.Sigmoid, [pb2, b_vec], [b_av],
                            bias=vec[:, V_A0 + d * 4 + fc:V_A0 + d * 4 + fc + 1], scale=1.0)
                        yield
                        S.op("dve", lambda e, cs=cs, sgm=sgm: e.tensor_tensor_scan(out=cs, data0=RESET, data1=sgm, initial=0.0,
                                                                                   op0=ALU.mult, op1=ALU.add),
                             [b_sgm, b_c], [b_cs])
                        cs3 = cs.rearrange("p (c t) -> p c t", c=4)
                        tot_bc = cs3[:, :, 127:128].to_broadcast([128, 4, 128])
                        tt("pool", ex, cs, sgm, ALU.subtract, [b_cs, b_sgm], [b_ex])
                        tt("dve", rs_.rearrange("p (c t) -> p c t", c=4), tot_bc, cs3, ALU.subtract, [b_cs], [b_rs])
                        if d == 0:
                            e1s, e2s, e4s = cs, ex, rs_
                            br1, br2, br4 = b_cs, b_ex, b_rs
                        else:
                            tt("dve", ri, rs_, sgm, ALU.add, [b_rs, b_sgm], [b_ri])
                            e1s, e2s, e4s = ri, rs_, ex
                            br1, br2, br4 = b_ri, b_rs, b_ex
                        ts("dve", t2, av, vec[:, V_KA + fc:V_KA + fc + 1], vec[:, V_1MKA + fc:V_1MKA + fc + 1],
                           ALU.mult, ALU.add, [b_av, b_vec], [b_t2])
                        tt("dve", kd, t2, k_t[:, fc, :], ALU.mult, [b_t2, b_k], [b_kd])
                        tt("pool", bb, kk[:, fc, :], av, ALU.mult, [b_kk, b_av], [b_bb])
                        yield
                        act(E[0], e1s, AF.Exp, [br1], [b_E[0]], scale=CDEC)
                        act(E[1], e2s, AF.Exp, [br2], [b_E[1]], scale=CDEC)
                        act(E[2], e1s, AF.Exp, [br1], [b_E[2]], scale=-CDEC)
                        act(E[3], e4s, AF.Exp, [br4], [b_E[3]], scale=CDEC)
                        act(gc_s[:, :, fc], cs3[:, :, 127], AF.Exp, [b_cs], [b_gc], scale=CDEC)
                        yield
                        F_ = fmst[fc]
                        bF = b_fm[fc]

                        def v4(ap):
                            return ap.rearrange("p (c t) -> p c t", c=4)

                        tt("dve", F_[:, :, 0, :], v4(kk[:, fc, :]), v4(E[1]), ALU.mult, [b_kk, b_E[1]], [bF])
                        tt("dve", F_[:, :, 1, :], v4(bb), v4(E[2]), ALU.mult, [b_bb, b_E[2]], [bF])
                        tt("dve", F_[:, :, 2, :], v4(kd), v4(E[2]), ALU.mult, [b_kd, b_E[2]], [bF])
                        tt("dve", F_[:, :, 3, :], v4(r_t[:, fc, :]), v4(E[0]), ALU.mult, [b_r, b_E[0]], [bF])
                        tt("dve", sc_s[:, 0, fc, :], kd, E[3], ALU.mult, [b_kd, b_E[3]], [b_sc])
                        tt("pool", sc_s[:, 1, fc, :], bb, E[3], ALU.mult, [b_bb, b_E[3]], [b_sc])
                        for half in range(2):
                            hp = 2 * fc + half
                            dst = fmS[d, t0 // 128:t0 // 128 + 4].rearrange("c k (h x) -> k c h x", h=8)[:, :, hp, :]
                            dma(dst, F_[half * 64:(half + 1) * 64].rearrange("p c a t -> p c (a t)"), [bF], ())
                gens = [fc_gen(d, fc) for fc in range(4)]
                while gens:
                    alive = []
                    for gen in gens:
                        try:
                            next(gen)
                            alive.append(gen)
                        except StopIteration:
                            pass
                    gens = alive
                dma(gcS[d, t0 // 128:t0 // 128 + 4].rearrange("c p f -> p c f"), gc_s, [b_gc], ())
                for a_ in range(2):
                    for tb in range(4):
                        ps, pb = nextps()
                        psT = ps[:].bitcast(BF16)
                        for fc in range(4):
                            transpose(psT[:, fc * 128:(fc + 1) * 128], sc_s[:, a_, fc, tb * 128:(tb + 1) * 128],
                                      [b_sc], [pb])
                        cp("act", tm_s[:, tb, a_, :], psT[:, 0:512], [pb], [b_tm])
                dma(tmS[d, t0:t0 + 512, :].rearrange("(b p) x -> p b x", p=128),
                    tm_s.rearrange("p b a f -> p b (a f)"), [b_tm], ())
        S.barrier()

    def pass_P2scan(l):
        A.off = 0
        NG = NCH * 8
        gall = [A.alloc(NG, F32).rearrange("p (c a f) -> p c a f", a=2, f=4) for _ in range(2)]
        b_gall = Buf()
        for d in range(2):
            for half in range(2):
                for c0 in range(0, NCH, 16):
                    c1 = min(NCH, c0 + 16)
                    dma(gall[d][0:64, c0:c1, half, :],
                        gcS[d, c0:c1, half * 64:(half + 1) * 64, :].rearrange("c p f -> p c f"), (), [b_gall])

        class Ctx:
            pass

        def h4(n=1):
            return [A.alloc(512, BF16).rearrange("p (h t) -> p h t", h=4) for _ in range(n)]

        ctxs = []
        for d in range(2):
            cx = Ctx()
            cx.d = d
            cx.fm = [A.alloc(8 * 4 * 128, BF16).rearrange("p (h a t) -> p h a t", h=8, a=4) for _ in range(3)]
            cx.tm = [A.alloc(1024, BF16) for _ in range(3)]
            cx.v = [A.alloc(512, BF16) for _ in range(3)]
            cx.b_in = [Buf() for _ in range(3)]
            cx.N = [h4(2) for _ in range(2)]
            cx.NT = [h4(2) for _ in range(2)]
            cx.bN = [[Buf(), Buf()] for _ in range(2)]
            cx.bNT = [[Buf(), Buf()] for _ in range(2)]
            cx.P = [h4(2) for _ in range(2)]
            cx.ARB = [h4(2) for _ in range(2)]
            cx.AKD = [h4(2) for _ in range(2)]
            cx.ARKD = [h4(2) for _ in range(2)]
            cx.bP = [[Buf(), Buf()] for _ in range(2)]
            cx.bARB = [[Buf(), Buf()] for _ in range(2)]
            cx.bAKD = [[Buf(), Buf()] for _ in range(2)]
            cx.bARKD = [[Buf(), Buf()] for _ in range(2)]
            cx.Xn = A.alloc(512, BF16)
            cx.bXn = Buf()
            cx.U = A.alloc(512, BF16)
            cx.bU = Buf()
            cx.ybuf = [A.alloc(512, F32) for _ in range(2)]
            cx.bY = [Buf(), Buf()]
            cx.S32 = [A.alloc(512, F32) for _ in range(2)]
            cx.Sbf = [A.alloc(512, BF16) for _ in range(2)]
            cx.bS32 = [Buf(), Buf()]
            cx.bSbf = [Buf(), Buf()]
            cx.t1 = A.alloc(512, F32)
            cx.bt1 = Buf()
            cx.cur = 0
            ctxs.append(cx)

        ident4 = cb[:, 0:128].rearrange("p (o t) -> p o t", o=1).to_broadcast([128, 4, 128])

        def m4(m):
            return m.rearrange("p (o t) -> p o t", o=1).to_broadcast([128, 4, 128])

        def ps4(ps):
            return ps[:].rearrange("p (h t) -> p h t", h=4)

        def chunk_of(cx, it):
            return it if cx.d == 0 else NCH - 1 - it

        def load(cx, it):
            c = chunk_of(cx, it)
            i3 = it % 3
            d = cx.d
            dma(cx.fm[i3][0:64].rearrange("p h a t -> p (h a t)"), fmS[d, c], (), [cx.b_in[i3]])
            dma(cx.tm[i3], tmS[d, c * 128:(c + 1) * 128, :], (), [cx.b_in[i3]])
            dma(cx.v[i3], vS[c * 128:(c + 1) * 128, :], (), [cx.b_in[i3]])

        def prod4(g, lf, rf, rbufs):
            ps, pb = nextps()
            for i in range(4):
                h = g * 4 + i
                mm(ps[:, i * 128:(i + 1) * 128], lf(h), rf(h), True, True, rbufs, [pb])
            return ps, pb

        def local(cx, it, g):
            d = cx.d
            i3 = it % 3
            par = it % 2
            fm_ = cx.fm[i3]
            b_in = cx.b_in[i3]
            mS, mSt, mR = (SU, SL, UI) if d == 0 else (SL, SU, LI)
            KK = lambda h: fm_[0:64, h, 0, :]
            BH = lambda h: fm_[0:64, h, 1, :]
            KD = lambda h: fm_[0:64, h, 2, :]
            RT = lambda h: fm_[0:64, h, 3, :]
            cn = 0
            N, NT_ = cx.N[g], cx.NT[g]
            bN, bNT = cx.bN[g], cx.bNT[g]
            P, bP = cx.P[par][g], cx.bP[par][g]
            ps, pb = prod4(g, BH, KK, [b_in])
            tt("dve", N[cn], ps4(ps), m4(mS), ALU.mult, [pb, b_c], [bN[cn]])
            yield
            ps, pb = prod4(g, KK, BH, [b_in])
            tt("dve", NT_[cn], ps4(ps), m4(mSt), ALU.mult, [pb, b_c], [bNT[cn]])
            tt("pool", P, ident4, N[cn], ALU.subtract, [b_c, bN[cn]], [bP])
            yield
            for lvl in range(1, 7):
                nn = 1 - cn
                if lvl < 6:
                    ps, pb = prod4(g, lambda h: NT_[cn][:, h % 4, :], lambda h: N[cn][:, h % 4, :], [bN[cn], bNT[cn]])
                    cp("act", N[nn], ps4(ps), [pb], [bN[nn]])
                ps, pb = prod4(g, lambda h: N[cn][:, h % 4, :], lambda h: NT_[cn][:, h % 4, :], [bN[cn], bNT[cn]])
                cp("act", NT_[nn], ps4(ps), [pb], [bNT[nn]])
                yield
                ps, pb = prod4(g, lambda h: NT_[nn][:, h % 4, :], lambda h: P[:, h % 4, :], [bNT[nn], bP])
                tt("dve", P, ps4(ps), P, ALU.add, [pb, bP], [bP])
                cn = nn
                yield
                if lvl == 1:
                    ps, pb = prod4(g, BH, RT, [b_in])
                    tt("dve", cx.ARB[par][g], ps4(ps), m4(mR), ALU.mult, [pb, b_c], [cx.bARB[par][g]])
                    yield
                elif lvl == 2:
                    ps, pb = prod4(g, KD, KK, [b_in])
                    tt("dve", cx.AKD[par][g], ps4(ps), m4(mS), ALU.mult, [pb, b_c], [cx.bAKD[par][g]])
                    yield
                elif lvl == 3:
                    ps, pb = prod4(g, KD, RT, [b_in])
                    tt("dve", cx.ARKD[par][g], ps4(ps), m4(mR), ALU.mult, [pb, b_c], [cx.bARKD[par][g]])
                    yield

        def chain(cx, it):
            d = cx.d
            c = chunk_of(cx, it)
            i3 = it % 3
            par = it % 2
            fm_, tm_, v_ = cx.fm[i3], cx.tm[i3], cx.v[i3]
            b_in = cx.b_in[i3]
            KK = lambda h: fm_[0:64, h, 0, :]
            RT = lambda h: fm_[0:64, h, 3, :]
            Vh = lambda h: v_[:, h * 64:(h + 1) * 64]
            KDs = lambda h: tm_[:, h * 64:(h + 1) * 64]
            Bs = lambda h: tm_[:, 512 + h * 64:512 + (h + 1) * 64]
            P, bP = cx.P[par], cx.bP[par]
            ARB, bARB = cx.ARB[par], cx.bARB[par]
            AKD, bAKD = cx.AKD[par], cx.bAKD[par]
            ARKD, bARKD = cx.ARKD[par], cx.bARKD[par]
            cur = cx.cur
            nxt = 1 - cur
            S32, Sbf = cx.S32[cur], cx.Sbf[cur]
            bS32, bSbf = cx.bS32[cur], cx.bSbf[cur]
            first = (c % CPS == 0) if d == 0 else ((c + 1) % CPS == 0)
            if first:
                b = c // CPS if d == 0 else (c + 1) // CPS
                halo_fix("pool", S32[0:64, :], b, [bS32], [bS32])
                halo_fix("pool", Sbf[0:64, :], b, [bSbf], [bSbf])
            S32v = S32[0:64, :].rearrange("p (f a v) -> p a f v", f=4, a=2)
            t1v = cx.t1[0:64, :].rearrange("p (f a v) -> p a f v", f=4, a=2)
            gCv = gall[d][0:64, c].rearrange("p a (f o) -> p a f o", o=1).to_broadcast([64, 2, 4, 64])
            tt("pool", t1v, S32v, gCv, ALU.mult, [bS32, b_gall], [cx.bt1])
            ps, pb = nextps()
            for h in range(8):
                g = h // 4
                mm(ps[:, h * 64:(h + 1) * 64], KK(h), Sbf[0:64, h * 64:(h + 1) * 64], True, False, [b_in, bSbf], [pb])
                mm(ps[:, h * 64:(h + 1) * 64], AKD[g][:, h % 4, :], Vh(h), False, True, [bAKD[g], b_in], [pb])
            amul(cx.Xn, ps[:], -1.0, [pb], [cx.bXn])
            yield
            ps, pb = nextps()
            for h in range(8):
                g = h // 4
                mm(ps[:, h * 64:(h + 1) * 64], P[g][:, h % 4, :], cx.Xn[:, h * 64:(h + 1) * 64], True, True,
                   [bP[g], cx.bXn], [pb])
            cp("dve", cx.U, ps[:], [pb], [cx.bU])
            yield
            ps, pb = nextps()
            for h in range(8):
                o = ps[0:64, h * 64:(h + 1) * 64]
                mm(o, KDs(h), Vh(h), True, False, [b_in], [pb])
                mm(o, Bs(h), cx.U[:, h * 64:(h + 1) * 64], False, True, [b_in, cx.bU], [pb])
            tt("dve", cx.Sbf[nxt][0:64, :], ps[0:64, :], cx.t1[0:64, :], ALU.add, [pb, cx.bt1], [cx.bSbf[nxt]])
            tt("dve", cx.S32[nxt][0:64, :], ps[0:64, :], cx.t1[0:64, :], ALU.add, [pb, cx.bt1], [cx.bS32[nxt]])
            cx.cur = nxt
            yield
            ps, pb = nextps()
            for h in range(8):
                g = h // 4
                o = ps[:, h * 64:(h + 1) * 64]
                mm(o, RT(h), Sbf[0:64, h * 64:(h + 1) * 64], True, False, [b_in, bSbf], [pb])
                mm(o, ARKD[g][:, h % 4, :], Vh(h), False, False, [bARKD[g], b_in], [pb])
                mm(o, ARB[g][:, h % 4, :], cx.U[:, h * 64:(h + 1) * 64], False, True, [bARB[g], cx.bU], [pb])
            yb_, bY = cx.ybuf[par], cx.bY[par]
            cp("act", yb_, ps[:], [pb], [bY])
            dma(yS[d, c * 128:(c + 1) * 128, :], yb_, [bY], ())
            yield

        for cx in ctxs:
            load(cx, 0)
        for it in range(NCH + 1):
            gens = []
            if it >= 1:
                gens += [chain(ctxs[0], it - 1), chain(ctxs[1], it - 1)]
            if it < NCH:
                if it + 1 < NCH:
                    for cx in ctxs:
                        load(cx, it + 1)
                for cx in ctxs:
                    for g in range(2):
                        gens.append(local(cx, it, g))
            while gens:
                alive = []
                for gen in gens:
                    try:
                        next(gen)
                        alive.append(gen)
                    except StopIteration:
                        pass
                gens = alive
        S.barrier()

    def pass_P3a(l, xin):
        A.off = 0
        wa = A.alloc(4 * D, BF16).rearrange("p (k n) -> p k n", k=4)
        wb = A.alloc(4 * D, BF16).rearrange("p (k n) -> p k n", k=4)
        wo = A.alloc(8 * D, BF16).rearrange("p (k n) -> p k n", k=8)
        g2 = A.alloc(512, BF16)
        gnw = A.alloc(512, F32)
        gnb = A.alloc(512, F32)
        stg = [A.alloc(D, F32) for _ in range(2)]
        bst = [Buf(), Buf()]
        b_w = Buf()
        for kc in range(4):
            load_weight(wa[:, kc, :], w_a[l, kc * 128:(kc + 1) * 128, :], stg, bst, b_w)
            load_weight(wb[:, kc, :], w_b[l, kc * 128:(kc + 1) * 128, :], stg, bst, b_w)
        for kc in range(8):
            load_weight(wo[:, kc, :], w_out[l, kc * 128:(kc + 1) * 128, :], stg, bst, b_w)
        load_weight(g2, g2_d[l], [s[:, 0:512] for s in stg], bst, b_w)
        dma(gnw, gnwb_d[l][:, 0:512], (), [b_w])
        dma(gnb, gnwb_d[l][:, 512:1024], (), [b_w])

        def a3(n, c, dt):
            return A.alloc(c * n, dt).rearrange("p (c t) -> p c t", c=c)

        q_t = a3(514, 4, BF16)
        bg_t = a3(512, 4, BF16)
        sgd_t = A.alloc(512, BF16)
        sgc_t = a3(512, 8, BF16)
        sgr_t = a3(512, 8, BF16)
        yf = [A.alloc(512, F32) for _ in range(2)]
        ybk = [A.alloc(512, F32) for _ in range(2)]
        v_t = [A.alloc(512, BF16) for _ in range(2)]
        rk_t = [A.alloc(8, F32) for _ in range(2)]
        class TS:
            pass

        tsets = []
        for _i in range(2):
            X = TS()
            X.y32, X.tmp, X.tmp2 = A.alloc(512, F32), A.alloc(512, F32), A.alloc(512, F32)
            X.stat, X.o_bf = A.alloc(64, F32), A.alloc(512, BF16)
            X.b_y32, X.b_tmp, X.b_tmp2, X.b_stat, X.b_o = [Buf() for _ in range(5)]
            tsets.append(X)
        oT = a3(512, 4, BF16)
        cqs = [A.alloc(512, F32) for _ in range(2)]
        b_cqs = [Buf(), Buf()]
        ca = a3(512, 4, BF16)
        mg1 = a3(512, 8, F32)
        x_t = mg1
        merged = a3(512, 8, BF16)
        m32 = a3(512, 8, F32)
        sqm = a3(512, 8, BF16)
        rtmp = A.alloc(512, F32)
        rstd = A.alloc(512, F32)
        x1_t = m32
        (b_q, b_bg, b_sgd, b_sgc, b_sgr, b_x_unused, b_y32, b_tmp, b_tmp2, b_stat, b_o, b_oT, b_cq, b_ca, b_mg1, b_mer,
         b_m32, b_sqm, b_rt, b_rs, b_x1) = [Buf() for _ in range(21)]
        b_x1 = b_m32
        b_tb = [Buf(), Buf()]
        b_x = b_mg1

        def fm(ap):
            return ap.rearrange("(c p) t -> p c t", p=128)

        def h8(ap):
            return ap.rearrange("p (h v) -> p h v", h=8)

        def treduce(out, in_, r, w):
            S.op("dve", lambda e: e.tensor_reduce(out=out, in_=in_, axis=AX.X, op=ALU.add), r, w)

        for ti in range(NT // 512):
            t0 = ti * 512
            dma(q_t, fm(qS)[:, :, t0:t0 + 514], (), [b_q])
            dma(bg_t, fm(bgS)[:, :, t0:t0 + 512], (), [b_bg])
            dma(sgd_t, sgdS[:, t0:t0 + 512], (), [b_sgd])
            dma(sgc_t, fm(sgcS)[:, :, t0:t0 + 512], (), [b_sgc])
            dma(sgr_t, fm(sgrS)[:, :, t0:t0 + 512], (), [b_sgr])
            import os
            P3LVL = int(os.environ.get("P3LVL", "20"))
            if P3LVL < 2:
                continue
            if t0 % SEG == 0:
                halo_fix("dve", q_t[:, :, 0:1], t0 // SEG, [b_q], [b_q])
            if (t0 + 512) % SEG == 0:
                halo_fix("dve", q_t[:, :, 513:514], (t0 + 512) // SEG, [b_q], [b_q])
            for tb in range(4):
                if P3LVL < 3:
                    continue
                pp = tb % 2
                r0 = t0 + tb * 128
                dma(yf[pp], yS[0, r0:r0 + 128, :], (), [b_tb[pp]])
                dma(ybk[pp], yS[1, r0:r0 + 128, :], (), [b_tb[pp]])
                dma(v_t[pp], vS[r0:r0 + 128, :], (), [b_tb[pp]])
                dma(rk_t[pp], rkS[r0:r0 + 128, :], (), [b_tb[pp]])
                bt = b_tb[pp]
                X = tsets[pp]
                y32, tmp, tmp2, stat, o_bf = X.y32, X.tmp, X.tmp2, X.stat, X.o_bf
                b_y32, b_tmp, b_tmp2, b_stat, b_o = X.b_y32, X.b_tmp, X.b_tmp2, X.b_stat, X.b_o

                def st8(i, stat=stat):
                    return stat[:, i * 8:(i + 1) * 8]

                def bc8(i, stat=stat):
                    return stat[:, i * 8:(i + 1) * 8].rearrange("p (h o) -> p h o", o=1).to_broadcast([128, 8, 64])

                tt("pool", y32, yf[pp], ybk[pp], ALU.add, [bt], [b_y32])
                treduce(st8(0), h8(y32), [b_y32], [b_stat])
                act(tmp, y32, AF.Square, [b_y32], [b_tmp])
                treduce(st8(1), h8(tmp), [b_tmp], [b_stat])
                ts("dve", st8(0), st8(0), 1.0 / 64, None, ALU.mult, None, [b_stat], [b_stat])
                tt("dve", st8(2), st8(0), st8(0), ALU.mult, [b_stat], [b_stat])
                stt(st8(3), st8(1), 1.0 / 64, st8(2), ALU.mult, ALU.subtract, [b_stat], [b_stat])
                act(st8(4), st8(3), AF.Ln, [b_stat, b_c], [b_stat], bias=cf[:, C_GNEPS:C_GNEPS + 1], scale=1.0)
                act(st8(5), st8(4), AF.Exp, [b_stat], [b_stat], scale=-0.5)
                if P3LVL < 4:
                    continue
                tt("dve", h8(tmp), h8(y32), bc8(0), ALU.subtract, [b_y32, b_stat], [b_tmp])
                tt("dve", h8(tmp2), h8(tmp), bc8(5), ALU.mult, [b_tmp, b_stat], [b_tmp2])
                tt("dve", tmp, tmp2, gnw, ALU.mult, [b_tmp2, b_w], [b_tmp])
                tt("dve", tmp2, tmp, gnb, ALU.add, [b_tmp, b_w], [b_tmp2])
                rkb = rk_t[pp].rearrange("p (h o) -> p h o", o=1).to_broadcast([128, 8, 64])
                tt("dve", h8(tmp), h8(v_t[pp]), rkb, ALU.mult, [bt], [b_tmp])
                tt("dve", y32, tmp2, tmp, ALU.add, [b_tmp2, b_tmp], [b_y32])
                if P3LVL < 5:
                    continue
                ps, pb = nextps()
                mm(ps[:], sgd_t[:, tb * 128:(tb + 1) * 128], g2, True, True, [b_sgd, b_w], [pb])
                tt("dve", o_bf, ps[:], y32, ALU.mult, [pb, b_y32], [b_o])
                ps, pb = nextps()
                psT = ps[:].bitcast(BF16)
                for fc in range(4):
                    transpose(psT[:, fc * 128:(fc + 1) * 128], o_bf[:, fc * 128:(fc + 1) * 128], [b_o], [pb])
                cp("act", oT[:, :, tb * 128:(tb + 1) * 128], psT[:, 0:512].rearrange("p (c t) -> p c t", c=4),
                   [pb], [b_oT])
            if P3LVL < 6:
                continue
            for fc in range(4):
                cq, b_cq = cqs[fc % 2], b_cqs[fc % 2]
                cw = lambda j: vec[:, V_CONVW + j * 4 + fc:V_CONVW + j * 4 + fc + 1]
                ts("dve", cq, q_t[:, fc, 1:513], cw(1), vec[:, V_CONVB + fc:V_CONVB + fc + 1], ALU.mult, ALU.add,
                   [b_q, b_vec], [b_cq])
                stt(cq, q_t[:, fc, 0:512], cw(0), cq, ALU.mult, ALU.add, [b_q, b_vec, b_cq], [b_cq])
                stt(cq, q_t[:, fc, 2:514], cw(2), cq, ALU.mult, ALU.add, [b_q, b_vec, b_cq], [b_cq])
                tt("dve", ca[:, fc, :], cq, bg_t[:, fc, :], ALU.mult, [b_cq, b_bg], [b_ca])
            if P3LVL < 7:
                continue
            for mc in range(8):
                ps, pb = nextps()
                for kc in range(4):
                    mm(ps[:], wa[:, kc, mc * 128:(mc + 1) * 128], ca[:, kc, :], kc == 0, kc == 3, [b_w, b_ca], [pb])
                tt("dve", mg1[:, mc, :], ps[:], sgc_t[:, mc, :], ALU.mult, [pb, b_sgc], [b_mg1])
            if P3LVL < 8:
                continue
            for mc in range(8):
                ps, pb = nextps()
                for kc in range(4):
                    mm(ps[:], wb[:, kc, mc * 128:(mc + 1) * 128], oT[:, kc, :], kc == 0, kc == 3, [b_w, b_oT], [pb])
                X = tsets[mc % 2]
                tt("dve", X.tmp, ps[:], sgr_t[:, mc, :], ALU.mult, [pb, b_sgr], [X.b_tmp])
                tt("dve", merged[:, mc, :], X.tmp, mg1[:, mc, :], ALU.add, [X.b_tmp, b_mg1], [b_mer])
            dma(x_t, fm(xin)[:, :, t0:t0 + 512], (), [b_x])
            if P3LVL < 9:
                continue
            for mc in range(8):
                ps, pb = nextps()
                for kc in range(8):
                    mm(ps[:], wo[:, kc, mc * 128:(mc + 1) * 128], merged[:, kc, :], kc == 0, kc == 7, [b_w, b_mer], [pb])
                cp("dve", m32[:, mc, :], ps[:], [pb], [b_m32])
                act(sqm[:, mc, :], m32[:, mc, :], AF.Square, [b_m32], [b_sqm])
            if P3LVL < 10:
                continue
            rms_rstd(lambda c: sqm[:, c, :], 8, 512, rtmp, rstd, b_sqm, b_rt, b_rs)
            if P3LVL < 11:
                continue
            for mc in range(8):
                X = tsets[mc % 2]
                tt("dve", X.tmp, m32[:, mc, :], rstd, ALU.mult, [b_m32, b_rs], [X.b_tmp])
                stt(x1_t[:, mc, :], X.tmp, vec[:, V_NMPOST + mc:V_NMPOST + mc + 1], x_t[:, mc, :], ALU.mult, ALU.add,
                    [X.b_tmp, b_vec, b_x], [b_x1])
            if P3LVL < 12:
                continue
            dma(fm(x1S)[:, :, 1 + t0:1 + t0 + 512], x1_t, [b_x1], ())
        S.barrier()

    def pass_P3b(l, xout):
        A.off = 0
        wd = A.alloc(22 * D, BF16).rearrange("p (k n) -> p k n", k=22)
        wpg = A.alloc(8 * D, BF16).rearrange("p (k n) -> p k n", k=8)
        wpl = A.alloc(2 * D, BF16).rearrange("p (k n) -> p k n", k=2)
        mark = A.off
        stg = [A.alloc(D, F32) for _ in range(2)]
        bst = [Buf(), Buf()]
        b_w = Buf()
        for kc in range(22):
            load_weight(wd[:, kc, :], w_down[l, kc * 128:(kc + 1) * 128, :], stg, bst, b_w)
        for kc in range(8):
            load_weight(wpg[:, kc, :], w_pg[l, kc * 128:(kc + 1) * 128, :], stg, bst, b_w)
        for kc in range(2):
            load_weight(wpl[:, kc, :], w_ple[l, kc * 128:(kc + 1) * 128, :], stg, bst, b_w)
        S.barrier()
        A.off = mark
        WM = 412

        def a3(n, c, dt):
            return A.alloc(c * n, dt).rearrange("p (c t) -> p c t", c=c)

        x1w = a3(WM, 8, F32)
        sq = a3(WM, 8, BF16)
        u = a3(WM, 8, BF16)
        rtmp = A.alloc(WM, F32)
        rstd = A.alloc(WM, F32)
        p_t = a3(WM, 2, F32)
        p_b = a3(WM, 2, BF16)
        wj = [A.alloc(8 * 2 * 128, BF16).rearrange("p (k g m) -> p k g m", k=8, g=2) for _ in range(3)]
        b_wj = [Buf() for _ in range(3)]
        class TS:
            pass

        jsets = []
        for _i in range(2):
            X = TS()
            for nm in ("cg", "cv", "g1", "g2_", "g3", "gate", "tmp"):
                setattr(X, nm, A.alloc(WM, F32))
                setattr(X, "b_" + nm, Buf())
            jsets.append(X)
        actb = a3(WM, 22, BF16)
        m32 = a3(WM, 8, F32)
        sqm = sq
        x2_t = a3(WM, 8, F32)
        x2b = u
        (b_x1, b_sq, b_u, b_rt, b_rs, b_p, b_pb, b_cg_u, b_cv_u, b_g1_u, b_g2_u, b_g3_u, b_act, b_m32, b_sqm, b_x2, b_x2b,
         b_gate_u, b_tmp_u) = [Buf() for _ in range(19)]
        b_sqm = b_sq
        b_x2b = b_u

        def fm(ap):
            return ap.rearrange("(c p) t -> p c t", p=128)

        tiles = []
        for sgi in range(NSEG):
            a = sgi * SEG
            npc = (SEG + 409) // 410
            base = SEG // npc
            rem = SEG - base * npc
            for i in range(npc):
                n = base + (1 if i < rem else 0)
                tiles.append((a, n))
                a += n
        wctr = 0
        for (a0, n) in tiles:
            W = n + 2
            dma(x1w[:, :, 0:W], fm(x1S)[:, :, a0:a0 + W], (), [b_x1])
            dma(p_t[:, :, 0:n], pT[l].rearrange("(c p) t -> p c t", p=128)[:, :, a0:a0 + n], (), [b_p])
            cp("pool", p_b[:, :, 0:n], p_t[:, :, 0:n], [b_p], [b_pb])
            act(sq[:, :, 0:W], x1w[:, :, 0:W], AF.Square, [b_x1], [b_sq])
            rms_rstd(lambda c: sq[:, c, 0:W], 8, W, rtmp[:, 0:W], rstd[:, 0:W], b_sq, b_rt, b_rs)
            for c in range(8):
                stt(u[:, c, 0:W], x1w[:, c, 0:W], vec[:, V_NFP + c:V_NFP + c + 1], rstd[:, 0:W], ALU.mult, ALU.mult,
                    [b_x1, b_rs, b_vec], [b_u])
            if a0 % SEG == 0:
                halo_fix("dve", u[:, :, 0:1], a0 // SEG, [b_u], [b_u])
            if (a0 + n) % SEG == 0:
                halo_fix("dve", u[:, :, W - 1:W], (a0 + n) // SEG, [b_u], [b_u])
            for j in range(22):
                X = jsets[j % 2]
                cg, cv, g1, g2_, g3 = X.cg, X.cv, X.g1, X.g2_, X.g3
                b_cg, b_cv, b_g1, b_g2, b_g3 = X.b_cg, X.b_cv, X.b_g1, X.b_g2_, X.b_g3
                wi = wctr % 3
                wctr += 1
                dma(wj[wi].rearrange("p k g m -> p (k g m)"), wupS[l, j], (), [b_wj[wi]])
                res = []
                for gv in range(2):
                    ps, pb = nextps()
                    for kc in range(8):
                        mm(ps[:, 0:W], wj[wi][:, kc, gv, :], u[:, kc, 0:W], kc == 0, kc == 7, [b_wj[wi], b_u], [pb])
                    res.append((ps, pb))
                for gv in range(2):
                    ps, pb = res[gv]
                    c_ = gv * 22 + j
                    dst, bd = (cg, b_cg) if gv == 0 else (cv, b_cv)
                    fw = lambda jj: vec[:, V_FCW + jj * 44 + c_:V_FCW + jj * 44 + c_ + 1]
                    act(dst[:, 0:n], ps[:, 1:W - 1], AF.Identity, [pb, b_vec], [bd],
                        bias=vec[:, V_FCB + c_:V_FCB + c_ + 1], scale=fw(1))
                    stt(dst[:, 0:n], ps[:, 0:n], fw(0), dst[:, 0:n], ALU.mult, ALU.add, [pb, b_vec, bd], [bd])
                    stt(dst[:, 0:n], ps[:, 2:W], fw(2), dst[:, 0:n], ALU.mult, ALU.add, [pb, b_vec, bd], [bd])
                act(g1[:, 0:n], cg[:, 0:n], AF.Square, [b_cg], [b_g1])
                ts("pool", g1[:, 0:n], g1[:, 0:n], 0.044715, 1.0, ALU.mult, ALU.add, [b_g1], [b_g1])
                tt("dve", g2_[:, 0:n], g1[:, 0:n], cg[:, 0:n], ALU.mult, [b_g1, b_cg], [b_g2])
                act(g3[:, 0:n], g2_[:, 0:n], AF.Sigmoid, [b_g2], [b_g3], scale=GELU_C)
                tt("dve", g1[:, 0:n], cg[:, 0:n], cv[:, 0:n], ALU.mult, [b_cg, b_cv, b_g2], [b_g1])
                tt("dve", actb[:, j, 0:n], g1[:, 0:n], g3[:, 0:n], ALU.mult, [b_g1, b_g3], [b_act])
            for mc in range(8):
                ps, pb = nextps()
                for kc in range(22):
                    mm(ps[:, 0:n], wd[:, kc, mc * 128:(mc + 1) * 128], actb[:, kc, 0:n], kc == 0, kc == 21,
                       [b_w, b_act], [pb])
                cp("dve", m32[:, mc, 0:n], ps[:, 0:n], [pb], [b_m32])
                act(sqm[:, mc, 0:n], m32[:, mc, 0:n], AF.Square, [b_m32], [b_sqm])
            rms_rstd(lambda c: sqm[:, c, 0:n], 8, n, rtmp[:, 0:n], rstd[:, 0:n], b_sqm, b_rt, b_rs)
            for mc in range(8):
                tmp, b_tmp = jsets[mc % 2].tmp, jsets[mc % 2].b_tmp
                tt("dve", tmp[:, 0:n], m32[:, mc, 0:n], rstd[:, 0:n], ALU.mult, [b_m32, b_rs], [b_tmp])
                stt(x2_t[:, mc, 0:n], tmp[:, 0:n], vec[:, V_NFPOST + mc:V_NFPOST + mc + 1], x1w[:, mc, 1:W - 1],
                    ALU.mult, ALU.add, [b_tmp, b_vec, b_x1], [b_x2])
                cp("act", x2b[:, mc, 0:n], x2_t[:, mc, 0:n], [b_x2], [b_x2b])
            for mc in range(8):
                ps, pb = nextps()
                for kc in range(8):
                    mm(ps[:, 0:n], wpg[:, kc, mc * 128:(mc + 1) * 128], x2b[:, kc, 0:n], kc == 0, kc == 7,
                       [b_w, b_x2b], [pb])
                gate, b_gate = jsets[mc % 2].gate, jsets[mc % 2].b_gate
                act(gate[:, 0:n], ps[:, 0:n], AF.Sigmoid, [pb], [b_gate])
                ps2, pb2 = nextps()
                for kc in range(2):
                    mm(ps2[:, 0:n], wpl[:, kc, mc * 128:(mc + 1) * 128], p_b[:, kc, 0:n], kc == 0, kc == 1,
                       [b_w, b_pb], [pb2])
                tt("dve", m32[:, mc, 0:n], ps2[:, 0:n], gate[:, 0:n], ALU.mult, [pb2, b_gate], [b_m32])
                act(sqm[:, mc, 0:n], m32[:, mc, 0:n], AF.Square, [b_m32], [b_sqm])
            rms_rstd(lambda c: sqm[:, c, 0:n], 8, n, rtmp[:, 0:n], rstd[:, 0:n], b_sqm, b_rt, b_rs)
            for mc in range(8):
                tmp, b_tmp = jsets[mc % 2].tmp, jsets[mc % 2].b_tmp
                tt("dve", tmp[:, 0:n], m32[:, mc, 0:n], rstd[:, 0:n], ALU.mult, [b_m32, b_rs], [b_tmp])
                stt(x1w[:, mc, 0:n], tmp[:, 0:n], vec[:, V_NPLE + mc:V_NPLE + mc + 1], x2_t[:, mc, 0:n],
                    ALU.mult, ALU.add, [b_tmp, b_vec, b_x2], [b_x1])
            dma(fm(xout)[:, :, a0:a0 + n], x1w[:, :, 0:n], [b_x1], ())
        S.barrier()

    pass_W0()
    for l in range(L):
        xin = xT if l == 0 else xL
        xout = yT if l == L - 1 else xL
        if upto >= 1:
            pass_P1(l, xin)
        if upto >= 2:
            pass_P2pre(l)
        if upto >= 3:
            pass_P2scan(l)
        if upto >= 4:
            pass_P3a(l, xin)
        if upto >= 5:
            pass_P3b(l, xout)
    S.finish()
    S.emit(nc)
    st.close()
    return nc


def make_consts():
    c = np.zeros((128, NCON), np.float32)
    i = np.arange(128)
    c[:, C_IDENT:C_IDENT + 128] = np.eye(128)
    c[:, C_SU:C_SU + 128] = (i[:, None] < i[None, :])
    c[:, C_SL:C_SL + 128] = (i[:, None] > i[None, :])
    c[:, C_UI:C_UI + 128] = (i[:, None] <= i[None, :])
    c[:, C_LI:C_LI + 128] = (i[:, None] >= i[None, :])
    c[:, C_BLK:C_BLK + 128] = ((i[:, None] // 64) == (i[None, :] // 64))
    c[:, C_ONES:C_ONES + 128] = 1.0
    for fc in range(4):
        for h in range(8):
            c[:, C_HSEL + fc * 8 + h] = (h == 2 * fc + i // 64)
    r = np.ones(512, np.float32)
    r[::128] = 0.0
    c[:, C_RESET:C_RESET + 512] = r[None, :]
    c[:, C_EPS] = NORM_EPS
    c[:, C_GNEPS] = GN_EPS
    return c


def make_vecs(inp, L):
    v = np.zeros((L, 128, NV), np.float32)

    def fmaj(a):
        return np.ascontiguousarray(a.reshape(-1, 128).T)

    for l in range(L):
        v[l, :, V_NMP:V_NMP + 8] = fmaj(inp["norm_mix_pre"][l])
        v[l, :, V_NMPOST:V_NMPOST + 8] = fmaj(inp["norm_mix_post"][l])
        v[l, :, V_NFP:V_NFP + 8] = fmaj(inp["norm_ffn_pre"][l])
        v[l, :, V_NFPOST:V_NFPOST + 8] = fmaj(inp["norm_ffn_post"][l])
        v[l, :, V_NPLE:V_NPLE + 8] = fmaj(inp["norm_ple_post"][l])
        for j in range(3):
            v[l, :, V_CONVW + j * 4:V_CONVW + j * 4 + 4] = fmaj(inp["conv_w"][l, j])
            v[l, :, V_FCW + j * 44:V_FCW + j * 44 + 44] = fmaj(inp["ffn_conv_w"][l, j])
        v[l, :, V_CONVB:V_CONVB + 4] = fmaj(inp["conv_b"][l])
        v[l, :, V_FCB:V_FCB + 44] = fmaj(inp["ffn_conv_b"][l])
        v[l, :, V_KK:V_KK + 4] = fmaj(inp["k_k"][l])
        v[l, :, V_KA:V_KA + 4] = fmaj(inp["k_a"][l])
        v[l, :, V_RK:V_RK + 4] = fmaj(inp["r_k"][l].reshape(-1))
        for d in range(2):
            v[l, :, V_W0 + d * 4:V_W0 + d * 4 + 4] = fmaj(inp["decay_w0"][l, d])
            v[l, :, V_A0 + d * 4:V_A0 + d * 4 + 4] = fmaj(inp["iclr_a0"][l, d])
            v[l, 0:64, V_MU + d * 2] = inp["shift_mu"][l, d, 0:64]
            v[l, 0:64, V_MU + d * 2 + 1] = inp["shift_mu"][l, d, 64:128]
    return v


_PROG_CACHE = {}


def run_cores(seqs_per_core, carry_per_core, inp, NSEG, SEG, L, debug=False, upto=99):
    key = (NSEG, SEG, L, debug, upto)
    if key not in _PROG_CACHE:
        _PROG_CACHE[key] = build_program(NSEG, SEG, L, debug, upto)
    nc = _PROG_CACHE[key]
    consts = make_consts()
    vecs = make_vecs(inp, L)
    gnwb = np.zeros((L, 128, 1024), np.float32)
    for l in range(L):
        gnwb[l, :, 0:512] = inp["gn_w"][l][None, :]
        gnwb[l, :, 512:1024] = inp["gn_b"][l][None, :]
    shared = {
        "consts": consts, "vecs": vecs, "gnwb": gnwb,
        "w_in": inp["w_in"], "w_branch_a": inp["w_branch_a"], "w_branch_b": inp["w_branch_b"],
        "w_out": inp["w_out"], "w_up": inp["w_up"], "w_down": inp["w_down"], "w_ple": inp["w_ple"],
        "w_ple_gate": inp["w_ple_gate"], "decay_w2": inp["decay_w2"], "iclr_a2": inp["iclr_a2"],
        "gate_g2": inp["gate_g2"],
    }
    shared = {k: np.ascontiguousarray(np.asarray(v, np.float32)) for k, v in shared.items()}
    in_maps = []
    for (x, p), carry in zip(seqs_per_core, carry_per_core):
        m = dict(shared)
        m["xT"] = np.ascontiguousarray(x.T)
        m["pT"] = np.ascontiguousarray(np.transpose(p, (0, 2, 1)))
        mk = np.zeros((128, NSEG + 1), np.float32)
        mk[:, :] = np.asarray(carry, np.float32)[None, :]
        m["masks"] = mk
        in_maps.append(m)
    res = run_bass_kernel_spmd(nc, in_maps, core_ids=list(range(len(in_maps))))
    return res.results


def kernel(**inp):
    inp = {k: np.asarray(v) for k, v in inp.items()}
    xp, xs = inp["x_prompt"], inp["x_sample"]
    pp, psm = inp["p_prompt"], inp["p_sample"]
    L = pp.shape[0]
    SEG, NSEG = 2048, 6
    per_core = []
    carries = []
    plan = []
    for c in range(8):
        if c < 4:
            segs = [("p", c), ("s", 2 * c), ("s", 2 * c + 1)]
            carry = [0, 1, 1, 1, 0, 0, 0]
        else:
            segs = [("s", 8 + 6 * (c - 4) + i) for i in range(6)]
            carry = [0] * 7
        xs_l, ps_l = [], []
        for kind, i in segs:
            if kind == "p":
                xs_l.append(xp[i])
                ps_l.append(pp[:, i])
            else:
                xs_l.append(xs[i])
                ps_l.append(psm[:, i])
        per_core.append((np.concatenate(xs_l, axis=0), np.concatenate(ps_l, axis=1)))
        carries.append(carry)
        plan.append(segs)
    results = run_cores(per_core, carries, inp, NSEG, SEG, L)
    y_p = np.empty(xp.shape, np.float32)
    y_s = np.empty(xs.shape, np.float32)
    for c in range(8):
        y = np.ascontiguousarray(results[c]["yT"].T)
        off = 0
        for kind, i in plan[c]:
            if kind == "p":
                y_p[i] = y[off:off + 8192]
                off += 8192
            else:
                y_s[i] = y[off:off + 2048]
                off += 2048
    return (y_p, y_s)
```
